# Optimizing a Trainium2 kernel written in Bass

```python
import math
import jax, jax.numpy as jnp
from jax import lax
import numpy as np

D_MODEL = 1024
BATCH = 4
SEQ = 4096
DEPTH = 4

PLE_DIM = 256
N_A_LAYERS = DEPTH // 2
N_B_LAYERS = DEPTH - N_A_LAYERS
GDN_HEADS = 8
GDN_HEAD_DIM = 128
GDN_WIDTH = GDN_HEADS * GDN_HEAD_DIM
CONV_WIDTH = 4
GDN_CHUNK = 64
SB_HEADS = 8
SB_HEAD_DIM = 128
SB_WIDTH = SB_HEADS * SB_HEAD_DIM
SB_BLOCK = 128
FFN_HIDDEN = -(-8 * D_MODEL // (3 * 256)) * 256
EPS = 1e-6

kernel_name = "yoco_gdn_stickbreaking_hybrid"


def rms_norm(x, g):
    xf = x.astype(jnp.float32)
    y = xf * lax.rsqrt(jnp.mean(xf * xf, axis=-1, keepdims=True) + EPS)
    return (y * g.astype(jnp.float32)).astype(x.dtype)


def l2_norm(x):
    return x * lax.rsqrt(jnp.sum(x * x, axis=-1, keepdims=True) + EPS)


def causal_conv(x, w):
    k_w, c = w.shape
    return lax.conv_general_dilated(x, w[:, None, :], window_strides=(1,), padding=[(k_w - 1, 0)],
                                    dimension_numbers=('NWC', 'WIO', 'NWC'), feature_group_count=c)


def to_chunks(t):
    b, s, h = t.shape[:3]
    t = t.reshape((b, s // GDN_CHUNK, GDN_CHUNK, h) + t.shape[3:])
    return jnp.swapaxes(t, 2, 3)


def gated_delta_rule(q, k, v, g, beta):
    b_, s_, h_, dk = q.shape
    dv = v.shape[-1]
    c = GDN_CHUNK
    q, k, v, g, beta = to_chunks(q), to_chunks(k), to_chunks(v), to_chunks(g), to_chunks(beta)
    G = jnp.cumsum(g, axis=-1)
    causal = jnp.tril(jnp.ones((c, c), dtype=bool))
    strict = jnp.tril(jnp.ones((c, c), dtype=bool), -1)
    decay_mat = jnp.exp(jnp.where(causal, G[..., :, None] - G[..., None, :], -jnp.inf))
    kk = jnp.einsum('bnhrd,bnhsd->bnhrs', k, k)
    a_low = jnp.where(strict, beta[..., None] * kk * decay_mat, 0.0)
    eye = jnp.eye(c, dtype=q.dtype)
    rhs = jnp.concatenate([v * beta[..., None], k * (beta * jnp.exp(G))[..., None]], axis=-1)
    sol = lax.linalg.triangular_solve(a_low + eye, rhs, left_side=True, lower=True, unit_diagonal=True)
    u, w = sol[..., :dv], sol[..., dv:]
    qk = jnp.einsum('bnhrd,bnhsd->bnhrs', q, k) * decay_mat
    q_dec = q * jnp.exp(G)[..., None]
    k_dec = k * jnp.exp(G[..., -1:] - G)[..., None]
    g_last = jnp.exp(G[..., -1])

    def step(state, xs):
        q_c, k_c, qk_c, u_c, w_c, gl_c = xs
        v_new = u_c - jnp.einsum('bhcd,bhde->bhce', w_c, state)
        o = jnp.einsum('bhcd,bhde->bhce', q_c, state) + jnp.einsum('bhrs,bhse->bhre', qk_c, v_new)
        state = state * gl_c[..., None, None] + jnp.einsum('bhcd,bhce->bhde', k_c, v_new)
        return state, o

    xs = tuple(jnp.moveaxis(t, 1, 0) for t in (q_dec, k_dec, qk, u, w, g_last))
    s0 = jnp.zeros((b_, h_, dk, dv), q.dtype)
    _, o = lax.scan(step, s0, xs)
    return o.transpose(1, 0, 3, 2, 4).reshape(b_, s_, h_, dv)


def gdn_mixer(hn, w_in, w_conv, a_log, dt_bias, norm_g, w_out):
    b_, s_, _ = hn.shape
    proj = hn @ w_in
    qkv = jax.nn.silu(causal_conv(proj[..., :3 * GDN_WIDTH], w_conv))
    gate = proj[..., 3 * GDN_WIDTH:4 * GDN_WIDTH].reshape(b_, s_, GDN_HEADS, GDN_HEAD_DIM)
    a_in = proj[..., 4 * GDN_WIDTH:4 * GDN_WIDTH + GDN_HEADS].astype(jnp.float32)
    b_in = proj[..., 4 * GDN_WIDTH + GDN_HEADS:].astype(jnp.float32)
    qkv = qkv.astype(jnp.float32).reshape(b_, s_, 3, GDN_HEADS, GDN_HEAD_DIM)
    q = l2_norm(qkv[:, :, 0]) * (GDN_HEAD_DIM ** -0.5)
    k = l2_norm(qkv[:, :, 1])
    v = qkv[:, :, 2]
    beta = jax.nn.sigmoid(b_in)
    g = -jnp.exp(a_log.astype(jnp.float32)) * jax.nn.softplus(a_in + dt_bias.astype(jnp.float32))
    o = gated_delta_rule(q, k, v, g, beta).astype(hn.dtype)
    o = rms_norm(o, norm_g) * jax.nn.silu(gate)
    return o.reshape(b_, s_, GDN_WIDTH) @ w_out


def shared_kv(h, kv_norm, w_kv, k_norm):
    b_, s_, _ = h.shape
    kv = (rms_norm(h, kv_norm) @ w_kv).reshape(b_, s_, 2, SB_HEADS, SB_HEAD_DIM)
    k = rms_norm(kv[:, :, 0], k_norm).transpose(0, 2, 1, 3)
    v = kv[:, :, 1].transpose(0, 2, 1, 3)
    return k, v


def stick_breaking(q, k, v):
    s_ = q.shape[2]
    outs = []
    for blk in range(s_ // SB_BLOCK):
        t0, t1 = blk * SB_BLOCK, (blk + 1) * SB_BLOCK
        z = jnp.einsum('bhtd,bhsd->bhts', q[:, :, t0:t1], k[:, :, :t1]).astype(jnp.float32)
        t_idx = t0 + jnp.arange(SB_BLOCK)[:, None]
        s_idx = jnp.arange(t1)[None, :]
        mask = s_idx < t_idx
        log_not = jnp.where(mask, jax.nn.log_sigmoid(-z), 0.0)
        suffix = lax.cumsum(log_not, axis=3, reverse=True) - log_not
        log_w = jnp.where(mask, jax.nn.log_sigmoid(z) + suffix, -jnp.inf)
        wgt = jnp.exp(log_w).astype(v.dtype)
        outs.append(jnp.einsum('bhts,bhsd->bhtd', wgt, v[:, :, :t1]))
    return jnp.concatenate(outs, axis=2)


def sb_mixer(hn, k_sh, v_sh, w_q, q_norm, w_out):
    b_, s_, _ = hn.shape
    q = (hn @ w_q).reshape(b_, s_, SB_HEADS, SB_HEAD_DIM)
    q = (rms_norm(q, q_norm) * (SB_HEAD_DIM ** -0.5)).transpose(0, 2, 1, 3)
    o = stick_breaking(q, k_sh, v_sh)
    return o.transpose(0, 2, 1, 3).reshape(b_, s_, SB_WIDTH) @ w_out


def swiglu(hn, w_in, w_out):
    gu = hn @ w_in
    return (jax.nn.silu(gu[..., :FFN_HIDDEN]) * gu[..., FFN_HIDDEN:]) @ w_out


def setup_inputs(seed: int = 0) -> dict:
    key = jax.random.key(seed)
    ks = jax.random.split(key, 24)
    f32 = jnp.float32

    def nrm(k, shape, fan_in):
        return jax.random.normal(k, shape, f32) * (fan_in ** -0.5)

    def gain(k, shape):
        return 1.0 + 0.02 * jax.random.normal(k, shape, f32)

    dt = jnp.exp(jax.random.uniform(ks[8], (N_A_LAYERS, GDN_HEADS), f32, math.log(1e-3), math.log(1e-1)))
    return {
        "x": jax.random.normal(ks[0], (BATCH, SEQ, D_MODEL), f32),
        "p": jax.random.normal(ks[1], (DEPTH, BATCH, SEQ, PLE_DIM), f32),
        "ln_mix": gain(ks[2], (DEPTH, D_MODEL)),
        "ln_ffn": gain(ks[3], (DEPTH, D_MODEL)),
        "ln_ple": gain(ks[4], (DEPTH, D_MODEL)),
        "gdn_w_in": nrm(ks[5], (N_A_LAYERS, D_MODEL, 4 * GDN_WIDTH + 2 * GDN_HEADS), D_MODEL),
        "gdn_conv": nrm(ks[6], (N_A_LAYERS, CONV_WIDTH, 3 * GDN_WIDTH), CONV_WIDTH),
        "gdn_a_log": jnp.log(jax.random.uniform(ks[7], (N_A_LAYERS, GDN_HEADS), f32, 1.0, 16.0)),
        "gdn_dt_bias": dt + jnp.log(-jnp.expm1(-dt)),
        "gdn_norm": gain(ks[9], (N_A_LAYERS, GDN_HEAD_DIM)),
        "gdn_w_out": nrm(ks[10], (N_A_LAYERS, GDN_WIDTH, D_MODEL), GDN_WIDTH),
        "kv_norm": gain(ks[11], (D_MODEL,)),
        "w_kv": nrm(ks[12], (D_MODEL, 2 * SB_WIDTH), D_MODEL),
        "k_norm": gain(ks[13], (SB_HEAD_DIM,)),
        "sb_w_q": nrm(ks[14], (N_B_LAYERS, D_MODEL, SB_WIDTH), D_MODEL),
        "sb_q_norm": gain(ks[15], (N_B_LAYERS, SB_HEAD_DIM)),
        "sb_w_out": nrm(ks[16], (N_B_LAYERS, SB_WIDTH, D_MODEL), SB_WIDTH),
        "ffn_w_in": nrm(ks[17], (DEPTH, D_MODEL, 2 * FFN_HIDDEN), D_MODEL),
        "ffn_w_out": nrm(ks[18], (DEPTH, FFN_HIDDEN, D_MODEL), FFN_HIDDEN),
        "ple_w_proj": nrm(ks[19], (DEPTH, PLE_DIM, D_MODEL), PLE_DIM),
        "ple_w_gate": nrm(ks[20], (DEPTH, D_MODEL, D_MODEL), D_MODEL),
    }


def reference(x, p, ln_mix, ln_ffn, ln_ple, gdn_w_in, gdn_conv, gdn_a_log, gdn_dt_bias, gdn_norm,
              gdn_w_out, kv_norm, w_kv, k_norm, sb_w_q, sb_q_norm, sb_w_out, ffn_w_in, ffn_w_out,
              ple_w_proj, ple_w_gate):
    h = x
    k_sh, v_sh = None, None
    for i in range(DEPTH):
        hn = rms_norm(h, ln_mix[i])
        if i < N_A_LAYERS:
            h = h + gdn_mixer(hn, gdn_w_in[i], gdn_conv[i], gdn_a_log[i], gdn_dt_bias[i],
                              gdn_norm[i], gdn_w_out[i])
        else:
            j = i - N_A_LAYERS
            h = h + sb_mixer(hn, k_sh, v_sh, sb_w_q[j], sb_q_norm[j], sb_w_out[j])
        h = h + swiglu(rms_norm(h, ln_ffn[i]), ffn_w_in[i], ffn_w_out[i])
        h = h + (p[i] @ ple_w_proj[i]) * jax.nn.sigmoid(rms_norm(h, ln_ple[i]) @ ple_w_gate[i])
        if i == N_A_LAYERS - 1:
            k_sh, v_sh = shared_kv(h, kv_norm, w_kv, k_norm)
    return h
```

```python
import numpy as np
import ml_dtypes
import concourse.bass as bass
import concourse.mybir as mybir
from concourse.bass_utils import run_bass_kernel_spmd

F32 = mybir.dt.float32
BF16 = mybir.dt.bfloat16
AF = mybir.ActivationFunctionType
ALU = mybir.AluOpType
NPBF = ml_dtypes.bfloat16

D = 1024
S = 4096
NB = 4
DEPTH = 4
NT = 2048
TT = 512
NTT = NT // TT
FH = 2816
NJ = FH // 128
EPS = 1e-6
NCORES = 8


class Buf:
    __slots__ = ("name", "t", "w", "r", "dsem", "dcnt", "excl")

    def __init__(self, name, t, excl=False):
        self.name = name
        self.t = t
        self.w = None
        self.r = {}
        self.dsem = None
        self.dcnt = 0
        self.excl = excl

    def __getitem__(self, idx):
        return self.t[idx]


class Eng:
    def __init__(self, k, name):
        self.name = name
        self.e = getattr(k.nc, name)
        self.sem = k.nc.alloc_semaphore("es_" + name)
        self.cnt = 0
        self.waited = {}

    def wait(self, dep):
        sem, val = dep
        key = id(sem)
        if self.waited.get(key, 0) >= val:
            return
        self.waited[key] = val
        self.e.wait_ge(sem, val)


class KB:
    def __init__(self, nc, same_engine_raw=True):
        self.nc = nc
        self.same_engine_raw = same_engine_raw
        self.pe = Eng(self, "tensor")
        self.dve = Eng(self, "vector")
        self.act = Eng(self, "scalar")
        self.pool = Eng(self, "gpsimd")
        self.sp = Eng(self, "sync")
        self.n_inst = 0
        self._uid = 0

    def uid(self, s):
        self._uid += 1
        return "%s_%d" % (s, self._uid)

    def sb(self, name, shape, dt):
        return Buf(name, self.nc.alloc_sbuf_tensor(self.uid(name), list(shape), dt))

    def ps(self, name, shape, dt=F32):
        return Buf(name, self.nc.alloc_psum_tensor(self.uid(name), list(shape), dt), excl=True)

    def dram(self, name, shape, dt, kind):
        return Buf(name, self.nc.dram_tensor(name, list(shape), dt, kind=kind).ap())

    def _deps(self, eng, reads, writes):
        raw = []
        war = []
        for b in reads:
            if b.w is not None:
                raw.append(b.w)
            if b.excl:
                war.extend(b.r.values())
        for b in writes:
            if b.w is not None:
                raw.append(b.w)
            war.extend(b.r.values())
        for d in raw:
            if d[0] is eng.sem:
                if not self.same_engine_raw or eng is self.pe or eng is self.sp:
                    continue
            eng.wait(d)
        for d in war:
            if d[0] is eng.sem:
                continue
            eng.wait(d)

    def op(self, eng, fn, reads=(), writes=()):
        self._deps(eng, reads, writes)
        inst = fn(eng.e)
        eng.cnt += 1
        inst.then_inc(eng.sem, 1)
        tk = (eng.sem, eng.cnt)
        for b in reads:
            b.r[id(tk[0])] = tk
        for b in writes:
            b.w = tk
            b.r = {}
        self.n_inst += 1
        return inst

    def dma(self, eng, out_ap, in_ap, reads=(), writes=(), sembuf=None, untracked_out=None, **kw):
        self._deps(eng, reads, writes)
        if untracked_out is not None:
            sembuf = untracked_out
            writes = [untracked_out]
        sb_ = sembuf or (writes[0] if writes else reads[0])
        if sb_.dsem is None:
            sb_.dsem = self.nc.alloc_semaphore(self.uid("ds_" + sb_.name))
        inst = eng.e.dma_start(out=out_ap, in_=in_ap, **kw)
        sb_.dcnt += 16
        inst.then_inc(sb_.dsem, 16)
        tk = (sb_.dsem, sb_.dcnt)
        for b in reads:
            b.r[id(tk[0])] = tk
        for b in writes:
            b.w = tk
            b.r = {}
        self.n_inst += 1
        return inst

    def finish(self, out_bufs):
        for b in out_bufs:
            if b.w is not None:
                self.sp.wait(b.w)


def emit_consts(k):
    c = {}
    c["ones_bf"] = k.sb("ones_bf", [128, 128], BF16)
    k.op(k.pool, lambda e: e.memset(c["ones_bf"][:], 1.0), writes=[c["ones_bf"]])
    c["eps"] = k.sb("eps", [128, 1], F32)
    k.op(k.pool, lambda e: e.memset(c["eps"][:], EPS), writes=[c["eps"]])
    c["one"] = k.sb("one", [128, 1], F32)
    k.op(k.pool, lambda e: e.memset(c["one"][:], 1.0), writes=[c["one"]])
    return c


class Rot:
    def __init__(self, bufs):
        self.bufs = bufs
        self.i = 0

    def next(self):
        b = self.bufs[self.i % len(self.bufs)]
        self.i += 1
        return b


def emit_rstd(k, c, srcs, inv_n, ps_ss, sq_rot, tmp_rot, width=TT, ln_bias=None, out=None):
    n = len(srcs)
    for i, (ap, b) in enumerate(srcs):
        sq = sq_rot.next()
        k.op(k.act, lambda e, ap=ap, sq=sq: e.activation(out=sq[:, 0:width], in_=ap, func=AF.Square),
             reads=[b], writes=[sq])
        k.op(k.pe, lambda e, sq=sq, i=i: e.matmul(ps_ss[:, 0:width], lhsT=c["ones_bf"][:], rhs=sq[:, 0:width],
                                                  start=(i == 0), stop=(i == n - 1)),
             reads=[sq, c["ones_bf"]], writes=[ps_ss])
    lnv = tmp_rot.next()
    k.op(k.act, lambda e: e.activation(out=lnv[:, 0:width], in_=ps_ss[:, 0:width], func=AF.Ln, bias=c["eps"][:], scale=inv_n),
         reads=[ps_ss, c["eps"]], writes=[lnv])
    rstd = out if out is not None else tmp_rot.next()
    if ln_bias is None:
        k.op(k.act, lambda e: e.activation(out=rstd[:, 0:width], in_=lnv[:, 0:width], func=AF.Exp, scale=-0.5),
             reads=[lnv], writes=[rstd])
    else:
        k.op(k.act, lambda e: e.activation(out=rstd[:, 0:width], in_=lnv[:, 0:width], func=AF.Exp, scale=-0.5, bias=ln_bias[:]),
             reads=[lnv, ln_bias], writes=[rstd])
    return rstd


def emit_token_phase(k, c, io, cfg):
    tail, nxt, kv = cfg["tail"], cfg["next"], cfg.get("kv", False)
    sp, pool, pe, act, dve = k.sp, k.pool, k.pe, k.act, k.dve

    hT = [[k.sb("hT%d_%d" % (cc, tt), [128, TT], F32) for tt in range(NTT)] for cc in range(8)]
    hin = io["hT_in"]
    for cc in range(8):
        for tt in range(NTT):
            k.dma(sp, hT[cc][tt][:], hin[cc * 128:(cc + 1) * 128, tt * TT:(tt + 1) * TT], reads=[hin], writes=[hT[cc][tt]])

    gains = k.sb("gains", [128, io["gains"].t.shape[1]], F32)
    k.dma(sp, gains[:], io["gains"][:, :], reads=[io["gains"]], writes=[gains])

    hn = [k.sb("hn%d" % tt, [128, 8, TT], BF16) for tt in range(NTT)]
    psb = [k.ps("psb%d" % i, [128, TT], F32) for i in range(8)]
    sq_rot = Rot([k.sb("sq%d" % i, [128, TT], BF16) for i in range(3)])
    tmp_rot = Rot([k.sb("tmpf%d" % i, [128, TT], F32) for i in range(3)])
    wsq = [k.sb("wsq%d" % i, [128, 8, 1024], BF16) for i in range(2)]
    wsq_rot = Rot(wsq)

    def load_w(dst, src, ncol, col0=0, kc=8):
        v = src.t.rearrange("(c p) n -> p c n", p=128)
        k.dma(pool, dst[:, 0:kc, 0:ncol], v[:, 0:kc, col0:col0 + ncol], reads=[src], writes=[dst])

    def norm_to_hn(gcol):
        for tt in range(NTT):
            rstd = emit_rstd(k, c, [(hT[cc][tt][:], hT[cc][tt]) for cc in range(8)], 1.0 / D, psb[7], sq_rot, tmp_rot)
            for cc in range(8):
                k.op(dve, lambda e, cc=cc, tt=tt, rstd=rstd: e.scalar_tensor_tensor(
                    out=hn[tt][:, cc, :], in0=hT[cc][tt][:], scalar=gains[:, gcol + cc:gcol + cc + 1], in1=rstd[:],
                    op0=ALU.mult, op1=ALU.mult), reads=[hT[cc][tt], gains, rstd], writes=[hn[tt]])

    def proj_add(w, rhs_of, nk, pbanks):
        for tt in range(NTT):
            for n in range(8):
                ps = pbanks.next()
                for kk in range(nk):
                    ap, b = rhs_of(tt, kk)
                    k.op(pe, lambda e, ps=ps, kk=kk, n=n, ap=ap: e.matmul(ps[:], lhsT=w[:, kk, n * 128:(n + 1) * 128], rhs=ap,
                                                                         start=(kk == 0), stop=(kk == nk - 1)),
                         reads=[w, b], writes=[ps])
                k.op(dve, lambda e, ps=ps, n=n, tt=tt: e.tensor_tensor(out=hT[n][tt][:], in0=ps[:], in1=hT[n][tt][:], op=ALU.add),
                     reads=[ps, hT[n][tt]], writes=[hT[n][tt]])

    GC = cfg["gcols"]
    if tail:
        big = [k.sb("big%d" % i, [128, 2 * NT], BF16) for i in range(2)]
        ov = io["oT"].t.rearrange("(c p) n -> p c n", p=128)
        w_o = wsq_rot.next()
        load_w(w_o, io["w_o"], 1024)

        def o_rhs(tt, kk):
            b = big[tt % 2]
            if kk == 0:
                k.dma(sp, b[:, :].rearrange("p (c t) -> p c t", c=8), ov[:, :, tt * TT:(tt + 1) * TT], reads=[io["oT"]], writes=[b])
            return b[:, kk * TT:(kk + 1) * TT], b
        if 'oproj' not in cfg.get('skip', ()):
            proj_add(w_o, o_rhs, 8, Rot(psb[0:4]))

        norm_to_hn(GC["ln_ffn"])
        PC = 2
        npieces = NJ // PC
        wg = [k.sb("wg%d" % i, [128, 8, PC * 128], BF16) for i in range(2)]
        wu = [k.sb("wu%d" % i, [128, 8, PC * 128], BF16) for i in range(2)]
        wo = [k.sb("wo%d" % i, [128, PC, 1024], BF16) for i in range(2)]
        class _HV:
            def __init__(self, b, jj):
                self.b, self.jj = b, jj

            def __getitem__(self, idx):
                p, fs = idx
                return self.b[p, self.jj * NT + fs.start:self.jj * NT + fs.stop]
        hidb = big
        hid = [[_HV(big[i], jj) for jj in range(PC)] for i in range(2)]
        sg_rot = Rot([k.sb("sg%d" % i, [128, TT], F32) for i in range(2)])
        w_in, w_out = io["ffn_w_in"], io["ffn_w_out"]
        wout_v = w_out.t.rearrange("(c p) n -> p c n", p=128)

        def load_piece(P):
            s = P % 2
            load_w(wg[s], w_in, PC * 128, col0=P * PC * 128)
            load_w(wu[s], w_in, PC * 128, col0=FH + P * PC * 128)
            k.dma(pool, wo[s][:], wout_v[:, P * PC:(P + 1) * PC, :], reads=[w_out], writes=[wo[s]])

        if 'ffn' in cfg.get('skip', ()):
            npieces = 0
        else:
            load_piece(0)
        gbanks = Rot(psb[0:2]); ubanks = Rot(psb[2:4]); ybanks = Rot(psb[4:7])
        for P in range(npieces):
            s = P % 2
            if P + 1 < npieces:
                load_piece(P + 1)
            for jj in range(PC):
                for tt in range(NTT):
                    pg = gbanks.next(); pu = ubanks.next()
                    for kk in range(8):
                        k.op(pe, lambda e, pg=pg, kk=kk, jj=jj, tt=tt: e.matmul(pg[:], lhsT=wg[s][:, kk, jj * 128:(jj + 1) * 128], rhs=hn[tt][:, kk, :],
                                                                                start=(kk == 0), stop=(kk == 7)), reads=[wg[s], hn[tt]], writes=[pg])
                    for kk in range(8):
                        k.op(pe, lambda e, pu=pu, kk=kk, jj=jj, tt=tt: e.matmul(pu[:], lhsT=wu[s][:, kk, jj * 128:(jj + 1) * 128], rhs=hn[tt][:, kk, :],
                                                                                start=(kk == 0), stop=(kk == 7)), reads=[wu[s], hn[tt]], writes=[pu])
                    sg = sg_rot.next()
                    k.op(act, lambda e, sg=sg, pg=pg: e.activation(out=sg[:], in_=pg[:], func=AF.Silu), reads=[pg], writes=[sg])
                    k.op(dve, lambda e, sg=sg, pu=pu, jj=jj, tt=tt: e.tensor_tensor(out=hid[s][jj][:, tt * TT:(tt + 1) * TT], in0=pu[:], in1=sg[:], op=ALU.mult),
                         reads=[pu, sg], writes=[hidb[s]])
            for tt in range(NTT):
                for n in range(8):
                    py = ybanks.next()
                    for jj in range(PC):
                        k.op(pe, lambda e, py=py, jj=jj, n=n, tt=tt: e.matmul(py[:], lhsT=wo[s][:, jj, n * 128:(n + 1) * 128], rhs=hid[s][jj][:, tt * TT:(tt + 1) * TT],
                                                                              start=(jj == 0), stop=(jj == PC - 1)), reads=[wo[s], hidb[s]], writes=[py])
                    k.op(dve, lambda e, py=py, n=n, tt=tt: e.tensor_tensor(out=hT[n][tt][:], in0=py[:], in1=hT[n][tt][:], op=ALU.add),
                         reads=[py, hT[n][tt]], writes=[hT[n][tt]])

        norm_to_hn(GC["ln_ple"])
        w_pg = wsq_rot.next()
        load_w(w_pg, io["ple_w_gate"], 1024)
        w_pp = k.sb("w_pp", [128, 2, 1024], BF16)
        load_w(w_pp, io["ple_w_proj"], 1024, kc=2)
        pTb = [k.sb("pT%d" % i, [128, 2, TT], BF16) for i in range(2)]
        pT = [pTb[tt % 2] for tt in range(NTT)]
        pv = io["pT"].t.rearrange("(c p) n -> p c n", p=128)
        gb = Rot(psb[0:2]); pb = Rot(psb[2:4])
        for tt in range(NTT if 'ple' not in cfg.get('skip', ()) else 0):
            k.dma(pool, pT[tt][:], pv[:, :, tt * TT:(tt + 1) * TT], reads=[io["pT"]], writes=[pT[tt]])
            for n in range(8):
                pg = gb.next(); pp = pb.next()
                for kk in range(8):
                    k.op(pe, lambda e, pg=pg, kk=kk, n=n, tt=tt: e.matmul(pg[:], lhsT=w_pg[:, kk, n * 128:(n + 1) * 128], rhs=hn[tt][:, kk, :],
                                                                          start=(kk == 0), stop=(kk == 7)), reads=[w_pg, hn[tt]], writes=[pg])
                for kk in range(2):
                    k.op(pe, lambda e, pp=pp, kk=kk, n=n, tt=tt: e.matmul(pp[:], lhsT=w_pp[:, kk, n * 128:(n + 1) * 128], rhs=pT[tt][:, kk, :],
                                                                          start=(kk == 0), stop=(kk == 1)), reads=[w_pp, pT[tt]], writes=[pp])
                sg = sg_rot.next()
                k.op(act, lambda e, sg=sg, pg=pg: e.activation(out=sg[:], in_=pg[:], func=AF.Sigmoid), reads=[pg], writes=[sg])
                k.op(dve, lambda e, sg=sg, pp=pp: e.tensor_tensor(out=sg[:], in0=pp[:], in1=sg[:], op=ALU.mult), reads=[pp, sg], writes=[sg])
                k.op(dve, lambda e, sg=sg, n=n, tt=tt: e.tensor_tensor(out=hT[n][tt][:], in0=sg[:], in1=hT[n][tt][:], op=ALU.add),
                     reads=[sg, hT[n][tt]], writes=[hT[n][tt]])

    if "hT_out" in io:
        hout = io["hT_out"]
        for cc in range(8):
            for tt in range(NTT):
                k.dma(sp, hout[cc * 128:(cc + 1) * 128, tt * TT:(tt + 1) * TT], hT[cc][tt][:], reads=[hT[cc][tt]], untracked_out=hout)

    obuf_rot = Rot([k.sb("obuf%d" % i, [128, TT], BF16) for i in range(3)])

    def headnorm_proj(w, gcol, out_dram, extra_scale):
        lnb = None
        if extra_scale != 1.0:
            lnb = k.sb("lnb", [128, 1], F32)
            k.op(pool, lambda e: e.memset(lnb[:], float(np.log(extra_scale))), writes=[lnb])
        qb = Rot(psb[0:3])
        for tt in range(NTT):
            for n in range(8):
                pq = qb.next()
                for kk in range(8):
                    k.op(pe, lambda e, pq=pq, kk=kk, n=n, tt=tt: e.matmul(pq[:], lhsT=w[:, kk, n * 128:(n + 1) * 128], rhs=hn[tt][:, kk, :],
                                                                          start=(kk == 0), stop=(kk == 7)), reads=[w, hn[tt]], writes=[pq])
                rstd = emit_rstd(k, c, [(pq[:], pq)], 1.0 / 128, psb[7], sq_rot, tmp_rot, ln_bias=lnb)
                ob = obuf_rot.next()
                k.op(dve, lambda e, ob=ob, pq=pq, n=n, rstd=rstd: e.scalar_tensor_tensor(out=ob[:], in0=pq[:], scalar=gains[:, gcol:gcol + 1], in1=rstd[:],
                                                                                       op0=ALU.mult, op1=ALU.mult), reads=[pq, gains, rstd], writes=[ob])
                k.dma(sp, out_dram[n * 128:(n + 1) * 128, tt * TT:(tt + 1) * TT], ob[:], reads=[ob], untracked_out=out_dram)

    if kv:
        norm_to_hn(GC["kv_norm"])
        w_k = wsq_rot.next()
        load_w(w_k, io["w_kv"], 1024, col0=0)
        headnorm_proj(w_k, GC["k_norm"], io["kT_out"], 1.0)
        w_v = wsq_rot.next()
        load_w(w_v, io["w_kv"], 1024, col0=1024)
        vb = Rot(psb[3:6])
        vo_rot = Rot([k.sb("vo%d" % i, [128, TT], BF16) for i in range(2)])
        vout = io["v_out"]
        for tt in range(NTT):
            for t4 in range(4):
                for nh in range(2):
                    pv_ = vb.next()
                    for kk in range(8):
                        k.op(pe, lambda e, pv_=pv_, kk=kk, nh=nh, tt=tt, t4=t4: e.matmul(pv_[:], lhsT=hn[tt][:, kk, t4 * 128:(t4 + 1) * 128], rhs=w_v[:, kk, nh * 512:(nh + 1) * 512],
                                                                                      start=(kk == 0), stop=(kk == 7)), reads=[w_v, hn[tt]], writes=[pv_])
                    vo = vo_rot.next()
                    k.op(act, lambda e, vo=vo, pv_=pv_: e.activation(out=vo[:], in_=pv_[:], func=AF.Copy), reads=[pv_], writes=[vo])
                    r0 = tt * TT + t4 * 128
                    k.dma(sp, vout[r0:r0 + 128, nh * 512:(nh + 1) * 512], vo[:], reads=[vo], untracked_out=vout)

    if nxt is not None:
        norm_to_hn(GC["ln_mix_next"])
        if nxt == "gdn":
            hnout = io["hnT_out"].t.rearrange("(c p) n -> p c n", p=128)
            for tt in range(NTT):
                k.dma(sp, hnout[:, :, tt * TT:(tt + 1) * TT], hn[tt][:], reads=[hn[tt]], untracked_out=io["hnT_out"])
        else:
            w_q = wsq_rot.next()
            load_w(w_q, io["w_q"], 1024)
            headnorm_proj(w_q, GC["q_norm"], io["qT_out"], 128 ** -0.5)

    outs = [io[n] for n in io if n.endswith("_out")]
    k.finish(outs)


GCOLS = {"ln_ffn": 0, "ln_ple": 8, "kv_norm": 16, "k_norm": 24, "ln_mix_next": 25, "q_norm": 33}
NGCOL = 34


def col8(v):
    return np.ascontiguousarray(np.asarray(v, np.float32).reshape(8, 128).T)


def build_token(cfg):
    nc = bass.Bass("TRN2", target_bir_lowering=False)
    k = KB(nc)
    io = {}
    io["hT_in"] = k.dram("hT_in", [D, NT], F32, "ExternalInput")
    io["gains"] = k.dram("gains", [128, NGCOL], F32, "ExternalInput")
    if cfg["tail"]:
        io["oT"] = k.dram("oT", [D, NT], BF16, "ExternalInput")
        io["w_o"] = k.dram("w_o", [D, D], F32, "ExternalInput")
        io["ffn_w_in"] = k.dram("ffn_w_in", [D, 2 * FH], F32, "ExternalInput")
        io["ffn_w_out"] = k.dram("ffn_w_out", [FH, D], F32, "ExternalInput")
        io["ple_w_gate"] = k.dram("ple_w_gate", [D, D], F32, "ExternalInput")
        io["ple_w_proj"] = k.dram("ple_w_proj", [256, D], F32, "ExternalInput")
        io["pT"] = k.dram("pT", [256, NT], F32, "ExternalInput")
    if cfg.get("kv"):
        io["w_kv"] = k.dram("w_kv", [D, 2 * D], F32, "ExternalInput")
        io["kT_out"] = k.dram("kT_out", [D, NT], BF16, "ExternalOutput")
        io["v_out"] = k.dram("v_out", [NT, D], BF16, "ExternalOutput")
    if cfg["next"] == "gdn":
        io["hnT_out"] = k.dram("hnT_out", [D, NT], BF16, "ExternalOutput")
    elif cfg["next"] == "sb":
        io["w_q"] = k.dram("w_q", [D, D], F32, "ExternalInput")
        io["qT_out"] = k.dram("qT_out", [D, NT], BF16, "ExternalOutput")
    io["hT_out"] = k.dram("hT_out", [D, NT], F32, "ExternalOutput")
    c = emit_consts(k)
    cfg = dict(cfg)
    cfg["gcols"] = GCOLS
    emit_token_phase(k, c, io, cfg)
    return nc, k


def emit_masks(k, c):
    nc = k.nc
    m = k.sb("mstrict", [128, 128], F32)
    k.op(k.pool, lambda e: e.memset(m[:], 1.0), writes=[m])
    k.op(k.pool, lambda e: e.affine_select(out=m[:], in_=m[:], pattern=[[1, 128]], compare_op=ALU.is_gt, fill=0.0,
                                           base=0, channel_multiplier=-1), reads=[m], writes=[m])
    c["mstrict"] = m
    t = k.sb("triinc", [128, 128], BF16)
    k.op(k.pool, lambda e: e.memset(t[:], 1.0), writes=[t])
    k.op(k.pool, lambda e: e.affine_select(out=t[:], in_=t[:], pattern=[[-1, 128]], compare_op=ALU.is_ge, fill=0.0,
                                           base=0, channel_multiplier=1), reads=[t], writes=[t])
    c["triinc"] = t


def emit_sb_phase(k, c, io, nheads=4, seq=S):
    sp, pool, pe, act, dve = k.sp, k.pool, k.pe, k.act, k.dve
    nblk = seq // 128
    nsb = seq // 512
    qT = k.sb("qT", [128, nheads, seq], BF16)
    kT = k.sb("kT", [128, nheads, seq], BF16)
    V = k.sb("V", [128, nblk, nheads * 128], BF16)
    k.dma(sp, kT[:], io["kT"].t.rearrange("(h p) t -> p h t", p=128), reads=[io["kT"]], writes=[kT])
    k.dma(sp, qT[:], io["qT"].t.rearrange("(h p) t -> p h t", p=128), reads=[io["qT"]], writes=[qT])
    k.dma(sp, V[:], io["v"].t.rearrange("(b p) n -> p b n", p=128), reads=[io["v"]], writes=[V])

    E = [k.sb("E%d" % i, [128, 512], F32) for i in range(3)]
    SP = [k.sb("SP%d" % i, [128, 512], BF16) for i in range(3)]
    W1 = [k.sb("W1%d" % i, [128, 512], F32) for i in range(2)]
    WG = [k.sb("WG%d" % i, [128, 512], BF16) for i in range(3)]
    LS = [k.sb("LS%d" % i, [128, 512], BF16) for i in range(2)]
    OB = [k.sb("OB%d" % i, [128, 512], BF16) for i in range(2)]
    pz = [k.ps("pz%d" % i, [128, 512], F32) for i in range(2)]
    pi = [k.ps("pi%d" % i, [128, 512], F32) for i in range(2)]
    po = [k.ps("po%d" % i, [128, 512], F32) for i in range(2)]

    steps = []
    sbi = 0
    for h in range(nheads):
        for sb in range(nsb):
            n_s = 4 * sb + 4
            for idx, i in enumerate(range(n_s - 1, -1, -1)):
                m = i - 4 * sb
                c0 = max(0, 128 * m)
                steps.append(dict(h=h, sb=sb, i=i, c0=c0, diag=(m >= 0), first=(idx == 0), last=(i == 0), sbi=sbi))
            sbi += 1
    ns = len(steps)
    oT = io["oT_out"]

    def stageA(n):
        st = steps[n]
        h, i, c0, t0 = st["h"], st["i"], st["c0"], st["sb"] * 512
        z = pz[n % 2]; e_ = E[n % 3]; s_ = SP[n % 3]
        k.op(pe, lambda e: e.matmul(z[:, c0:512], lhsT=kT[:, h, i * 128:(i + 1) * 128], rhs=qT[:, h, t0 + c0:t0 + 512], start=True, stop=True),
             reads=[kT, qT], writes=[z])
        k.op(act, lambda e: e.activation(out=e_[:, c0:512], in_=z[:, c0:512], func=AF.Exp), reads=[z], writes=[e_])
        if st["diag"]:
            k.op(pool, lambda e: e.tensor_tensor(out=e_[:, c0:c0 + 128], in0=e_[:, c0:c0 + 128], in1=c["mstrict"][:], op=ALU.mult),
                 reads=[e_, c["mstrict"]], writes=[e_])
        k.op(act, lambda e: e.activation(out=s_[:, c0:512], in_=e_[:, c0:512], func=AF.Ln, bias=c["one"][:], scale=1.0), reads=[e_, c["one"]], writes=[s_])

    def stageB(n):
        st = steps[n]
        c0 = st["c0"]
        e_ = E[n % 3]; s_ = SP[n % 3]; inc = pi[n % 2]; w1 = W1[n % 2]; wg = WG[n % 3]; ls = LS[st["sbi"] % 2]
        if st["first"]:
            k.op(pool, lambda e: e.memset(ls[:], 0.0), writes=[ls])
        k.op(pe, lambda e: e.matmul(inc[:, c0:512], lhsT=c["triinc"][:], rhs=s_[:, c0:512], start=True, stop=st["first"]),
             reads=[c["triinc"], s_], writes=[inc])
        if not st["first"]:
            k.op(pe, lambda e: e.matmul(inc[:, c0:512], lhsT=c["ones_bf"][:], rhs=ls[:, c0:512], start=False, stop=True),
                 reads=[c["ones_bf"], ls], writes=[inc])
        if not st["last"]:
            k.op(pool, lambda e: e.tensor_tensor(out=ls[:, c0:512], in0=ls[:, c0:512], in1=s_[:, c0:512], op=ALU.add),
                 reads=[ls, s_], writes=[ls])
        k.op(act, lambda e: e.activation(out=w1[:, c0:512], in_=inc[:, c0:512], func=AF.Exp, scale=-1.0), reads=[inc], writes=[w1])
        if c0 > 0:
            k.op(pool, lambda e: e.memset(wg[:, 0:c0], 0.0), writes=[wg])
        k.op(dve, lambda e: e.tensor_tensor(out=wg[:, c0:512], in0=e_[:, c0:512], in1=w1[:, c0:512], op=ALU.mult),
             reads=[e_, w1], writes=[wg])

    def stageC(n):
        st = steps[n]
        h, i = st["h"], st["i"]
        wg = WG[n % 3]; o = po[st["sbi"] % 2]
        k.op(pe, lambda e: e.matmul(o[:], lhsT=V[:, i, h * 128:(h + 1) * 128], rhs=wg[:], start=st["first"], stop=st["last"]),
             reads=[V, wg], writes=[o])
        if st["last"]:
            ob = OB[st["sbi"] % 2]
            k.op(act, lambda e: e.activation(out=ob[:], in_=o[:], func=AF.Copy), reads=[o], writes=[ob])
            t0 = st["sb"] * 512
            k.dma(sp, oT[h * 128:(h + 1) * 128, t0:t0 + 512], ob[:], reads=[ob], untracked_out=oT)

    for it in range(ns + 2):
        if it < ns:
            stageA(it)
        if 0 <= it - 1 < ns:
            stageB(it - 1)
        if 0 <= it - 2 < ns:
            stageC(it - 2)
    k.finish([oT])


def build_sb(nheads=4, seq=S):
    nc = bass.Bass("TRN2", target_bir_lowering=False)
    k = KB(nc)
    io = {}
    io["qT"] = k.dram("qT", [nheads * 128, seq], BF16, "ExternalInput")
    io["kT"] = k.dram("kT", [nheads * 128, seq], BF16, "ExternalInput")
    io["v"] = k.dram("v", [seq, nheads * 128], BF16, "ExternalInput")
    io["oT_out"] = k.dram("oT_out", [nheads * 128, seq], BF16, "ExternalOutput")
    c = emit_consts(k)
    emit_masks(k, c)
    emit_sb_phase(k, c, io, nheads, seq)
    return nc, k


NEG = -30000.0
AX = mybir.AxisListType


def emit_gdn_consts(k, c):
    def mk(name, dt, init, pattern, cm, cmp, fill):
        t = k.sb(name, [128, 128], dt)
        k.op(k.pool, lambda e: e.memset(t[:], init), writes=[t])
        if pattern is not None:
            k.op(k.pool, lambda e: e.affine_select(out=t[:], in_=t[:], pattern=pattern, compare_op=cmp, fill=fill,
                                                   base=0, channel_multiplier=cm), reads=[t], writes=[t])
        c[name] = t
    mk("triincl", F32, 1.0, [[1, 128]], -1, ALU.is_ge, 0.0)
    mk("strictgt", F32, 1.0, [[-1, 128]], 1, ALU.is_gt, 0.0)
    mk("mneg_strict", F32, 0.0, [[-1, 128]], 1, ALU.is_gt, NEG)
    mk("mneg_inclT", F32, 0.0, [[1, 128]], -1, ALU.is_ge, NEG)
    mk("ident_f", F32, 1.0, [[-1, 128]], 1, ALU.is_equal, 0.0)
    mk("ident_bf", BF16, 1.0, [[-1, 128]], 1, ALU.is_equal, 0.0)
    mk("ones_f", F32, 1.0, None, 0, None, 0.0)


def emit_gdn_phase(k, c, io, seq=S):
    sp, pool, pe, act, dve = k.sp, k.pool, k.pe, k.act, k.dve
    NH = 4
    nst = seq // 512

    W = k.sb("gW", [128, 8, NH * 4 * 128], BF16)
    k.dma(pool, W[:], io["w_qkvg"].t.rearrange("(c p) n -> p c n", p=128), reads=[io["w_qkvg"]], writes=[W])
    Wab = k.sb("gWab", [128, 8, 8], BF16)
    k.dma(pool, Wab[:], io["w_ab"].t.rearrange("(c p) n -> p c n", p=128), reads=[io["w_ab"]], writes=[Wab])
    convw = k.sb("convw", [128, 48], F32)
    k.dma(sp, convw[:], io["convw"][:, :], reads=[io["convw"]], writes=[convw])
    dtb = k.sb("dtb", [128, 4], F32)
    k.dma(sp, dtb[:], io["dtb"][:, :], reads=[io["dtb"]], writes=[dtb])
    nega = k.sb("nega", [128, 4], F32)
    k.dma(sp, nega[:], io["alog"][:, :], reads=[io["alog"]], writes=[nega])
    k.op(act, lambda e: e.activation(out=nega[:], in_=nega[:], func=AF.Exp), reads=[nega], writes=[nega])
    k.op(dve, lambda e: e.tensor_scalar(out=nega[:], in0=nega[:], scalar1=-1.0, scalar2=None, op0=ALU.mult), reads=[nega], writes=[nega])
    normg = k.sb("normg", [128, 128], F32)
    k.dma(sp, normg[:], io["normg"][:, :], reads=[io["normg"]], writes=[normg])
    lnb_q = k.sb("lnb_q", [128, 1], F32)
    k.op(pool, lambda e: e.memset(lnb_q[:], float(np.log(128 ** -0.5))), writes=[lnb_q])

    projb = Rot([k.ps("projb%d" % i, [128, 512], F32) for i in range(2)])
    ssb = k.ps("ssb", [128, 512], F32)
    fb = Rot([k.ps("fb%d" % i, [128, 512], F32) for i in range(4)])
    bfb = k.ps("bfb", [128, 1024], BF16)

    def qv(bank, i):
        return bank[:, i * 128:(i + 1) * 128]

    S32 = [k.sb("S32_%d" % h, [128, 128], F32) for h in range(NH)]
    Sbf = [k.sb("Sbf_%d" % h, [128, 128], BF16) for h in range(NH)]
    for h in range(NH):
        k.op(pool, lambda e, h=h: e.memset(S32[h][:], 0.0), writes=[S32[h]])
        k.op(pool, lambda e, h=h: e.memset(Sbf[h][:], 0.0), writes=[Sbf[h]])
    pre = [k.sb("pre%d" % ch, [128, 515], F32) for ch in range(12)]
    for ch in range(12):
        k.op(pool, lambda e, ch=ch: e.memset(pre[ch][:, 0:3], 0.0), writes=[pre[ch]])

    hnb = [k.sb("ghn%d" % i, [128, 8, 512], BF16) for i in range(2)]
    hnv = io["hnT"].t.rearrange("(c p) t -> p c t", p=128)
    acc_rot = Rot([k.sb("cacc%d" % i, [128, 512], F32) for i in range(2)])
    cs_rot = Rot([k.sb("ccs%d" % i, [128, 512], F32) for i in range(2)])
    sq_rot = Rot([k.sb("gsq%d" % i, [128, 512], BF16) for i in range(2)])
    tmp_rot = Rot([k.sb("gtmp%d" % i, [128, 512], F32) for i in range(3)])
    qTs = [[k.sb("qTs%d_%d" % (p, h), [128, 512], BF16) for h in range(NH)] for p in range(2)]
    kTs = [[k.sb("kTs%d_%d" % (p, h), [128, 512], BF16) for h in range(NH)] for p in range(2)]
    vTs = [[k.sb("vTs%d_%d" % (p, h), [128, 512], BF16) for h in range(NH)] for p in range(2)]
    sgT = [[k.sb("sgT%d_%d" % (p, h), [128, 512], BF16) for h in range(NH)] for p in range(2)]
    ogT = [[k.sb("ogT%d_%d" % (p, h), [128, 512], BF16) for h in range(NH)] for p in range(2)]

    def small(name, w=4):
        return [[k.sb("%s%d_%d" % (name, p, j), [128, w], F32) for j in range(4)] for p in range(2)]
    gcol, beta, nbeta, eG, nbeG, eGL, eGLmG = (small(n) for n in ("gcol", "beta", "nbeta", "eG", "nbeG", "eGL", "eGLmG"))
    st1, st2 = small("st1"), small("st2")

    def tile_tmp(name, dt, single=False):
        l = [[k.sb("%s%d_%d" % (name, p, h), [128, 128], dt) for h in range(NH)] for p in range(1 if single else 2)]
        return l if not single else [l[0], l[0]]
    GM1, GM2, Dm, DTm, sqo = (tile_tmp(n, F32, True) for n in ("GM1", "GM2", "Dm", "DTm", "sqo"))
    bV, tmpo, otok = (tile_tmp(n, F32) for n in ("bV", "tmpo", "otok"))
    CH = F32
    QKDT, Kd, vnb, onb = (tile_tmp(n, BF16) for n in ("QKDT", "Kd", "vnb", "onb"))
    Bm, BTm, Xb = (tile_tmp(n, CH) for n in ("Bm", "BTm", "Xb"))
    Pm = [tile_tmp("Pm%d" % i, CH) for i in range(2)]
    PTm = [tile_tmp("PTm%d" % i, CH) for i in range(2)]
    TTm = [tile_tmp("TTm%d" % i, CH) for i in range(2)]
    ssv = [[k.sb("ssv%d_%d" % (p, h), [128, 1], F32) for h in range(NH)] for p in range(2)]
    lnv1 = [[k.sb("lnv1%d_%d" % (p, h), [128, 1], F32) for h in range(NH)] for p in range(2)]
    rsv = [[k.sb("rsv%d_%d" % (p, h), [128, 1], F32) for h in range(NH)] for p in range(2)]

    def mm(out_b, out_ap, l_b, l_ap, r_b, r_ap, start=True, stop=True):
        k.op(pe, lambda e: e.matmul(out_ap, lhsT=l_ap, rhs=r_ap, start=start, stop=stop), reads=[l_b, r_b], writes=[out_b])

    def tr(out_ap, in_b, in_ap):
        k.op(pe, lambda e: e.transpose(out=out_ap, in_=in_ap, identity=c["ident_bf"][:]), reads=[in_b, c["ident_bf"]], writes=[bfb])

    def ev(i):
        return bfb[:, i * 128:(i + 1) * 128]

    HL = range(NH)
    tile_no = 0
    for st in range(nst):
        sp_ = st % 2
        hb = hnb[st % 2]
        k.dma(sp, hb[:], hnv[:, :, st * 512:(st + 1) * 512], reads=[io["hnT"]], writes=[hb])
        for hl in HL:
            for typ in range(4):
                ps = projb.next()
                col = (hl * 4 + typ) * 128
                for kc in range(8):
                    mm(ps, ps[:], W, W[:, kc, col:col + 128], hb, hb[:, kc, :], start=(kc == 0), stop=(kc == 7))
                if typ == 3:
                    k.op(act, lambda e, ps=ps: e.activation(out=sgT[sp_][hl][:], in_=ps[:], func=AF.Silu), reads=[ps], writes=[sgT[sp_][hl]])
                    continue
                ch = hl * 3 + typ
                pr = pre[ch]
                k.op(act, lambda e, ps=ps, pr=pr: e.activation(out=pr[:, 3:515], in_=ps[:], func=AF.Copy), reads=[ps], writes=[pr])
                acc = acc_rot.next()
                k.op(dve, lambda e, pr=pr, acc=acc, ch=ch: e.tensor_scalar(out=acc[:], in0=pr[:, 0:512], scalar1=convw[:, ch * 4:ch * 4 + 1], scalar2=None, op0=ALU.mult),
                     reads=[pr, convw], writes=[acc])
                for tap in range(1, 4):
                    k.op(dve, lambda e, pr=pr, acc=acc, ch=ch, tap=tap: e.scalar_tensor_tensor(out=acc[:], in0=pr[:, tap:tap + 512], scalar=convw[:, ch * 4 + tap:ch * 4 + tap + 1],
                                                                                           in1=acc[:], op0=ALU.mult, op1=ALU.add), reads=[pr, convw, acc], writes=[acc])
                k.op(pool, lambda e, pr=pr: e.tensor_copy(out=pr[:, 0:3], in_=pr[:, 512:515]), reads=[pr], writes=[pr])
                if typ == 2:
                    k.op(act, lambda e, acc=acc: e.activation(out=vTs[sp_][hl][:], in_=acc[:], func=AF.Silu), reads=[acc], writes=[vTs[sp_][hl]])
                else:
                    cs = cs_rot.next()
                    k.op(act, lambda e, acc=acc, cs=cs: e.activation(out=cs[:], in_=acc[:], func=AF.Silu), reads=[acc], writes=[cs])
                    rstd = emit_rstd(k, c, [(cs[:], cs)], 1.0, ssb, sq_rot, tmp_rot, ln_bias=(lnb_q if typ == 0 else None))
                    dst = qTs[sp_][hl] if typ == 0 else kTs[sp_][hl]
                    k.op(dve, lambda e, cs=cs, rstd=rstd, dst=dst: e.tensor_tensor(out=dst[:], in0=cs[:], in1=rstd[:], op=ALU.mult), reads=[cs, rstd], writes=[dst])
        pab = fb.next()
        for j in range(4):
            jc = slice(j * 128, (j + 1) * 128)
            for kc in range(8):
                mm(pab, pab[:, j * 8:j * 8 + 8], hb, hb[:, kc, jc], Wab, Wab[:, kc, :], start=(kc == 0), stop=(kc == 7))
        ab_sb = [st1[sp_][j] for j in range(4)]
        bb_sb = [st2[sp_][j] for j in range(4)]
        for j in range(4):
            k.op(dve, lambda e, j=j: e.tensor_tensor(out=ab_sb[j][:], in0=pab[:, j * 8:j * 8 + 4], in1=dtb[:], op=ALU.add), reads=[pab, dtb], writes=[ab_sb[j]])
            k.op(dve, lambda e, j=j: e.tensor_copy(out=bb_sb[j][:], in_=pab[:, j * 8 + 4:j * 8 + 8]), reads=[pab], writes=[bb_sb[j]])
        pG = fb.next()
        for j in range(4):
            a1, a2 = ab_sb[j], bb_sb[j]
            k.op(act, lambda e, a1=a1: e.activation(out=a1[:], in_=a1[:], func=AF.Exp), reads=[a1], writes=[a1])
            k.op(act, lambda e, a1=a1: e.activation(out=a1[:], in_=a1[:], func=AF.Ln, bias=c["one"][:], scale=1.0), reads=[a1, c["one"]], writes=[a1])
            g_ = gcol[sp_][j]
            k.op(dve, lambda e, a1=a1, g_=g_: e.tensor_tensor(out=g_[:], in0=a1[:], in1=nega[:], op=ALU.mult), reads=[a1, nega], writes=[g_])
            b_ = beta[sp_][j]
            k.op(act, lambda e, a2=a2: e.activation(out=a2[:], in_=a2[:], func=AF.Exp, scale=-1.0), reads=[a2], writes=[a2])
            k.op(dve, lambda e, a2=a2: e.tensor_scalar(out=a2[:], in0=a2[:], scalar1=1.0, scalar2=None, op0=ALU.add), reads=[a2], writes=[a2])
            k.op(dve, lambda e, a2=a2, b_=b_: e.reciprocal(out=b_[:], in_=a2[:]), reads=[a2], writes=[b_])
            k.op(dve, lambda e, j=j, b_=b_: e.tensor_scalar(out=nbeta[sp_][j][:], in0=b_[:], scalar1=-1.0, scalar2=None, op0=ALU.mult), reads=[b_], writes=[nbeta[sp_][j]])
            mm(pG, pG[:, j * 8:j * 8 + 4], c["triincl"], c["triincl"][:], g_, g_[:])
            mm(pG, pG[:, j * 8 + 4:j * 8 + 8], c["ones_f"], c["ones_f"][:], g_, g_[:])
        for j in range(4):
            a2 = bb_sb[j]
            k.op(act, lambda e, j=j: e.activation(out=eG[sp_][j][:], in_=pG[:, j * 8:j * 8 + 4], func=AF.Exp), reads=[pG], writes=[eG[sp_][j]])
            k.op(act, lambda e, j=j: e.activation(out=eGL[sp_][j][:], in_=pG[:, j * 8 + 4:j * 8 + 8], func=AF.Exp), reads=[pG], writes=[eGL[sp_][j]])
            k.op(act, lambda e, j=j, a2=a2: e.activation(out=a2[:], in_=pG[:, j * 8:j * 8 + 4], func=AF.Copy), reads=[pG], writes=[a2])
        for j in range(4):
            a2 = bb_sb[j]
            k.op(dve, lambda e, j=j, a2=a2: e.tensor_tensor(out=a2[:], in0=pG[:, j * 8 + 4:j * 8 + 8], in1=a2[:], op=ALU.subtract), reads=[pG, a2], writes=[a2])
            k.op(act, lambda e, j=j, a2=a2: e.activation(out=eGLmG[sp_][j][:], in_=a2[:], func=AF.Exp), reads=[a2], writes=[eGLmG[sp_][j]])
            k.op(dve, lambda e, j=j: e.scalar_tensor_tensor(out=nbeG[sp_][j][:], in0=eG[sp_][j][:], scalar=-1.0, in1=beta[sp_][j][:], op0=ALU.mult, op1=ALU.mult),
                 reads=[eG[sp_][j], beta[sp_][j]], writes=[nbeG[sp_][j]])

        for j in range(4):
            jc = slice(j * 128, (j + 1) * 128)
            tp = tile_no % 2
            tile_no += 1
            sc = lambda arr, hl: arr[sp_][j][:, hl:hl + 1]
            for hl in HL:
                g1, g2 = GM1[tp][hl], GM2[tp][hl]
                k.op(pool, lambda e, g1=g1, hl=hl: e.tensor_scalar(out=g1[:], in0=c["triincl"][:], scalar1=sc(gcol, hl), scalar2=None, op0=ALU.mult),
                     reads=[c["triincl"], gcol[sp_][j]], writes=[g1])
                k.op(pool, lambda e, g2=g2, hl=hl: e.tensor_scalar(out=g2[:], in0=c["strictgt"][:], scalar1=sc(gcol, hl), scalar2=None, op0=ALU.mult),
                     reads=[c["strictgt"], gcol[sp_][j]], writes=[g2])
            b_dd = fb.next()
            for hl in HL:
                mm(b_dd, qv(b_dd, hl), GM1[tp][hl], GM1[tp][hl][:], c["strictgt"], c["strictgt"][:], start=True, stop=False)
                mm(b_dd, qv(b_dd, hl), c["ident_f"], c["ident_f"][:], c["mneg_strict"], c["mneg_strict"][:], start=False, stop=True)
            b_ddT = fb.next()
            for hl in HL:
                mm(b_ddT, qv(b_ddT, hl), GM2[tp][hl], GM2[tp][hl][:], c["triincl"], c["triincl"][:], start=True, stop=False)
                mm(b_ddT, qv(b_ddT, hl), c["ident_f"], c["ident_f"][:], c["mneg_inclT"], c["mneg_inclT"][:], start=False, stop=True)
            for hl in HL:
                k.op(act, lambda e, hl=hl: e.activation(out=Dm[tp][hl][:], in_=qv(b_dd, hl), func=AF.Exp), reads=[b_dd], writes=[Dm[tp][hl]])
            for hl in HL:
                k.op(act, lambda e, hl=hl: e.activation(out=DTm[tp][hl][:], in_=qv(b_ddT, hl), func=AF.Exp), reads=[b_ddT], writes=[DTm[tp][hl]])
            b_kk = fb.next()
            for hl in HL:
                kT_ = kTs[sp_][hl]
                mm(b_kk, qv(b_kk, hl), kT_, kT_[:, jc], kT_, kT_[:, jc])
            b_kq = fb.next()
            for hl in HL:
                kT_, qT_ = kTs[sp_][hl], qTs[sp_][hl]
                mm(b_kq, qv(b_kq, hl), kT_, kT_[:, jc], qT_, qT_[:, jc])
            for hl in HL:
                k.op(dve, lambda e, hl=hl: e.scalar_tensor_tensor(out=Bm[tp][hl][:], in0=qv(b_kk, hl), scalar=sc(nbeta, hl), in1=Dm[tp][hl][:], op0=ALU.mult, op1=ALU.mult),
                     reads=[b_kk, nbeta[sp_][j], Dm[tp][hl]], writes=[Bm[tp][hl]])
            for hl in HL:
                k.op(dve, lambda e, hl=hl: e.tensor_tensor(out=QKDT[tp][hl][:], in0=qv(b_kq, hl), in1=DTm[tp][hl][:], op=ALU.mult),
                     reads=[b_kq, DTm[tp][hl]], writes=[QKDT[tp][hl]])
            b_bt = fb.next()
            for hl in HL:
                k.op(pe, lambda e, hl=hl: e.transpose(out=qv(b_bt, hl), in_=Bm[tp][hl][:], identity=c["ident_f"][:]),
                     reads=[Bm[tp][hl], c["ident_f"]], writes=[b_bt])
            for hl in HL:
                k.op(act, lambda e, hl=hl: e.activation(out=BTm[tp][hl][:], in_=qv(b_bt, hl), func=AF.Copy), reads=[b_bt], writes=[BTm[tp][hl]])
                k.op(pool, lambda e, hl=hl: e.tensor_tensor(out=TTm[0][tp][hl][:], in0=BTm[tp][hl][:], in1=c["ident_f"][:], op=ALU.add),
                     reads=[BTm[tp][hl], c["ident_f"]], writes=[TTm[0][tp][hl]])
            for hl in HL:
                tr(ev(hl), vTs[sp_][hl], vTs[sp_][hl][:, jc])
                tr(ev(4 + hl), kTs[sp_][hl], kTs[sp_][hl][:, jc])
            for hl in HL:
                k.op(act, lambda e, hl=hl: e.activation(out=bV[tp][hl][:], in_=ev(hl), func=AF.Copy, scale=sc(beta, hl)), reads=[bfb, beta[sp_][j]], writes=[bV[tp][hl]])
                k.op(act, lambda e, hl=hl: e.activation(out=Kd[tp][hl][:], in_=ev(4 + hl), func=AF.Copy, scale=sc(eGLmG, hl)), reads=[bfb, eGLmG[sp_][j]], writes=[Kd[tp][hl]])
            Pold = {hl: Bm[tp][hl] for hl in HL}
            PTold = {hl: BTm[tp][hl] for hl in HL}
            TTold = {hl: TTm[0][tp][hl] for hl in HL}
            for lvl in range(1, 7):
                b_p = fb.next()
                for hl in HL:
                    mm(b_p, qv(b_p, hl), PTold[hl], PTold[hl][:], Pold[hl], Pold[hl][:])
                b_pt = None
                if lvl < 6:
                    b_pt = fb.next()
                    for hl in HL:
                        mm(b_pt, qv(b_pt, hl), Pold[hl], Pold[hl][:], PTold[hl], PTold[hl][:])
                Pn = {hl: Pm[lvl % 2][tp][hl] for hl in HL}
                for hl in HL:
                    k.op(act, lambda e, hl=hl: e.activation(out=Pn[hl][:], in_=qv(b_p, hl), func=AF.Copy), reads=[b_p], writes=[Pn[hl]])
                PTn = {hl: None for hl in HL}
                if lvl < 6:
                    PTn = {hl: PTm[lvl % 2][tp][hl] for hl in HL}
                    for hl in HL:
                        k.op(dve, lambda e, hl=hl: e.tensor_copy(out=PTn[hl][:], in_=qv(b_pt, hl)), reads=[b_pt], writes=[PTn[hl]])
                b_t = fb.next()
                for hl in HL:
                    mm(b_t, qv(b_t, hl), Pn[hl], Pn[hl][:], TTold[hl], TTold[hl][:])
                TTn = {hl: TTm[lvl % 2][tp][hl] for hl in HL}
                for hl in HL:
                    k.op(dve, lambda e, hl=hl, to=TTold[hl]: e.tensor_tensor(out=TTn[hl][:], in0=qv(b_t, hl), in1=to[:], op=ALU.add),
                         reads=[b_t, TTold[hl]], writes=[TTn[hl]])
                Pold, PTold, TTold = Pn, PTn, TTn
            b_ks = fb.next()
            for hl in HL:
                kT_ = kTs[sp_][hl]
                mm(b_ks, qv(b_ks, hl), kT_, kT_[:, jc], Sbf[hl], Sbf[hl][:])
            b_qs = fb.next()
            for hl in HL:
                qT_ = qTs[sp_][hl]
                mm(b_qs, qv(b_qs, hl), qT_, qT_[:, jc], Sbf[hl], Sbf[hl][:])
            for hl in HL:
                k.op(dve, lambda e, hl=hl: e.scalar_tensor_tensor(out=Xb[tp][hl][:], in0=qv(b_ks, hl), scalar=sc(nbeG, hl), in1=bV[tp][hl][:], op0=ALU.mult, op1=ALU.add),
                     reads=[b_ks, nbeG[sp_][j], bV[tp][hl]], writes=[Xb[tp][hl]])
            for hl in HL:
                k.op(act, lambda e, hl=hl: e.activation(out=tmpo[tp][hl][:], in_=qv(b_qs, hl), func=AF.Copy, scale=sc(eG, hl)), reads=[b_qs, eG[sp_][j]], writes=[tmpo[tp][hl]])
            b_vn = fb.next()
            for hl in HL:
                mm(b_vn, qv(b_vn, hl), TTold[hl], TTold[hl][:], Xb[tp][hl], Xb[tp][hl][:])
            for hl in HL:
                k.op(act, lambda e, hl=hl: e.activation(out=vnb[tp][hl][:], in_=qv(b_vn, hl), func=AF.Copy), reads=[b_vn], writes=[vnb[tp][hl]])
            b_o = fb.next()
            for hl in HL:
                mm(b_o, qv(b_o, hl), QKDT[tp][hl], QKDT[tp][hl][:], vnb[tp][hl], vnb[tp][hl][:])
            b_sn = fb.next()
            for hl in HL:
                mm(b_sn, qv(b_sn, hl), Kd[tp][hl], Kd[tp][hl][:], vnb[tp][hl], vnb[tp][hl][:])
            for hl in HL:
                k.op(dve, lambda e, hl=hl: e.tensor_tensor(out=otok[tp][hl][:], in0=qv(b_o, hl), in1=tmpo[tp][hl][:], op=ALU.add),
                     reads=[b_o, tmpo[tp][hl]], writes=[otok[tp][hl]])
            for hl in HL:
                k.op(dve, lambda e, hl=hl: e.scalar_tensor_tensor(out=S32[hl][:], in0=S32[hl][:], scalar=sc(eGL, hl), in1=qv(b_sn, hl), op0=ALU.mult, op1=ALU.add),
                     reads=[S32[hl], eGL[sp_][j], b_sn], writes=[S32[hl]])
                k.op(act, lambda e, hl=hl: e.activation(out=Sbf[hl][:], in_=S32[hl][:], func=AF.Copy), reads=[S32[hl]], writes=[Sbf[hl]])
            for hl in HL:
                k.op(act, lambda e, hl=hl: e.activation(out=sqo[tp][hl][:], in_=otok[tp][hl][:], func=AF.Square), reads=[otok[tp][hl]], writes=[sqo[tp][hl]])
                k.op(dve, lambda e, hl=hl: e.tensor_reduce(out=ssv[tp][hl][:], in_=sqo[tp][hl][:], axis=AX.X, op=ALU.add), reads=[sqo[tp][hl]], writes=[ssv[tp][hl]])
                k.op(act, lambda e, hl=hl: e.activation(out=lnv1[tp][hl][:], in_=ssv[tp][hl][:], func=AF.Ln, bias=c["eps"][:], scale=1.0 / 128),
                     reads=[ssv[tp][hl], c["eps"]], writes=[lnv1[tp][hl]])
                k.op(act, lambda e, hl=hl: e.activation(out=rsv[tp][hl][:], in_=lnv1[tp][hl][:], func=AF.Exp, scale=-0.5), reads=[lnv1[tp][hl]], writes=[rsv[tp][hl]])
                k.op(pool, lambda e, hl=hl: e.tensor_scalar(out=otok[tp][hl][:], in0=otok[tp][hl][:], scalar1=rsv[tp][hl][:, 0:1], scalar2=None, op0=ALU.mult),
                     reads=[otok[tp][hl], rsv[tp][hl]], writes=[otok[tp][hl]])
                k.op(pool, lambda e, hl=hl: e.tensor_tensor(out=onb[tp][hl][:], in0=otok[tp][hl][:], in1=normg[:], op=ALU.mult),
                     reads=[otok[tp][hl], normg], writes=[onb[tp][hl]])
            for hl in HL:
                tr(ev(hl), onb[tp][hl], onb[tp][hl][:])
            for hl in HL:
                k.op(dve, lambda e, hl=hl: e.tensor_tensor(out=ogT[sp_][hl][:, jc], in0=ev(hl), in1=sgT[sp_][hl][:, jc], op=ALU.mult),
                     reads=[bfb, sgT[sp_][hl]], writes=[ogT[sp_][hl]])
        for hl in HL:
            k.dma(sp, io["ogT_out"][hl * 128:(hl + 1) * 128, st * 512:(st + 1) * 512], ogT[sp_][hl][:], reads=[ogT[sp_][hl]], untracked_out=io["ogT_out"])
    k.finish([io["ogT_out"]])


def build_gdn(seq=S):
    nc = bass.Bass("TRN2", target_bir_lowering=False)
    k = KB(nc)
    io = {}
    io["hnT"] = k.dram("hnT", [D, seq], BF16, "ExternalInput")
    io["w_qkvg"] = k.dram("w_qkvg", [D, 2048], F32, "ExternalInput")
    io["w_ab"] = k.dram("w_ab", [D, 8], F32, "ExternalInput")
    io["convw"] = k.dram("convw", [128, 48], F32, "ExternalInput")
    io["dtb"] = k.dram("dtb", [128, 4], F32, "ExternalInput")
    io["alog"] = k.dram("alog", [128, 4], F32, "ExternalInput")
    io["normg"] = k.dram("normg", [128, 128], F32, "ExternalInput")
    io["ogT_out"] = k.dram("ogT_out", [512, seq], BF16, "ExternalOutput")
    c = emit_consts(k)
    emit_gdn_consts(k, c)
    emit_gdn_phase(k, c, io, seq)
    return nc, k


def gdn_host_inputs(inp, layer, r):
    heads = [4 * r + hl for hl in range(4)]
    w_in = inp["gdn_w_in"][layer]
    cols = []
    for h in heads:
        for typ in range(4):
            cols.append(w_in[:, typ * 1024 + h * 128: typ * 1024 + (h + 1) * 128])
    w_qkvg = np.ascontiguousarray(np.concatenate(cols, axis=1))
    w_ab = np.ascontiguousarray(np.concatenate([w_in[:, 4096 + heads[0]:4096 + heads[0] + 4], w_in[:, 4104 + heads[0]:4104 + heads[0] + 4]], axis=1))
    cw = inp["gdn_conv"][layer]
    convw = np.zeros((128, 48), np.float32)
    for hl, h in enumerate(heads):
        for typ in range(3):
            ch = hl * 3 + typ
            convw[:, ch * 4:(ch + 1) * 4] = cw[:, typ * 1024 + h * 128: typ * 1024 + (h + 1) * 128].T
    dtb = np.ascontiguousarray(np.broadcast_to(inp["gdn_dt_bias"][layer][heads[0]:heads[0] + 4][None, :], (128, 4))).astype(np.float32)
    alog = np.ascontiguousarray(np.broadcast_to(inp["gdn_a_log"][layer][heads[0]:heads[0] + 4][None, :], (128, 4))).astype(np.float32)
    normg = np.ascontiguousarray(np.broadcast_to(inp["gdn_norm"][layer][None, :], (128, 128))).astype(np.float32)
    return dict(w_qkvg=w_qkvg, w_ab=w_ab, convw=convw, dtb=dtb, alog=alog, normg=normg)


_PROGS = {}


def _prog(key, builder):
    if key not in _PROGS:
        _PROGS[key] = builder()[0]
    return _PROGS[key]


def _run(nc, maps):
    res = run_bass_kernel_spmd(nc, maps, core_ids=list(range(NCORES)))
    return res.results


def _gains(inp, layer, nxt_layer, kv, q_layer):
    g = np.zeros((128, NGCOL), np.float32)
    if layer is not None:
        g[:, 0:8] = col8(inp["ln_ffn"][layer])
        g[:, 8:16] = col8(inp["ln_ple"][layer])
    if kv:
        g[:, 16:24] = col8(inp["kv_norm"])
        g[:, 24] = inp["k_norm"]
    if nxt_layer is not None:
        g[:, 25:33] = col8(inp["ln_mix"][nxt_layer])
    if q_layer is not None:
        g[:, 33] = inp["sb_q_norm"][q_layer]
    return g


def kernel(**inp):
    inp = {k_: np.asarray(v) for k_, v in inp.items()}
    x, p = inp["x"], inp["p"]
    f32 = np.float32

    def tok(c):
        return c // 2, slice((c % 2) * NT, (c % 2 + 1) * NT)

    nc = _prog("tokA", lambda: build_token(dict(tail=False, next="gdn")))
    g = _gains(inp, None, 0, False, None)
    maps = []
    for c in range(NCORES):
        b, sl = tok(c)
        maps.append({"hT_in": np.ascontiguousarray(x[b, sl].T), "gains": g})
    res = _run(nc, maps)
    hT = [res[c]["hT_out"] for c in range(NCORES)]
    hnT = [res[c]["hnT_out"] for c in range(NCORES)]

    def tail_maps(layer, oT_full, w_o, extra):
        maps = []
        for c in range(NCORES):
            b, sl = tok(c)
            m = {"hT_in": hT[c], "oT": np.ascontiguousarray(oT_full[b][:, sl]), "w_o": w_o,
                 "ffn_w_in": inp["ffn_w_in"][layer], "ffn_w_out": inp["ffn_w_out"][layer],
                 "ple_w_gate": inp["ple_w_gate"][layer], "ple_w_proj": inp["ple_w_proj"][layer],
                 "pT": np.ascontiguousarray(p[layer, b, sl].T)}
            m.update(extra)
            maps.append(m)
        return maps

    k_full = v_full = None
    for layer in range(DEPTH):
        if layer < 2:
            nc = _prog("gdn", build_gdn)
            maps = []
            for c in range(NCORES):
                b, r = c // 2, c % 2
                m = gdn_host_inputs(inp, layer, r)
                m["hnT"] = np.ascontiguousarray(np.concatenate([hnT[2 * b], hnT[2 * b + 1]], axis=1))
                maps.append(m)
            res = _run(nc, maps)
            oT_full = [np.concatenate([res[2 * b]["ogT_out"], res[2 * b + 1]["ogT_out"]], axis=0) for b in range(NB)]
            w_o = inp["gdn_w_out"][layer]
        else:
            nc = _prog("sb", build_sb)
            maps = []
            for c in range(NCORES):
                b, r = c // 2, c % 2
                rows = slice(512 * r, 512 * (r + 1))
                q_full = np.concatenate([qT[2 * b], qT[2 * b + 1]], axis=1)
                maps.append({"qT": np.ascontiguousarray(q_full[rows]), "kT": np.ascontiguousarray(k_full[b][rows]),
                             "v": np.ascontiguousarray(v_full[b][:, rows])})
            res = _run(nc, maps)
            oT_full = [np.concatenate([res[2 * b]["oT_out"], res[2 * b + 1]["oT_out"]], axis=0) for b in range(NB)]
            w_o = inp["sb_w_out"][layer - 2]
        if layer == 0:
            nc = _prog("tokB", lambda: build_token(dict(tail=True, next="gdn")))
            maps = tail_maps(layer, oT_full, w_o, {"gains": _gains(inp, layer, 1, False, None)})
        elif layer == 1:
            nc = _prog("tokC", lambda: build_token(dict(tail=True, next="sb", kv=True)))
            maps = tail_maps(layer, oT_full, w_o, {"gains": _gains(inp, layer, 2, True, 0), "w_kv": inp["w_kv"], "w_q": inp["sb_w_q"][0]})
        elif layer == 2:
            nc = _prog("tokD", lambda: build_token(dict(tail=True, next="sb")))
            maps = tail_maps(layer, oT_full, w_o, {"gains": _gains(inp, layer, 3, False, 1), "w_q": inp["sb_w_q"][1]})
        else:
            nc = _prog("tokE", lambda: build_token(dict(tail=True, next=None)))
            maps = tail_maps(layer, oT_full, w_o, {"gains": _gains(inp, layer, None, False, None)})
        res = _run(nc, maps)
        hT = [res[c]["hT_out"] for c in range(NCORES)]
        if layer == 0:
            hnT = [res[c]["hnT_out"] for c in range(NCORES)]
        if layer == 1:
            k_full = [np.concatenate([res[2 * b]["kT_out"], res[2 * b + 1]["kT_out"]], axis=1) for b in range(NB)]
            v_full = [np.concatenate([res[2 * b]["v_out"], res[2 * b + 1]["v_out"]], axis=0) for b in range(NB)]
        if layer in (1, 2):
            qT = [res[c]["qT_out"] for c in range(NCORES)]

    out = np.empty((NB, S, D), f32)
    for c in range(NCORES):
        b, sl = tok(c)
        out[b, sl] = np.asarray(hT[c], f32).T
    return out
```

```python
import numpy as np
import ml_dtypes
import concourse.bass as bass
import concourse.mybir as mybir
from concourse.bass_utils import run_bass_kernel_spmd

F32 = mybir.dt.float32
BF16 = mybir.dt.bfloat16
AF = mybir.ActivationFunctionType
ALU = mybir.AluOpType
NPBF = ml_dtypes.bfloat16

D = 1024
S = 4096
NB = 4
DEPTH = 4
NT = 2048
TT = 512
NTT = NT // TT
FH = 2816
NJ = FH // 128
EPS = 1e-6
NCORES = 8


class Buf:
    __slots__ = ("name", "t", "w", "r", "dsem", "dcnt", "excl")

    def __init__(self, name, t, excl=False):
        self.name = name
        self.t = t
        self.w = None
        self.r = {}
        self.dsem = None
        self.dcnt = 0
        self.excl = excl

    def __getitem__(self, idx):
        return self.t[idx]


class Eng:
    def __init__(self, k, name):
        self.name = name
        self.e = getattr(k.nc, name)
        self.sem = k.nc.alloc_semaphore("es_" + name)
        self.cnt = 0
        self.waited = {}

    def wait(self, dep):
        sem, val = dep
        key = id(sem)
        if self.waited.get(key, 0) >= val:
            return
        self.waited[key] = val
        self.e.wait_ge(sem, val)


class KB:
    def __init__(self, nc, same_engine_raw=True):
        self.nc = nc
        self.same_engine_raw = same_engine_raw
        self.pe = Eng(self, "tensor")
        self.dve = Eng(self, "vector")
        self.act = Eng(self, "scalar")
        self.pool = Eng(self, "gpsimd")
        self.sp = Eng(self, "sync")
        self.n_inst = 0
        self._uid = 0

    def uid(self, s):
        self._uid += 1
        return "%s_%d" % (s, self._uid)

    def sb(self, name, shape, dt):
        return Buf(name, self.nc.alloc_sbuf_tensor(self.uid(name), list(shape), dt))

    def ps(self, name, shape, dt=F32):
        return Buf(name, self.nc.alloc_psum_tensor(self.uid(name), list(shape), dt), excl=True)

    def dram(self, name, shape, dt, kind):
        return Buf(name, self.nc.dram_tensor(name, list(shape), dt, kind=kind).ap())

    def _deps(self, eng, reads, writes):
        raw = []
        war = []
        for b in reads:
            if b.w is not None:
                raw.append(b.w)
            if b.excl:
                war.extend(b.r.values())
        for b in writes:
            if b.w is not None:
                raw.append(b.w)
            war.extend(b.r.values())
        for d in raw:
            if d[0] is eng.sem:
                if not self.same_engine_raw or eng is self.pe or eng is self.sp:
                    continue
            eng.wait(d)
        for d in war:
            if d[0] is eng.sem:
                continue
            eng.wait(d)

    def op(self, eng, fn, reads=(), writes=()):
        self._deps(eng, reads, writes)
        inst = fn(eng.e)
        eng.cnt += 1
        inst.then_inc(eng.sem, 1)
        tk = (eng.sem, eng.cnt)
        for b in reads:
            b.r[id(tk[0])] = tk
        for b in writes:
            b.w = tk
            b.r = {}
        self.n_inst += 1
        return inst

    def dma(self, eng, out_ap, in_ap, reads=(), writes=(), sembuf=None, untracked_out=None, **kw):
        self._deps(eng, reads, writes)
        if untracked_out is not None:
            sembuf = untracked_out
            writes = [untracked_out]
        sb_ = sembuf or (writes[0] if writes else reads[0])
        if sb_.dsem is None:
            sb_.dsem = self.nc.alloc_semaphore(self.uid("ds_" + sb_.name))
        inst = eng.e.dma_start(out=out_ap, in_=in_ap, **kw)
        sb_.dcnt += 16
        inst.then_inc(sb_.dsem, 16)
        tk = (sb_.dsem, sb_.dcnt)
        for b in reads:
            b.r[id(tk[0])] = tk
        for b in writes:
            b.w = tk
            b.r = {}
        self.n_inst += 1
        return inst

    def finish(self, out_bufs):
        for b in out_bufs:
            if b.w is not None:
                self.sp.wait(b.w)


def emit_consts(k):
    c = {}
    c["ones_bf"] = k.sb("ones_bf", [128, 128], BF16)
    k.op(k.pool, lambda e: e.memset(c["ones_bf"][:], 1.0), writes=[c["ones_bf"]])
    c["eps"] = k.sb("eps", [128, 1], F32)
    k.op(k.pool, lambda e: e.memset(c["eps"][:], EPS), writes=[c["eps"]])
    c["one"] = k.sb("one", [128, 1], F32)
    k.op(k.pool, lambda e: e.memset(c["one"][:], 1.0), writes=[c["one"]])
    return c


class Rot:
    def __init__(self, bufs):
        self.bufs = bufs
        self.i = 0

    def next(self):
        b = self.bufs[self.i % len(self.bufs)]
        self.i += 1
        return b


def emit_rstd(k, c, srcs, inv_n, ps_ss, sq_rot, tmp_rot, width=TT, ln_bias=None, out=None):
    n = len(srcs)
    for i, (ap, b) in enumerate(srcs):
        sq = sq_rot.next()
        k.op(k.act, lambda e, ap=ap, sq=sq: e.activation(out=sq[:, 0:width], in_=ap, func=AF.Square),
             reads=[b], writes=[sq])
        k.op(k.pe, lambda e, sq=sq, i=i: e.matmul(ps_ss[:, 0:width], lhsT=c["ones_bf"][:], rhs=sq[:, 0:width],
                                                  start=(i == 0), stop=(i == n - 1)),
             reads=[sq, c["ones_bf"]], writes=[ps_ss])
    lnv = tmp_rot.next()
    k.op(k.act, lambda e: e.activation(out=lnv[:, 0:width], in_=ps_ss[:, 0:width], func=AF.Ln, bias=c["eps"][:], scale=inv_n),
         reads=[ps_ss, c["eps"]], writes=[lnv])
    rstd = out if out is not None else tmp_rot.next()
    if ln_bias is None:
        k.op(k.act, lambda e: e.activation(out=rstd[:, 0:width], in_=lnv[:, 0:width], func=AF.Exp, scale=-0.5),
             reads=[lnv], writes=[rstd])
    else:
        k.op(k.act, lambda e: e.activation(out=rstd[:, 0:width], in_=lnv[:, 0:width], func=AF.Exp, scale=-0.5, bias=ln_bias[:]),
             reads=[lnv, ln_bias], writes=[rstd])
    return rstd


def emit_token_phase(k, c, io, cfg):
    tail, nxt, kv = cfg["tail"], cfg["next"], cfg.get("kv", False)
    sp, pool, pe, act, dve = k.sp, k.pool, k.pe, k.act, k.dve

    hT = [[k.sb("hT%d_%d" % (cc, tt), [128, TT], F32) for tt in range(NTT)] for cc in range(8)]
    hin = io["hT_in"]
    for cc in range(8):
        for tt in range(NTT):
            k.dma(sp, hT[cc][tt][:], hin[cc * 128:(cc + 1) * 128, tt * TT:(tt + 1) * TT], reads=[hin], writes=[hT[cc][tt]])

    gains = k.sb("gains", [128, io["gains"].t.shape[1]], F32)
    k.dma(sp, gains[:], io["gains"][:, :], reads=[io["gains"]], writes=[gains])

    hn = [k.sb("hn%d" % tt, [128, 8, TT], BF16) for tt in range(NTT)]
    psb = [k.ps("psb%d" % i, [128, TT], F32) for i in range(8)]
    sq_rot = Rot([k.sb("sq%d" % i, [128, TT], BF16) for i in range(3)])
    tmp_rot = Rot([k.sb("tmpf%d" % i, [128, TT], F32) for i in range(3)])
    wsq = [k.sb("wsq%d" % i, [128, 8, 1024], BF16) for i in range(2)]
    wsq_rot = Rot(wsq)

    def load_w(dst, src, ncol, col0=0, kc=8):
        v = src.t.rearrange("(c p) n -> p c n", p=128)
        k.dma(pool, dst[:, 0:kc, 0:ncol], v[:, 0:kc, col0:col0 + ncol], reads=[src], writes=[dst])

    def norm_to_hn(gcol):
        for tt in range(NTT):
            rstd = emit_rstd(k, c, [(hT[cc][tt][:], hT[cc][tt]) for cc in range(8)], 1.0 / D, psb[7], sq_rot, tmp_rot)
            for cc in range(8):
                k.op(dve, lambda e, cc=cc, tt=tt, rstd=rstd: e.scalar_tensor_tensor(
                    out=hn[tt][:, cc, :], in0=hT[cc][tt][:], scalar=gains[:, gcol + cc:gcol + cc + 1], in1=rstd[:],
                    op0=ALU.mult, op1=ALU.mult), reads=[hT[cc][tt], gains, rstd], writes=[hn[tt]])

    def proj_add(w, rhs_of, nk, pbanks):
        for tt in range(NTT):
            for n in range(8):
                ps = pbanks.next()
                for kk in range(nk):
                    ap, b = rhs_of(tt, kk)
                    k.op(pe, lambda e, ps=ps, kk=kk, n=n, ap=ap: e.matmul(ps[:], lhsT=w[:, kk, n * 128:(n + 1) * 128], rhs=ap,
                                                                         start=(kk == 0), stop=(kk == nk - 1)),
                         reads=[w, b], writes=[ps])
                k.op(dve, lambda e, ps=ps, n=n, tt=tt: e.tensor_tensor(out=hT[n][tt][:], in0=ps[:], in1=hT[n][tt][:], op=ALU.add),
                     reads=[ps, hT[n][tt]], writes=[hT[n][tt]])

    GC = cfg["gcols"]
    if tail:
        big = [k.sb("big%d" % i, [128, 2 * NT], BF16) for i in range(2)]
        ov = io["oT"].t.rearrange("(c p) n -> p c n", p=128)
        w_o = wsq_rot.next()
        load_w(w_o, io["w_o"], 1024)

        def o_rhs(tt, kk):
            b = big[tt % 2]
            if kk == 0:
                k.dma(sp, b[:, :].rearrange("p (c t) -> p c t", c=8), ov[:, :, tt * TT:(tt + 1) * TT], reads=[io["oT"]], writes=[b])
            return b[:, kk * TT:(kk + 1) * TT], b
        if 'oproj' not in cfg.get('skip', ()):
            proj_add(w_o, o_rhs, 8, Rot(psb[0:4]))

        norm_to_hn(GC["ln_ffn"])
        PC = 2
        npieces = NJ // PC
        wg = [k.sb("wg%d" % i, [128, 8, PC * 128], BF16) for i in range(2)]
        wu = [k.sb("wu%d" % i, [128, 8, PC * 128], BF16) for i in range(2)]
        wo = [k.sb("wo%d" % i, [128, PC, 1024], BF16) for i in range(2)]
        class _HV:
            def __init__(self, b, jj):
                self.b, self.jj = b, jj

            def __getitem__(self, idx):
                p, fs = idx
                return self.b[p, self.jj * NT + fs.start:self.jj * NT + fs.stop]
        hidb = big
        hid = [[_HV(big[i], jj) for jj in range(PC)] for i in range(2)]
        sg_rot = Rot([k.sb("sg%d" % i, [128, TT], F32) for i in range(2)])
        w_in, w_out = io["ffn_w_in"], io["ffn_w_out"]
        wout_v = w_out.t.rearrange("(c p) n -> p c n", p=128)

        def load_piece(P):
            s = P % 2
            load_w(wg[s], w_in, PC * 128, col0=P * PC * 128)
            load_w(wu[s], w_in, PC * 128, col0=FH + P * PC * 128)
            k.dma(pool, wo[s][:], wout_v[:, P * PC:(P + 1) * PC, :], reads=[w_out], writes=[wo[s]])

        if 'ffn' in cfg.get('skip', ()):
            npieces = 0
        else:
            load_piece(0)
        gbanks = Rot(psb[0:2]); ubanks = Rot(psb[2:4]); ybanks = Rot(psb[4:7])
        for P in range(npieces):
            s = P % 2
            if P + 1 < npieces:
                load_piece(P + 1)
            for jj in range(PC):
                for tt in range(NTT):
                    pg = gbanks.next(); pu = ubanks.next()
                    for kk in range(8):
                        k.op(pe, lambda e, pg=pg, kk=kk, jj=jj, tt=tt: e.matmul(pg[:], lhsT=wg[s][:, kk, jj * 128:(jj + 1) * 128], rhs=hn[tt][:, kk, :],
                                                                                start=(kk == 0), stop=(kk == 7)), reads=[wg[s], hn[tt]], writes=[pg])
                    for kk in range(8):
                        k.op(pe, lambda e, pu=pu, kk=kk, jj=jj, tt=tt: e.matmul(pu[:], lhsT=wu[s][:, kk, jj * 128:(jj + 1) * 128], rhs=hn[tt][:, kk, :],
                                                                                start=(kk == 0), stop=(kk == 7)), reads=[wu[s], hn[tt]], writes=[pu])
                    sg = sg_rot.next()
                    k.op(act, lambda e, sg=sg, pg=pg: e.activation(out=sg[:], in_=pg[:], func=AF.Silu), reads=[pg], writes=[sg])
                    k.op(dve, lambda e, sg=sg, pu=pu, jj=jj, tt=tt: e.tensor_tensor(out=hid[s][jj][:, tt * TT:(tt + 1) * TT], in0=pu[:], in1=sg[:], op=ALU.mult),
                         reads=[pu, sg], writes=[hidb[s]])
            for tt in range(NTT):
                for n in range(8):
                    py = ybanks.next()
                    for jj in range(PC):
                        k.op(pe, lambda e, py=py, jj=jj, n=n, tt=tt: e.matmul(py[:], lhsT=wo[s][:, jj, n * 128:(n + 1) * 128], rhs=hid[s][jj][:, tt * TT:(tt + 1) * TT],
                                                                              start=(jj == 0), stop=(jj == PC - 1)), reads=[wo[s], hidb[s]], writes=[py])
                    k.op(dve, lambda e, py=py, n=n, tt=tt: e.tensor_tensor(out=hT[n][tt][:], in0=py[:], in1=hT[n][tt][:], op=ALU.add),
                         reads=[py, hT[n][tt]], writes=[hT[n][tt]])

        norm_to_hn(GC["ln_ple"])
        w_pg = wsq_rot.next()
        load_w(w_pg, io["ple_w_gate"], 1024)
        w_pp = k.sb("w_pp", [128, 2, 1024], BF16)
        load_w(w_pp, io["ple_w_proj"], 1024, kc=2)
        pTb = [k.sb("pT%d" % i, [128, 2, TT], BF16) for i in range(2)]
        pT = [pTb[tt % 2] for tt in range(NTT)]
        pv = io["pT"].t.rearrange("(c p) n -> p c n", p=128)
        gb = Rot(psb[0:2]); pb = Rot(psb[2:4])
        for tt in range(NTT if 'ple' not in cfg.get('skip', ()) else 0):
            k.dma(pool, pT[tt][:], pv[:, :, tt * TT:(tt + 1) * TT], reads=[io["pT"]], writes=[pT[tt]])
            for n in range(8):
                pg = gb.next(); pp = pb.next()
                for kk in range(8):
                    k.op(pe, lambda e, pg=pg, kk=kk, n=n, tt=tt: e.matmul(pg[:], lhsT=w_pg[:, kk, n * 128:(n + 1) * 128], rhs=hn[tt][:, kk, :],
                                                                          start=(kk == 0), stop=(kk == 7)), reads=[w_pg, hn[tt]], writes=[pg])
                for kk in range(2):
                    k.op(pe, lambda e, pp=pp, kk=kk, n=n, tt=tt: e.matmul(pp[:], lhsT=w_pp[:, kk, n * 128:(n + 1) * 128], rhs=pT[tt][:, kk, :],
                                                                          start=(kk == 0), stop=(kk == 1)), reads=[w_pp, pT[tt]], writes=[pp])
                sg = sg_rot.next()
                k.op(act, lambda e, sg=sg, pg=pg: e.activation(out=sg[:], in_=pg[:], func=AF.Sigmoid), reads=[pg], writes=[sg])
                k.op(dve, lambda e, sg=sg, pp=pp: e.tensor_tensor(out=sg[:], in0=pp[:], in1=sg[:], op=ALU.mult), reads=[pp, sg], writes=[sg])
                k.op(dve, lambda e, sg=sg, n=n, tt=tt: e.tensor_tensor(out=hT[n][tt][:], in0=sg[:], in1=hT[n][tt][:], op=ALU.add),
                     reads=[sg, hT[n][tt]], writes=[hT[n][tt]])

    if "hT_out" in io:
        hout = io["hT_out"]
        for cc in range(8):
            for tt in range(NTT):
                k.dma(sp, hout[cc * 128:(cc + 1) * 128, tt * TT:(tt + 1) * TT], hT[cc][tt][:], reads=[hT[cc][tt]], untracked_out=hout)

    obuf_rot = Rot([k.sb("obuf%d" % i, [128, TT], BF16) for i in range(3)])

    def headnorm_proj(w, gcol, out_dram, extra_scale):
        lnb = None
        if extra_scale != 1.0:
            lnb = k.sb("lnb", [128, 1], F32)
            k.op(pool, lambda e: e.memset(lnb[:], float(np.log(extra_scale))), writes=[lnb])
        qb = Rot(psb[0:3])
        for tt in range(NTT):
            for n in range(8):
                pq = qb.next()
                for kk in range(8):
                    k.op(pe, lambda e, pq=pq, kk=kk, n=n, tt=tt: e.matmul(pq[:], lhsT=w[:, kk, n * 128:(n + 1) * 128], rhs=hn[tt][:, kk, :],
                                                                          start=(kk == 0), stop=(kk == 7)), reads=[w, hn[tt]], writes=[pq])
                rstd = emit_rstd(k, c, [(pq[:], pq)], 1.0 / 128, psb[7], sq_rot, tmp_rot, ln_bias=lnb)
                ob = obuf_rot.next()
                k.op(dve, lambda e, ob=ob, pq=pq, n=n, rstd=rstd: e.scalar_tensor_tensor(out=ob[:], in0=pq[:], scalar=gains[:, gcol:gcol + 1], in1=rstd[:],
                                                                                       op0=ALU.mult, op1=ALU.mult), reads=[pq, gains, rstd], writes=[ob])
                k.dma(sp, out_dram[n * 128:(n + 1) * 128, tt * TT:(tt + 1) * TT], ob[:], reads=[ob], untracked_out=out_dram)

    if kv:
        norm_to_hn(GC["kv_norm"])
        w_k = wsq_rot.next()
        load_w(w_k, io["w_kv"], 1024, col0=0)
        headnorm_proj(w_k, GC["k_norm"], io["kT_out"], 1.0)
        w_v = wsq_rot.next()
        load_w(w_v, io["w_kv"], 1024, col0=1024)
        vb = Rot(psb[3:6])
        vo_rot = Rot([k.sb("vo%d" % i, [128, TT], BF16) for i in range(2)])
        vout = io["v_out"]
        for tt in range(NTT):
            for t4 in range(4):
                for nh in range(2):
                    pv_ = vb.next()
                    for kk in range(8):
                        k.op(pe, lambda e, pv_=pv_, kk=kk, nh=nh, tt=tt, t4=t4: e.matmul(pv_[:], lhsT=hn[tt][:, kk, t4 * 128:(t4 + 1) * 128], rhs=w_v[:, kk, nh * 512:(nh + 1) * 512],
                                                                                      start=(kk == 0), stop=(kk == 7)), reads=[w_v, hn[tt]], writes=[pv_])
                    vo = vo_rot.next()
                    k.op(act, lambda e, vo=vo, pv_=pv_: e.activation(out=vo[:], in_=pv_[:], func=AF.Copy), reads=[pv_], writes=[vo])
                    r0 = tt * TT + t4 * 128
                    k.dma(sp, vout[r0:r0 + 128, nh * 512:(nh + 1) * 512], vo[:], reads=[vo], untracked_out=vout)

    if nxt is not None:
        norm_to_hn(GC["ln_mix_next"])
        if nxt == "gdn":
            hnout = io["hnT_out"].t.rearrange("(c p) n -> p c n", p=128)
            for tt in range(NTT):
                k.dma(sp, hnout[:, :, tt * TT:(tt + 1) * TT], hn[tt][:], reads=[hn[tt]], untracked_out=io["hnT_out"])
        else:
            w_q = wsq_rot.next()
            load_w(w_q, io["w_q"], 1024)
            headnorm_proj(w_q, GC["q_norm"], io["qT_out"], 128 ** -0.5)

    outs = [io[n] for n in io if n.endswith("_out")]
    k.finish(outs)


GCOLS = {"ln_ffn": 0, "ln_ple": 8, "kv_norm": 16, "k_norm": 24, "ln_mix_next": 25, "q_norm": 33}
NGCOL = 34


def col8(v):
    return np.ascontiguousarray(np.asarray(v, np.float32).reshape(8, 128).T)


def build_token(cfg):
    nc = bass.Bass("TRN2", target_bir_lowering=False)
    k = KB(nc)
    io = {}
    io["hT_in"] = k.dram("hT_in", [D, NT], F32, "ExternalInput")
    io["gains"] = k.dram("gains", [128, NGCOL], F32, "ExternalInput")
    if cfg["tail"]:
        io["oT"] = k.dram("oT", [D, NT], BF16, "ExternalInput")
        io["w_o"] = k.dram("w_o", [D, D], F32, "ExternalInput")
        io["ffn_w_in"] = k.dram("ffn_w_in", [D, 2 * FH], F32, "ExternalInput")
        io["ffn_w_out"] = k.dram("ffn_w_out", [FH, D], F32, "ExternalInput")
        io["ple_w_gate"] = k.dram("ple_w_gate", [D, D], F32, "ExternalInput")
        io["ple_w_proj"] = k.dram("ple_w_proj", [256, D], F32, "ExternalInput")
        io["pT"] = k.dram("pT", [256, NT], F32, "ExternalInput")
    if cfg.get("kv"):
        io["w_kv"] = k.dram("w_kv", [D, 2 * D], F32, "ExternalInput")
        io["kT_out"] = k.dram("kT_out", [D, NT], BF16, "ExternalOutput")
        io["v_out"] = k.dram("v_out", [NT, D], BF16, "ExternalOutput")
    if cfg["next"] == "gdn":
        io["hnT_out"] = k.dram("hnT_out", [D, NT], BF16, "ExternalOutput")
    elif cfg["next"] == "sb":
        io["w_q"] = k.dram("w_q", [D, D], F32, "ExternalInput")
        io["qT_out"] = k.dram("qT_out", [D, NT], BF16, "ExternalOutput")
    io["hT_out"] = k.dram("hT_out", [D, NT], F32, "ExternalOutput")
    c = emit_consts(k)
    cfg = dict(cfg)
    cfg["gcols"] = GCOLS
    emit_token_phase(k, c, io, cfg)
    return nc, k


def emit_masks(k, c):
    nc = k.nc
    m = k.sb("mstrict", [128, 128], F32)
    k.op(k.pool, lambda e: e.memset(m[:], 1.0), writes=[m])
    k.op(k.pool, lambda e: e.affine_select(out=m[:], in_=m[:], pattern=[[1, 128]], compare_op=ALU.is_gt, fill=0.0,
                                           base=0, channel_multiplier=-1), reads=[m], writes=[m])
    c["mstrict"] = m
    t = k.sb("triinc", [128, 128], BF16)
    k.op(k.pool, lambda e: e.memset(t[:], 1.0), writes=[t])
    k.op(k.pool, lambda e: e.affine_select(out=t[:], in_=t[:], pattern=[[-1, 128]], compare_op=ALU.is_ge, fill=0.0,
                                           base=0, channel_multiplier=1), reads=[t], writes=[t])
    c["triinc"] = t


def emit_sb_phase(k, c, io, nheads=4, seq=S):
    sp, pool, pe, act, dve = k.sp, k.pool, k.pe, k.act, k.dve
    nblk = seq // 128
    nsb = seq // 512
    qT = k.sb("qT", [128, nheads, seq], BF16)
    kT = k.sb("kT", [128, nheads, seq], BF16)
    V = k.sb("V", [128, nblk, nheads * 128], BF16)
    k.dma(sp, kT[:], io["kT"].t.rearrange("(h p) t -> p h t", p=128), reads=[io["kT"]], writes=[kT])
    k.dma(sp, qT[:], io["qT"].t.rearrange("(h p) t -> p h t", p=128), reads=[io["qT"]], writes=[qT])
    k.dma(sp, V[:], io["v"].t.rearrange("(b p) n -> p b n", p=128), reads=[io["v"]], writes=[V])

    E = [k.sb("E%d" % i, [128, 512], F32) for i in range(3)]
    SP = [k.sb("SP%d" % i, [128, 512], BF16) for i in range(3)]
    W1 = [k.sb("W1%d" % i, [128, 512], F32) for i in range(2)]
    WG = [k.sb("WG%d" % i, [128, 512], BF16) for i in range(3)]
    LS = [k.sb("LS%d" % i, [128, 512], BF16) for i in range(2)]
    OB = [k.sb("OB%d" % i, [128, 512], BF16) for i in range(2)]
    pz = [k.ps("pz%d" % i, [128, 512], F32) for i in range(2)]
    pi = [k.ps("pi%d" % i, [128, 512], F32) for i in range(2)]
    po = [k.ps("po%d" % i, [128, 512], F32) for i in range(2)]

    steps = []
    sbi = 0
    for h in range(nheads):
        for sb in range(nsb):
            n_s = 4 * sb + 4
            for idx, i in enumerate(range(n_s - 1, -1, -1)):
                m = i - 4 * sb
                c0 = max(0, 128 * m)
                steps.append(dict(h=h, sb=sb, i=i, c0=c0, diag=(m >= 0), first=(idx == 0), last=(i == 0), sbi=sbi))
            sbi += 1
    ns = len(steps)
    oT = io["oT_out"]

    def stageA(n):
        st = steps[n]
        h, i, c0, t0 = st["h"], st["i"], st["c0"], st["sb"] * 512
        z = pz[n % 2]; e_ = E[n % 3]; s_ = SP[n % 3]
        k.op(pe, lambda e: e.matmul(z[:, c0:512], lhsT=kT[:, h, i * 128:(i + 1) * 128], rhs=qT[:, h, t0 + c0:t0 + 512], start=True, stop=True),
             reads=[kT, qT], writes=[z])
        k.op(act, lambda e: e.activation(out=e_[:, c0:512], in_=z[:, c0:512], func=AF.Exp), reads=[z], writes=[e_])
        if st["diag"]:
            k.op(pool, lambda e: e.tensor_tensor(out=e_[:, c0:c0 + 128], in0=e_[:, c0:c0 + 128], in1=c["mstrict"][:], op=ALU.mult),
                 reads=[e_, c["mstrict"]], writes=[e_])
        k.op(act, lambda e: e.activation(out=s_[:, c0:512], in_=e_[:, c0:512], func=AF.Ln, bias=c["one"][:], scale=1.0), reads=[e_, c["one"]], writes=[s_])

    def stageB(n):
        st = steps[n]
        c0 = st["c0"]
        e_ = E[n % 3]; s_ = SP[n % 3]; inc = pi[n % 2]; w1 = W1[n % 2]; wg = WG[n % 3]; ls = LS[st["sbi"] % 2]
        if st["first"]:
            k.op(pool, lambda e: e.memset(ls[:], 0.0), writes=[ls])
        k.op(pe, lambda e: e.matmul(inc[:, c0:512], lhsT=c["triinc"][:], rhs=s_[:, c0:512], start=True, stop=st["first"]),
             reads=[c["triinc"], s_], writes=[inc])
        if not st["first"]:
            k.op(pe, lambda e: e.matmul(inc[:, c0:512], lhsT=c["ones_bf"][:], rhs=ls[:, c0:512], start=False, stop=True),
                 reads=[c["ones_bf"], ls], writes=[inc])
        if not st["last"]:
            k.op(pool, lambda e: e.tensor_tensor(out=ls[:, c0:512], in0=ls[:, c0:512], in1=s_[:, c0:512], op=ALU.add),
                 reads=[ls, s_], writes=[ls])
        k.op(act, lambda e: e.activation(out=w1[:, c0:512], in_=inc[:, c0:512], func=AF.Exp, scale=-1.0), reads=[inc], writes=[w1])
        if c0 > 0:
            k.op(pool, lambda e: e.memset(wg[:, 0:c0], 0.0), writes=[wg])
        k.op(dve, lambda e: e.tensor_tensor(out=wg[:, c0:512], in0=e_[:, c0:512], in1=w1[:, c0:512], op=ALU.mult),
             reads=[e_, w1], writes=[wg])

    def stageC(n):
        st = steps[n]
        h, i = st["h"], st["i"]
        wg = WG[n % 3]; o = po[st["sbi"] % 2]
        k.op(pe, lambda e: e.matmul(o[:], lhsT=V[:, i, h * 128:(h + 1) * 128], rhs=wg[:], start=st["first"], stop=st["last"]),
             reads=[V, wg], writes=[o])
        if st["last"]:
            ob = OB[st["sbi"] % 2]
            k.op(act, lambda e: e.activation(out=ob[:], in_=o[:], func=AF.Copy), reads=[o], writes=[ob])
            t0 = st["sb"] * 512
            k.dma(sp, oT[h * 128:(h + 1) * 128, t0:t0 + 512], ob[:], reads=[ob], untracked_out=oT)

    for it in range(ns + 2):
        if it < ns:
            stageA(it)
        if 0 <= it - 1 < ns:
            stageB(it - 1)
        if 0 <= it - 2 < ns:
            stageC(it - 2)
    k.finish([oT])


def build_sb(nheads=4, seq=S):
    nc = bass.Bass("TRN2", target_bir_lowering=False)
    k = KB(nc)
    io = {}
    io["qT"] = k.dram("qT", [nheads * 128, seq], BF16, "ExternalInput")
    io["kT"] = k.dram("kT", [nheads * 128, seq], BF16, "ExternalInput")
    io["v"] = k.dram("v", [seq, nheads * 128], BF16, "ExternalInput")
    io["oT_out"] = k.dram("oT_out", [nheads * 128, seq], BF16, "ExternalOutput")
    c = emit_consts(k)
    emit_masks(k, c)
    emit_sb_phase(k, c, io, nheads, seq)
    return nc, k


NEG = -30000.0
AX = mybir.AxisListType


def emit_gdn_consts(k, c):
    def mk(name, dt, init, pattern, cm, cmp, fill):
        t = k.sb(name, [128, 128], dt)
        k.op(k.pool, lambda e: e.memset(t[:], init), writes=[t])
        if pattern is not None:
            k.op(k.pool, lambda e: e.affine_select(out=t[:], in_=t[:], pattern=pattern, compare_op=cmp, fill=fill,
                                                   base=0, channel_multiplier=cm), reads=[t], writes=[t])
        c[name] = t
    mk("triincl", F32, 1.0, [[1, 128]], -1, ALU.is_ge, 0.0)
    mk("strictgt", F32, 1.0, [[-1, 128]], 1, ALU.is_gt, 0.0)
    mk("mneg_strict", F32, 0.0, [[-1, 128]], 1, ALU.is_gt, NEG)
    mk("mneg_inclT", F32, 0.0, [[1, 128]], -1, ALU.is_ge, NEG)
    mk("ident_f", F32, 1.0, [[-1, 128]], 1, ALU.is_equal, 0.0)
    mk("ident_bf", BF16, 1.0, [[-1, 128]], 1, ALU.is_equal, 0.0)
    mk("ones_f", F32, 1.0, None, 0, None, 0.0)


def emit_gdn_phase(k, c, io, seq=S):
    sp, pool, pe, act, dve = k.sp, k.pool, k.pe, k.act, k.dve
    NH = 4
    nst = seq // 512

    W = k.sb("gW", [128, 8, NH * 4 * 128], BF16)
    k.dma(pool, W[:], io["w_qkvg"].t.rearrange("(c p) n -> p c n", p=128), reads=[io["w_qkvg"]], writes=[W])
    Wab = k.sb("gWab", [128, 8, 8], BF16)
    k.dma(pool, Wab[:], io["w_ab"].t.rearrange("(c p) n -> p c n", p=128), reads=[io["w_ab"]], writes=[Wab])
    convw = k.sb("convw", [128, 48], F32)
    k.dma(sp, convw[:], io["convw"][:, :], reads=[io["convw"]], writes=[convw])
    dtb = k.sb("dtb", [128, 4], F32)
    k.dma(sp, dtb[:], io["dtb"][:, :], reads=[io["dtb"]], writes=[dtb])
    nega = k.sb("nega", [128, 4], F32)
    k.dma(sp, nega[:], io["alog"][:, :], reads=[io["alog"]], writes=[nega])
    k.op(act, lambda e: e.activation(out=nega[:], in_=nega[:], func=AF.Exp), reads=[nega], writes=[nega])
    k.op(dve, lambda e: e.tensor_scalar(out=nega[:], in0=nega[:], scalar1=-1.0, scalar2=None, op0=ALU.mult), reads=[nega], writes=[nega])
    normg = k.sb("normg", [128, 128], F32)
    k.dma(sp, normg[:], io["normg"][:, :], reads=[io["normg"]], writes=[normg])
    lnb_q = k.sb("lnb_q", [128, 1], F32)
    k.op(pool, lambda e: e.memset(lnb_q[:], float(np.log(128 ** -0.5))), writes=[lnb_q])

    projb = Rot([k.ps("projb%d" % i, [128, 512], F32) for i in range(2)])
    ssb = k.ps("ssb", [128, 512], F32)
    fb = Rot([k.ps("fb%d" % i, [128, 512], F32) for i in range(4)])
    bfb = k.ps("bfb", [128, 1024], BF16)

    def qv(bank, i):
        return bank[:, i * 128:(i + 1) * 128]

    S32 = [k.sb("S32_%d" % h, [128, 128], F32) for h in range(NH)]
    Sbf = [k.sb("Sbf_%d" % h, [128, 128], BF16) for h in range(NH)]
    for h in range(NH):
        k.op(pool, lambda e, h=h: e.memset(S32[h][:], 0.0), writes=[S32[h]])
        k.op(pool, lambda e, h=h: e.memset(Sbf[h][:], 0.0), writes=[Sbf[h]])
    pre = [k.sb("pre%d" % ch, [128, 515], F32) for ch in range(12)]
    for ch in range(12):
        k.op(pool, lambda e, ch=ch: e.memset(pre[ch][:, 0:3], 0.0), writes=[pre[ch]])

    hnb = [k.sb("ghn%d" % i, [128, 8, 512], BF16) for i in range(2)]
    hnv = io["hnT"].t.rearrange("(c p) t -> p c t", p=128)
    acc_rot = Rot([k.sb("cacc%d" % i, [128, 512], F32) for i in range(2)])
    cs_rot = Rot([k.sb("ccs%d" % i, [128, 512], F32) for i in range(2)])
    sq_rot = Rot([k.sb("gsq%d" % i, [128, 512], BF16) for i in range(2)])
    tmp_rot = Rot([k.sb("gtmp%d" % i, [128, 512], F32) for i in range(3)])
    qTs = [[k.sb("qTs%d_%d" % (p, h), [128, 512], BF16) for h in range(NH)] for p in range(2)]
    kTs = [[k.sb("kTs%d_%d" % (p, h), [128, 512], BF16) for h in range(NH)] for p in range(2)]
    vTs = [[k.sb("vTs%d_%d" % (p, h), [128, 512], BF16) for h in range(NH)] for p in range(2)]
    sgT = [[k.sb("sgT%d_%d" % (p, h), [128, 512], BF16) for h in range(NH)] for p in range(2)]
    ogT = [[k.sb("ogT%d_%d" % (p, h), [128, 512], BF16) for h in range(NH)] for p in range(2)]

    def small(name, w=4):
        return [[k.sb("%s%d_%d" % (name, p, j), [128, w], F32) for j in range(4)] for p in range(2)]
    gcol, beta, nbeta, eG, nbeG, eGL, eGLmG = (small(n) for n in ("gcol", "beta", "nbeta", "eG", "nbeG", "eGL", "eGLmG"))
    st1, st2 = small("st1"), small("st2")

    def tile_tmp(name, dt, single=False):
        l = [[k.sb("%s%d_%d" % (name, p, h), [128, 128], dt) for h in range(NH)] for p in range(1 if single else 2)]
        return l if not single else [l[0], l[0]]
    GM1, GM2, Dm, DTm, bV = (tile_tmp(n, F32) for n in ("GM1", "GM2", "Dm", "DTm", "bV"))
    sqo, tmpo, otok = (tile_tmp(n, F32, True) for n in ("sqo", "tmpo", "otok"))
    CH = F32
    QKDT, Kd = (tile_tmp(n, BF16) for n in ("QKDT", "Kd"))
    vnb, onb = (tile_tmp(n, BF16, True) for n in ("vnb", "onb"))
    Bm, BTm = (tile_tmp(n, CH) for n in ("Bm", "BTm"))
    Xb = tile_tmp("Xb", CH, True)
    Pm = [tile_tmp("Pm%d" % i, CH) for i in range(2)]
    PTm = [tile_tmp("PTm%d" % i, CH) for i in range(2)]
    TTm = [tile_tmp("TTm%d" % i, CH) for i in range(2)]
    ssv = [[k.sb("ssv%d_%d" % (p, h), [128, 1], F32) for h in range(NH)] for p in range(2)]
    lnv1 = [[k.sb("lnv1%d_%d" % (p, h), [128, 1], F32) for h in range(NH)] for p in range(2)]
    rsv = [[k.sb("rsv%d_%d" % (p, h), [128, 1], F32) for h in range(NH)] for p in range(2)]

    def mm(out_b, out_ap, l_b, l_ap, r_b, r_ap, start=True, stop=True):
        k.op(pe, lambda e: e.matmul(out_ap, lhsT=l_ap, rhs=r_ap, start=start, stop=stop), reads=[l_b, r_b], writes=[out_b])

    def tr(out_ap, in_b, in_ap):
        k.op(pe, lambda e: e.transpose(out=out_ap, in_=in_ap, identity=c["ident_bf"][:]), reads=[in_b, c["ident_bf"]], writes=[bfb])

    def ev(i):
        return bfb[:, i * 128:(i + 1) * 128]

    HL = range(NH)
    tile_no = 0
    for st in range(nst):
        sp_ = st % 2
        hb = hnb[st % 2]
        k.dma(sp, hb[:], hnv[:, :, st * 512:(st + 1) * 512], reads=[io["hnT"]], writes=[hb])
        for hl in HL:
            for typ in range(4):
                ps = projb.next()
                col = (hl * 4 + typ) * 128
                for kc in range(8):
                    mm(ps, ps[:], W, W[:, kc, col:col + 128], hb, hb[:, kc, :], start=(kc == 0), stop=(kc == 7))
                if typ == 3:
                    k.op(act, lambda e, ps=ps: e.activation(out=sgT[sp_][hl][:], in_=ps[:], func=AF.Silu), reads=[ps], writes=[sgT[sp_][hl]])
                    continue
                ch = hl * 3 + typ
                pr = pre[ch]
                k.op(act, lambda e, ps=ps, pr=pr: e.activation(out=pr[:, 3:515], in_=ps[:], func=AF.Copy), reads=[ps], writes=[pr])
                acc = acc_rot.next()
                k.op(dve, lambda e, pr=pr, acc=acc, ch=ch: e.tensor_scalar(out=acc[:], in0=pr[:, 0:512], scalar1=convw[:, ch * 4:ch * 4 + 1], scalar2=None, op0=ALU.mult),
                     reads=[pr, convw], writes=[acc])
                for tap in range(1, 4):
                    k.op(dve, lambda e, pr=pr, acc=acc, ch=ch, tap=tap: e.scalar_tensor_tensor(out=acc[:], in0=pr[:, tap:tap + 512], scalar=convw[:, ch * 4 + tap:ch * 4 + tap + 1],
                                                                                           in1=acc[:], op0=ALU.mult, op1=ALU.add), reads=[pr, convw, acc], writes=[acc])
                k.op(pool, lambda e, pr=pr: e.tensor_copy(out=pr[:, 0:3], in_=pr[:, 512:515]), reads=[pr], writes=[pr])
                if typ == 2:
                    k.op(act, lambda e, acc=acc: e.activation(out=vTs[sp_][hl][:], in_=acc[:], func=AF.Silu), reads=[acc], writes=[vTs[sp_][hl]])
                else:
                    cs = cs_rot.next()
                    k.op(act, lambda e, acc=acc, cs=cs: e.activation(out=cs[:], in_=acc[:], func=AF.Silu), reads=[acc], writes=[cs])
                    rstd = emit_rstd(k, c, [(cs[:], cs)], 1.0, ssb, sq_rot, tmp_rot, ln_bias=(lnb_q if typ == 0 else None))
                    dst = qTs[sp_][hl] if typ == 0 else kTs[sp_][hl]
                    k.op(dve, lambda e, cs=cs, rstd=rstd, dst=dst: e.tensor_tensor(out=dst[:], in0=cs[:], in1=rstd[:], op=ALU.mult), reads=[cs, rstd], writes=[dst])
        pab = fb.next()
        for j in range(4):
            jc = slice(j * 128, (j + 1) * 128)
            for kc in range(8):
                mm(pab, pab[:, j * 8:j * 8 + 8], hb, hb[:, kc, jc], Wab, Wab[:, kc, :], start=(kc == 0), stop=(kc == 7))
        ab_sb = [st1[sp_][j] for j in range(4)]
        bb_sb = [st2[sp_][j] for j in range(4)]
        for j in range(4):
            k.op(dve, lambda e, j=j: e.tensor_tensor(out=ab_sb[j][:], in0=pab[:, j * 8:j * 8 + 4], in1=dtb[:], op=ALU.add), reads=[pab, dtb], writes=[ab_sb[j]])
            k.op(dve, lambda e, j=j: e.tensor_copy(out=bb_sb[j][:], in_=pab[:, j * 8 + 4:j * 8 + 8]), reads=[pab], writes=[bb_sb[j]])
        pG = fb.next()
        for j in range(4):
            a1, a2 = ab_sb[j], bb_sb[j]
            k.op(act, lambda e, a1=a1: e.activation(out=a1[:], in_=a1[:], func=AF.Exp), reads=[a1], writes=[a1])
            k.op(act, lambda e, a1=a1: e.activation(out=a1[:], in_=a1[:], func=AF.Ln, bias=c["one"][:], scale=1.0), reads=[a1, c["one"]], writes=[a1])
            g_ = gcol[sp_][j]
            k.op(dve, lambda e, a1=a1, g_=g_: e.tensor_tensor(out=g_[:], in0=a1[:], in1=nega[:], op=ALU.mult), reads=[a1, nega], writes=[g_])
            b_ = beta[sp_][j]
            k.op(act, lambda e, a2=a2: e.activation(out=a2[:], in_=a2[:], func=AF.Exp, scale=-1.0), reads=[a2], writes=[a2])
            k.op(dve, lambda e, a2=a2: e.tensor_scalar(out=a2[:], in0=a2[:], scalar1=1.0, scalar2=None, op0=ALU.add), reads=[a2], writes=[a2])
            k.op(dve, lambda e, a2=a2, b_=b_: e.reciprocal(out=b_[:], in_=a2[:]), reads=[a2], writes=[b_])
            k.op(dve, lambda e, j=j, b_=b_: e.tensor_scalar(out=nbeta[sp_][j][:], in0=b_[:], scalar1=-1.0, scalar2=None, op0=ALU.mult), reads=[b_], writes=[nbeta[sp_][j]])
            mm(pG, pG[:, j * 8:j * 8 + 4], c["triincl"], c["triincl"][:], g_, g_[:])
            mm(pG, pG[:, j * 8 + 4:j * 8 + 8], c["ones_f"], c["ones_f"][:], g_, g_[:])
        for j in range(4):
            a2 = bb_sb[j]
            k.op(act, lambda e, j=j: e.activation(out=eG[sp_][j][:], in_=pG[:, j * 8:j * 8 + 4], func=AF.Exp), reads=[pG], writes=[eG[sp_][j]])
            k.op(act, lambda e, j=j: e.activation(out=eGL[sp_][j][:], in_=pG[:, j * 8 + 4:j * 8 + 8], func=AF.Exp), reads=[pG], writes=[eGL[sp_][j]])
            k.op(act, lambda e, j=j, a2=a2: e.activation(out=a2[:], in_=pG[:, j * 8:j * 8 + 4], func=AF.Copy), reads=[pG], writes=[a2])
        for j in range(4):
            a2 = bb_sb[j]
            k.op(dve, lambda e, j=j, a2=a2: e.tensor_tensor(out=a2[:], in0=pG[:, j * 8 + 4:j * 8 + 8], in1=a2[:], op=ALU.subtract), reads=[pG, a2], writes=[a2])
            k.op(act, lambda e, j=j, a2=a2: e.activation(out=eGLmG[sp_][j][:], in_=a2[:], func=AF.Exp), reads=[a2], writes=[eGLmG[sp_][j]])
            k.op(dve, lambda e, j=j: e.scalar_tensor_tensor(out=nbeG[sp_][j][:], in0=eG[sp_][j][:], scalar=-1.0, in1=beta[sp_][j][:], op0=ALU.mult, op1=ALU.mult),
                 reads=[eG[sp_][j], beta[sp_][j]], writes=[nbeG[sp_][j]])

        def pre_tile(j):
            jc = slice(j * 128, (j + 1) * 128)
            tp = j % 2
            sc = lambda arr, hl: arr[sp_][j][:, hl:hl + 1]
            for hl in HL:
                g1, g2 = GM1[tp][hl], GM2[tp][hl]
                k.op(pool, lambda e, g1=g1, hl=hl: e.tensor_scalar(out=g1[:], in0=c["triincl"][:], scalar1=sc(gcol, hl), scalar2=None, op0=ALU.mult),
                     reads=[c["triincl"], gcol[sp_][j]], writes=[g1])
                k.op(pool, lambda e, g2=g2, hl=hl: e.tensor_scalar(out=g2[:], in0=c["strictgt"][:], scalar1=sc(gcol, hl), scalar2=None, op0=ALU.mult),
                     reads=[c["strictgt"], gcol[sp_][j]], writes=[g2])
            yield
            b_dd = fb.next()
            for hl in HL:
                mm(b_dd, qv(b_dd, hl), GM1[tp][hl], GM1[tp][hl][:], c["strictgt"], c["strictgt"][:], start=True, stop=False)
                mm(b_dd, qv(b_dd, hl), c["ident_f"], c["ident_f"][:], c["mneg_strict"], c["mneg_strict"][:], start=False, stop=True)
            yield
            b_ddT = fb.next()
            for hl in HL:
                mm(b_ddT, qv(b_ddT, hl), GM2[tp][hl], GM2[tp][hl][:], c["triincl"], c["triincl"][:], start=True, stop=False)
                mm(b_ddT, qv(b_ddT, hl), c["ident_f"], c["ident_f"][:], c["mneg_inclT"], c["mneg_inclT"][:], start=False, stop=True)
            for hl in HL:
                k.op(act, lambda e, hl=hl: e.activation(out=Dm[tp][hl][:], in_=qv(b_dd, hl), func=AF.Exp), reads=[b_dd], writes=[Dm[tp][hl]])
            for hl in HL:
                k.op(act, lambda e, hl=hl: e.activation(out=DTm[tp][hl][:], in_=qv(b_ddT, hl), func=AF.Exp), reads=[b_ddT], writes=[DTm[tp][hl]])
            yield
            b_kk = fb.next()
            for hl in HL:
                kT_ = kTs[sp_][hl]
                mm(b_kk, qv(b_kk, hl), kT_, kT_[:, jc], kT_, kT_[:, jc])
            yield
            b_kq = fb.next()
            for hl in HL:
                kT_, qT_ = kTs[sp_][hl], qTs[sp_][hl]
                mm(b_kq, qv(b_kq, hl), kT_, kT_[:, jc], qT_, qT_[:, jc])
            for hl in HL:
                k.op(dve, lambda e, hl=hl: e.scalar_tensor_tensor(out=Bm[tp][hl][:], in0=qv(b_kk, hl), scalar=sc(nbeta, hl), in1=Dm[tp][hl][:], op0=ALU.mult, op1=ALU.mult),
                     reads=[b_kk, nbeta[sp_][j], Dm[tp][hl]], writes=[Bm[tp][hl]])
            for hl in HL:
                k.op(dve, lambda e, hl=hl: e.tensor_tensor(out=QKDT[tp][hl][:], in0=qv(b_kq, hl), in1=DTm[tp][hl][:], op=ALU.mult),
                     reads=[b_kq, DTm[tp][hl]], writes=[QKDT[tp][hl]])
            yield
            yield
            b_bt = fb.next()
            for hl in HL:
                k.op(pe, lambda e, hl=hl: e.transpose(out=qv(b_bt, hl), in_=Bm[tp][hl][:], identity=c["ident_f"][:]),
                     reads=[Bm[tp][hl], c["ident_f"]], writes=[b_bt])
            for hl in HL:
                k.op(act, lambda e, hl=hl: e.activation(out=BTm[tp][hl][:], in_=qv(b_bt, hl), func=AF.Copy), reads=[b_bt], writes=[BTm[tp][hl]])
                k.op(pool, lambda e, hl=hl: e.tensor_tensor(out=TTm[0][tp][hl][:], in0=BTm[tp][hl][:], in1=c["ident_f"][:], op=ALU.add),
                     reads=[BTm[tp][hl], c["ident_f"]], writes=[TTm[0][tp][hl]])
            yield
            for hl in HL:
                tr(ev(hl), vTs[sp_][hl], vTs[sp_][hl][:, jc])
                tr(ev(4 + hl), kTs[sp_][hl], kTs[sp_][hl][:, jc])
            for hl in HL:
                k.op(act, lambda e, hl=hl: e.activation(out=bV[tp][hl][:], in_=ev(hl), func=AF.Copy, scale=sc(beta, hl)), reads=[bfb, beta[sp_][j]], writes=[bV[tp][hl]])
                k.op(act, lambda e, hl=hl: e.activation(out=Kd[tp][hl][:], in_=ev(4 + hl), func=AF.Copy, scale=sc(eGLmG, hl)), reads=[bfb, eGLmG[sp_][j]], writes=[Kd[tp][hl]])
            Pold = {hl: Bm[tp][hl] for hl in HL}
            PTold = {hl: BTm[tp][hl] for hl in HL}
            TTold = {hl: TTm[0][tp][hl] for hl in HL}
            for lvl in range(1, 7):
                yield
                b_p = fb.next()
                for hl in HL:
                    mm(b_p, qv(b_p, hl), PTold[hl], PTold[hl][:], Pold[hl], Pold[hl][:])
                b_pt = None
                if lvl < 6:
                    yield
                    b_pt = fb.next()
                    for hl in HL:
                        mm(b_pt, qv(b_pt, hl), Pold[hl], Pold[hl][:], PTold[hl], PTold[hl][:])
                Pn = {hl: Pm[lvl % 2][tp][hl] for hl in HL}
                for hl in HL:
                    k.op(act, lambda e, hl=hl: e.activation(out=Pn[hl][:], in_=qv(b_p, hl), func=AF.Copy), reads=[b_p], writes=[Pn[hl]])
                PTn = {hl: None for hl in HL}
                if lvl < 6:
                    PTn = {hl: PTm[lvl % 2][tp][hl] for hl in HL}
                    for hl in HL:
                        k.op(dve, lambda e, hl=hl: e.tensor_copy(out=PTn[hl][:], in_=qv(b_pt, hl)), reads=[b_pt], writes=[PTn[hl]])
                yield
                b_t = fb.next()
                for hl in HL:
                    mm(b_t, qv(b_t, hl), Pn[hl], Pn[hl][:], TTold[hl], TTold[hl][:])
                TTn = {hl: TTm[lvl % 2][tp][hl] for hl in HL}
                for hl in HL:
                    k.op(dve, lambda e, hl=hl, to=TTold[hl]: e.tensor_tensor(out=TTn[hl][:], in0=qv(b_t, hl), in1=to[:], op=ALU.add),
                         reads=[b_t, TTold[hl]], writes=[TTn[hl]])
                Pold, PTold, TTold = Pn, PTn, TTn
            fin[j] = TTold

        def rec_tile(j):
            jc = slice(j * 128, (j + 1) * 128)
            tp = j % 2
            sc = lambda arr, hl: arr[sp_][j][:, hl:hl + 1]
            TTold = fin[j]
            b_ks = fb.next()
            for hl in HL:
                kT_ = kTs[sp_][hl]
                mm(b_ks, qv(b_ks, hl), kT_, kT_[:, jc], Sbf[hl], Sbf[hl][:])
            b_qs = fb.next()
            for hl in HL:
                qT_ = qTs[sp_][hl]
                mm(b_qs, qv(b_qs, hl), qT_, qT_[:, jc], Sbf[hl], Sbf[hl][:])
            for hl in HL:
                k.op(dve, lambda e, hl=hl: e.scalar_tensor_tensor(out=Xb[tp][hl][:], in0=qv(b_ks, hl), scalar=sc(nbeG, hl), in1=bV[tp][hl][:], op0=ALU.mult, op1=ALU.add),
                     reads=[b_ks, nbeG[sp_][j], bV[tp][hl]], writes=[Xb[tp][hl]])
            for hl in HL:
                k.op(act, lambda e, hl=hl: e.activation(out=tmpo[tp][hl][:], in_=qv(b_qs, hl), func=AF.Copy, scale=sc(eG, hl)), reads=[b_qs, eG[sp_][j]], writes=[tmpo[tp][hl]])
            b_vn = fb.next()
            for hl in HL:
                mm(b_vn, qv(b_vn, hl), TTold[hl], TTold[hl][:], Xb[tp][hl], Xb[tp][hl][:])
            for hl in HL:
                k.op(act, lambda e, hl=hl: e.activation(out=vnb[tp][hl][:], in_=qv(b_vn, hl), func=AF.Copy), reads=[b_vn], writes=[vnb[tp][hl]])
            b_o = fb.next()
            for hl in HL:
                mm(b_o, qv(b_o, hl), QKDT[tp][hl], QKDT[tp][hl][:], vnb[tp][hl], vnb[tp][hl][:])
            b_sn = fb.next()
            for hl in HL:
                mm(b_sn, qv(b_sn, hl), Kd[tp][hl], Kd[tp][hl][:], vnb[tp][hl], vnb[tp][hl][:])
            for hl in HL:
                k.op(dve, lambda e, hl=hl: e.tensor_tensor(out=otok[tp][hl][:], in0=qv(b_o, hl), in1=tmpo[tp][hl][:], op=ALU.add),
                     reads=[b_o, tmpo[tp][hl]], writes=[otok[tp][hl]])
            for hl in HL:
                k.op(dve, lambda e, hl=hl: e.scalar_tensor_tensor(out=S32[hl][:], in0=S32[hl][:], scalar=sc(eGL, hl), in1=qv(b_sn, hl), op0=ALU.mult, op1=ALU.add),
                     reads=[S32[hl], eGL[sp_][j], b_sn], writes=[S32[hl]])
                k.op(act, lambda e, hl=hl: e.activation(out=Sbf[hl][:], in_=S32[hl][:], func=AF.Copy), reads=[S32[hl]], writes=[Sbf[hl]])
            for hl in HL:
                k.op(act, lambda e, hl=hl: e.activation(out=sqo[tp][hl][:], in_=otok[tp][hl][:], func=AF.Square), reads=[otok[tp][hl]], writes=[sqo[tp][hl]])
                k.op(dve, lambda e, hl=hl: e.tensor_reduce(out=ssv[tp][hl][:], in_=sqo[tp][hl][:], axis=AX.X, op=ALU.add), reads=[sqo[tp][hl]], writes=[ssv[tp][hl]])
                k.op(act, lambda e, hl=hl: e.activation(out=lnv1[tp][hl][:], in_=ssv[tp][hl][:], func=AF.Ln, bias=c["eps"][:], scale=1.0 / 128),
                     reads=[ssv[tp][hl], c["eps"]], writes=[lnv1[tp][hl]])
                k.op(act, lambda e, hl=hl: e.activation(out=rsv[tp][hl][:], in_=lnv1[tp][hl][:], func=AF.Exp, scale=-0.5), reads=[lnv1[tp][hl]], writes=[rsv[tp][hl]])
                k.op(pool, lambda e, hl=hl: e.tensor_scalar(out=otok[tp][hl][:], in0=otok[tp][hl][:], scalar1=rsv[tp][hl][:, 0:1], scalar2=None, op0=ALU.mult),
                     reads=[otok[tp][hl], rsv[tp][hl]], writes=[otok[tp][hl]])
                k.op(pool, lambda e, hl=hl: e.tensor_tensor(out=onb[tp][hl][:], in0=otok[tp][hl][:], in1=normg[:], op=ALU.mult),
                     reads=[otok[tp][hl], normg], writes=[onb[tp][hl]])
            for hl in HL:
                tr(ev(hl), onb[tp][hl], onb[tp][hl][:])
            for hl in HL:
                k.op(dve, lambda e, hl=hl: e.tensor_tensor(out=ogT[sp_][hl][:, jc], in0=ev(hl), in1=sgT[sp_][hl][:, jc], op=ALU.mult),
                     reads=[bfb, sgT[sp_][hl]], writes=[ogT[sp_][hl]])
        fin = {}
        for jp in (0, 2):
            gens = [pre_tile(jp), pre_tile(jp + 1)]
            alive = True
            while alive:
                alive = False
                for g_ in gens:
                    try:
                        next(g_)
                        alive = True
                    except StopIteration:
                        pass
            rec_tile(jp)
            rec_tile(jp + 1)
        for hl in HL:
            k.dma(sp, io["ogT_out"][hl * 128:(hl + 1) * 128, st * 512:(st + 1) * 512], ogT[sp_][hl][:], reads=[ogT[sp_][hl]], untracked_out=io["ogT_out"])
    k.finish([io["ogT_out"]])


def build_gdn(seq=S):
    nc = bass.Bass("TRN2", target_bir_lowering=False)
    k = KB(nc)
    io = {}
    io["hnT"] = k.dram("hnT", [D, seq], BF16, "ExternalInput")
    io["w_qkvg"] = k.dram("w_qkvg", [D, 2048], F32, "ExternalInput")
    io["w_ab"] = k.dram("w_ab", [D, 8], F32, "ExternalInput")
    io["convw"] = k.dram("convw", [128, 48], F32, "ExternalInput")
    io["dtb"] = k.dram("dtb", [128, 4], F32, "ExternalInput")
    io["alog"] = k.dram("alog", [128, 4], F32, "ExternalInput")
    io["normg"] = k.dram("normg", [128, 128], F32, "ExternalInput")
    io["ogT_out"] = k.dram("ogT_out", [512, seq], BF16, "ExternalOutput")
    c = emit_consts(k)
    emit_gdn_consts(k, c)
    emit_gdn_phase(k, c, io, seq)
    return nc, k


def gdn_host_inputs(inp, layer, r):
    heads = [4 * r + hl for hl in range(4)]
    w_in = inp["gdn_w_in"][layer]
    cols = []
    for h in heads:
        for typ in range(4):
            cols.append(w_in[:, typ * 1024 + h * 128: typ * 1024 + (h + 1) * 128])
    w_qkvg = np.ascontiguousarray(np.concatenate(cols, axis=1))
    w_ab = np.ascontiguousarray(np.concatenate([w_in[:, 4096 + heads[0]:4096 + heads[0] + 4], w_in[:, 4104 + heads[0]:4104 + heads[0] + 4]], axis=1))
    cw = inp["gdn_conv"][layer]
    convw = np.zeros((128, 48), np.float32)
    for hl, h in enumerate(heads):
        for typ in range(3):
            ch = hl * 3 + typ
            convw[:, ch * 4:(ch + 1) * 4] = cw[:, typ * 1024 + h * 128: typ * 1024 + (h + 1) * 128].T
    dtb = np.ascontiguousarray(np.broadcast_to(inp["gdn_dt_bias"][layer][heads[0]:heads[0] + 4][None, :], (128, 4))).astype(np.float32)
    alog = np.ascontiguousarray(np.broadcast_to(inp["gdn_a_log"][layer][heads[0]:heads[0] + 4][None, :], (128, 4))).astype(np.float32)
    normg = np.ascontiguousarray(np.broadcast_to(inp["gdn_norm"][layer][None, :], (128, 128))).astype(np.float32)
    return dict(w_qkvg=w_qkvg, w_ab=w_ab, convw=convw, dtb=dtb, alog=alog, normg=normg)


_PROGS = {}


def _prog(key, builder):
    if key not in _PROGS:
        _PROGS[key] = builder()[0]
    return _PROGS[key]


def _run(nc, maps):
    res = run_bass_kernel_spmd(nc, maps, core_ids=list(range(NCORES)))
    return res.results


def _gains(inp, layer, nxt_layer, kv, q_layer):
    g = np.zeros((128, NGCOL), np.float32)
    if layer is not None:
        g[:, 0:8] = col8(inp["ln_ffn"][layer])
        g[:, 8:16] = col8(inp["ln_ple"][layer])
    if kv:
        g[:, 16:24] = col8(inp["kv_norm"])
        g[:, 24] = inp["k_norm"]
    if nxt_layer is not None:
        g[:, 25:33] = col8(inp["ln_mix"][nxt_layer])
    if q_layer is not None:
        g[:, 33] = inp["sb_q_norm"][q_layer]
    return g


def kernel(**inp):
    inp = {k_: np.asarray(v) for k_, v in inp.items()}
    x, p = inp["x"], inp["p"]
    f32 = np.float32

    def tok(c):
        return c // 2, slice((c % 2) * NT, (c % 2 + 1) * NT)

    nc = _prog("tokA", lambda: build_token(dict(tail=False, next="gdn")))
    g = _gains(inp, None, 0, False, None)
    maps = []
    for c in range(NCORES):
        b, sl = tok(c)
        maps.append({"hT_in": np.ascontiguousarray(x[b, sl].T), "gains": g})
    res = _run(nc, maps)
    hT = [res[c]["hT_out"] for c in range(NCORES)]
    hnT = [res[c]["hnT_out"] for c in range(NCORES)]

    def tail_maps(layer, oT_full, w_o, extra):
        maps = []
        for c in range(NCORES):
            b, sl = tok(c)
            m = {"hT_in": hT[c], "oT": np.ascontiguousarray(oT_full[b][:, sl]), "w_o": w_o,
                 "ffn_w_in": inp["ffn_w_in"][layer], "ffn_w_out": inp["ffn_w_out"][layer],
                 "ple_w_gate": inp["ple_w_gate"][layer], "ple_w_proj": inp["ple_w_proj"][layer],
                 "pT": np.ascontiguousarray(p[layer, b, sl].T)}
            m.update(extra)
            maps.append(m)
        return maps

    k_full = v_full = None
    for layer in range(DEPTH):
        if layer < 2:
            nc = _prog("gdn", build_gdn)
            maps = []
            for c in range(NCORES):
                b, r = c // 2, c % 2
                m = gdn_host_inputs(inp, layer, r)
                m["hnT"] = np.ascontiguousarray(np.concatenate([hnT[2 * b], hnT[2 * b + 1]], axis=1))
                maps.append(m)
            res = _run(nc, maps)
            oT_full = [np.concatenate([res[2 * b]["ogT_out"], res[2 * b + 1]["ogT_out"]], axis=0) for b in range(NB)]
            w_o = inp["gdn_w_out"][layer]
        else:
            nc = _prog("sb", build_sb)
            maps = []
            for c in range(NCORES):
                b, r = c // 2, c % 2
                rows = slice(512 * r, 512 * (r + 1))
                q_full = np.concatenate([qT[2 * b], qT[2 * b + 1]], axis=1)
                maps.append({"qT": np.ascontiguousarray(q_full[rows]), "kT": np.ascontiguousarray(k_full[b][rows]),
                             "v": np.ascontiguousarray(v_full[b][:, rows])})
            res = _run(nc, maps)
            oT_full = [np.concatenate([res[2 * b]["oT_out"], res[2 * b + 1]["oT_out"]], axis=0) for b in range(NB)]
            w_o = inp["sb_w_out"][layer - 2]
        if layer == 0:
            nc = _prog("tokB", lambda: build_token(dict(tail=True, next="gdn")))
            maps = tail_maps(layer, oT_full, w_o, {"gains": _gains(inp, layer, 1, False, None)})
        elif layer == 1:
            nc = _prog("tokC", lambda: build_token(dict(tail=True, next="sb", kv=True)))
            maps = tail_maps(layer, oT_full, w_o, {"gains": _gains(inp, layer, 2, True, 0), "w_kv": inp["w_kv"], "w_q": inp["sb_w_q"][0]})
        elif layer == 2:
            nc = _prog("tokD", lambda: build_token(dict(tail=True, next="sb")))
            maps = tail_maps(layer, oT_full, w_o, {"gains": _gains(inp, layer, 3, False, 1), "w_q": inp["sb_w_q"][1]})
        else:
            nc = _prog("tokE", lambda: build_token(dict(tail=True, next=None)))
            maps = tail_maps(layer, oT_full, w_o, {"gains": _gains(inp, layer, None, False, None)})
        res = _run(nc, maps)
        hT = [res[c]["hT_out"] for c in range(NCORES)]
        if layer == 0:
            hnT = [res[c]["hnT_out"] for c in range(NCORES)]
        if layer == 1:
            k_full = [np.concatenate([res[2 * b]["kT_out"], res[2 * b + 1]["kT_out"]], axis=1) for b in range(NB)]
            v_full = [np.concatenate([res[2 * b]["v_out"], res[2 * b + 1]["v_out"]], axis=0) for b in range(NB)]
        if layer in (1, 2):
            qT = [res[c]["qT_out"] for c in range(NCORES)]

    out = np.empty((NB, S, D), f32)
    for c in range(NCORES):
        b, sl = tok(c)
        out[b, sl] = np.asarray(hT[c], f32).T
    return out
```

```python
import numpy as np
import ml_dtypes
import concourse.bass as bass
import concourse.mybir as mybir
from concourse.bass_utils import run_bass_kernel_spmd

F32 = mybir.dt.float32
BF16 = mybir.dt.bfloat16
AF = mybir.ActivationFunctionType
ALU = mybir.AluOpType
NPBF = ml_dtypes.bfloat16

D = 1024
S = 4096
NB = 4
DEPTH = 4
NT = 2048
TT = 512
NTT = NT // TT
FH = 2816
NJ = FH // 128
EPS = 1e-6
NCORES = 8


class Buf:
    __slots__ = ("name", "t", "w", "r", "dsem", "dcnt", "excl")

    def __init__(self, name, t, excl=False):
        self.name = name
        self.t = t
        self.w = None
        self.r = {}
        self.dsem = None
        self.dcnt = 0
        self.excl = excl

    def __getitem__(self, idx):
        return self.t[idx]


class Eng:
    def __init__(self, k, name):
        self.name = name
        self.e = getattr(k.nc, name)
        self.sem = k.nc.alloc_semaphore("es_" + name)
        self.cnt = 0
        self.waited = {}

    def wait(self, dep):
        sem, val = dep
        key = id(sem)
        if self.waited.get(key, 0) >= val:
            return
        self.waited[key] = val
        self.e.wait_ge(sem, val)


class KB:
    def __init__(self, nc, same_engine_raw=True):
        self.nc = nc
        self.same_engine_raw = same_engine_raw
        self.pe = Eng(self, "tensor")
        self.dve = Eng(self, "vector")
        self.act = Eng(self, "scalar")
        self.pool = Eng(self, "gpsimd")
        self.sp = Eng(self, "sync")
        self.n_inst = 0
        self._uid = 0
        self.out_tk = {}

    def uid(self, s):
        self._uid += 1
        return "%s_%d" % (s, self._uid)

    def sb(self, name, shape, dt):
        return Buf(name, self.nc.alloc_sbuf_tensor(self.uid(name), list(shape), dt))

    def ps(self, name, shape, dt=F32):
        return Buf(name, self.nc.alloc_psum_tensor(self.uid(name), list(shape), dt), excl=True)

    def dram(self, name, shape, dt, kind):
        return Buf(name, self.nc.dram_tensor(name, list(shape), dt, kind=kind).ap())

    def _deps(self, eng, reads, writes):
        raw = []
        war = []
        for b in reads:
            if b.w is not None:
                raw.append(b.w)
            if b.excl:
                war.extend(b.r.values())
        for b in writes:
            if b.w is not None:
                raw.append(b.w)
            war.extend(b.r.values())
        for d in raw:
            if d[0] is eng.sem:
                if not self.same_engine_raw or eng is self.pe or eng is self.sp:
                    continue
            eng.wait(d)
        for d in war:
            if d[0] is eng.sem:
                continue
            eng.wait(d)

    def op(self, eng, fn, reads=(), writes=()):
        self._deps(eng, reads, writes)
        inst = fn(eng.e)
        eng.cnt += 1
        inst.then_inc(eng.sem, 1)
        tk = (eng.sem, eng.cnt)
        for b in reads:
            b.r[id(tk[0])] = tk
        for b in writes:
            b.w = tk
            b.r = {}
        self.n_inst += 1
        return inst

    def dma(self, eng, out_ap, in_ap, reads=(), writes=(), sembuf=None, untracked_out=None, **kw):
        self._deps(eng, reads, writes)
        if untracked_out is not None:
            sembuf = reads[0]
        sb_ = sembuf or (writes[0] if writes else reads[0])
        if sb_.dsem is None:
            sb_.dsem = self.nc.alloc_semaphore(self.uid("ds_" + sb_.name))
        inst = eng.e.dma_start(out=out_ap, in_=in_ap, **kw)
        sb_.dcnt += 16
        inst.then_inc(sb_.dsem, 16)
        tk = (sb_.dsem, sb_.dcnt)
        for b in reads:
            b.r[id(tk[0])] = tk
        for b in writes:
            b.w = tk
            b.r = {}
        if untracked_out is not None:
            self.out_tk[id(tk[0])] = tk
        self.n_inst += 1
        return inst

    def finish(self, out_bufs):
        for tk in self.out_tk.values():
            self.sp.wait(tk)


def emit_consts(k):
    c = {}
    c["ones_bf"] = k.sb("ones_bf", [128, 128], BF16)
    k.op(k.pool, lambda e: e.memset(c["ones_bf"][:], 1.0), writes=[c["ones_bf"]])
    c["eps"] = k.sb("eps", [128, 1], F32)
    k.op(k.pool, lambda e: e.memset(c["eps"][:], EPS), writes=[c["eps"]])
    c["one"] = k.sb("one", [128, 1], F32)
    k.op(k.pool, lambda e: e.memset(c["one"][:], 1.0), writes=[c["one"]])
    return c


class Rot:
    def __init__(self, bufs):
        self.bufs = bufs
        self.i = 0

    def next(self):
        b = self.bufs[self.i % len(self.bufs)]
        self.i += 1
        return b


def emit_rstd(k, c, srcs, inv_n, ps_ss, sq_rot, tmp_rot, width=TT, ln_bias=None, out=None):
    n = len(srcs)
    for i, (ap, b) in enumerate(srcs):
        sq = sq_rot.next()
        k.op(k.act, lambda e, ap=ap, sq=sq: e.activation(out=sq[:, 0:width], in_=ap, func=AF.Square),
             reads=[b], writes=[sq])
        k.op(k.pe, lambda e, sq=sq, i=i: e.matmul(ps_ss[:, 0:width], lhsT=c["ones_bf"][:], rhs=sq[:, 0:width],
                                                  start=(i == 0), stop=(i == n - 1)),
             reads=[sq, c["ones_bf"]], writes=[ps_ss])
    lnv = tmp_rot.next()
    k.op(k.act, lambda e: e.activation(out=lnv[:, 0:width], in_=ps_ss[:, 0:width], func=AF.Ln, bias=c["eps"][:], scale=inv_n),
         reads=[ps_ss, c["eps"]], writes=[lnv])
    rstd = out if out is not None else tmp_rot.next()
    if ln_bias is None:
        k.op(k.act, lambda e: e.activation(out=rstd[:, 0:width], in_=lnv[:, 0:width], func=AF.Exp, scale=-0.5),
             reads=[lnv], writes=[rstd])
    else:
        k.op(k.act, lambda e: e.activation(out=rstd[:, 0:width], in_=lnv[:, 0:width], func=AF.Exp, scale=-0.5, bias=ln_bias[:]),
             reads=[lnv, ln_bias], writes=[rstd])
    return rstd


def emit_token_phase(k, c, io, cfg):
    tail, nxt, kv = cfg["tail"], cfg["next"], cfg.get("kv", False)
    sp, pool, pe, act, dve = k.sp, k.pool, k.pe, k.act, k.dve

    hT = [[k.sb("hT%d_%d" % (cc, tt), [128, TT], F32) for tt in range(NTT)] for cc in range(8)]
    hin = io["hT_in"]
    for cc in range(8):
        for tt in range(NTT):
            k.dma(sp, hT[cc][tt][:], hin[cc * 128:(cc + 1) * 128, tt * TT:(tt + 1) * TT], reads=[hin], writes=[hT[cc][tt]])

    gains = k.sb("gains", [128, io["gains"].t.shape[1]], F32)
    k.dma(sp, gains[:], io["gains"][:, :], reads=[io["gains"]], writes=[gains])

    hn = [k.sb("hn%d" % tt, [128, 8, TT], BF16) for tt in range(NTT)]
    psb = [k.ps("psb%d" % i, [128, TT], F32) for i in range(8)]
    sq_rot = Rot([k.sb("sq%d" % i, [128, TT], BF16) for i in range(3)])
    tmp_rot = Rot([k.sb("tmpf%d" % i, [128, TT], F32) for i in range(3)])
    wsq = [k.sb("wsq%d" % i, [128, 8, 1024], BF16) for i in range(2)]
    wsq_rot = Rot(wsq)

    def load_w(dst, src, ncol, col0=0, kc=8):
        v = src.t.rearrange("(c p) n -> p c n", p=128)
        k.dma(pool, dst[:, 0:kc, 0:ncol], v[:, 0:kc, col0:col0 + ncol], reads=[src], writes=[dst])

    def norm_to_hn(gcol):
        for tt in range(NTT):
            rstd = emit_rstd(k, c, [(hT[cc][tt][:], hT[cc][tt]) for cc in range(8)], 1.0 / D, psb[7], sq_rot, tmp_rot)
            for cc in range(8):
                k.op(dve, lambda e, cc=cc, tt=tt, rstd=rstd: e.scalar_tensor_tensor(
                    out=hn[tt][:, cc, :], in0=hT[cc][tt][:], scalar=gains[:, gcol + cc:gcol + cc + 1], in1=rstd[:],
                    op0=ALU.mult, op1=ALU.mult), reads=[hT[cc][tt], gains, rstd], writes=[hn[tt]])

    def proj_add(w, rhs_of, nk, pbanks):
        for tt in range(NTT):
            for n in range(8):
                ps = pbanks.next()
                for kk in range(nk):
                    ap, b = rhs_of(tt, kk)
                    k.op(pe, lambda e, ps=ps, kk=kk, n=n, ap=ap: e.matmul(ps[:], lhsT=w[:, kk, n * 128:(n + 1) * 128], rhs=ap,
                                                                         start=(kk == 0), stop=(kk == nk - 1)),
                         reads=[w, b], writes=[ps])
                k.op(dve, lambda e, ps=ps, n=n, tt=tt: e.tensor_tensor(out=hT[n][tt][:], in0=ps[:], in1=hT[n][tt][:], op=ALU.add),
                     reads=[ps, hT[n][tt]], writes=[hT[n][tt]])

    GC = cfg["gcols"]
    if tail:
        big = [k.sb("big%d" % i, [128, 2 * NT], BF16) for i in range(2)]
        ov = io["oT"].t.rearrange("(c p) n -> p c n", p=128)
        w_o = wsq_rot.next()
        load_w(w_o, io["w_o"], 1024)

        def o_rhs(tt, kk):
            b = big[tt % 2]
            if kk == 0:
                k.dma(sp, b[:, :].rearrange("p (c t) -> p c t", c=8), ov[:, :, tt * TT:(tt + 1) * TT], reads=[io["oT"]], writes=[b])
            return b[:, kk * TT:(kk + 1) * TT], b
        if 'oproj' not in cfg.get('skip', ()):
            proj_add(w_o, o_rhs, 8, Rot(psb[0:4]))

        norm_to_hn(GC["ln_ffn"])
        PC = 2
        npieces = NJ // PC
        wg = [k.sb("wg%d" % i, [128, 8, PC * 128], BF16) for i in range(2)]
        wu = [k.sb("wu%d" % i, [128, 8, PC * 128], BF16) for i in range(2)]
        wo = [k.sb("wo%d" % i, [128, PC, 1024], BF16) for i in range(2)]
        class _HV:
            def __init__(self, b, jj):
                self.b, self.jj = b, jj

            def __getitem__(self, idx):
                p, fs = idx
                return self.b[p, self.jj * NT + fs.start:self.jj * NT + fs.stop]
        hidb = big
        hid = [[_HV(big[i], jj) for jj in range(PC)] for i in range(2)]
        sg_rot = Rot([k.sb("sg%d" % i, [128, TT], F32) for i in range(2)])
        w_in, w_out = io["ffn_w_in"], io["ffn_w_out"]
        wout_v = w_out.t.rearrange("(c p) n -> p c n", p=128)

        def load_piece(P):
            s = P % 2
            load_w(wg[s], w_in, PC * 128, col0=P * PC * 128)
            load_w(wu[s], w_in, PC * 128, col0=FH + P * PC * 128)
            k.dma(pool, wo[s][:], wout_v[:, P * PC:(P + 1) * PC, :], reads=[w_out], writes=[wo[s]])

        if 'ffn' in cfg.get('skip', ()):
            npieces = 0
        else:
            load_piece(0)
        gbanks = Rot(psb[0:2]); ubanks = Rot(psb[2:4]); ybanks = Rot(psb[4:7])
        for P in range(npieces):
            s = P % 2
            if P + 1 < npieces:
                load_piece(P + 1)
            for jj in range(PC):
                for tt in range(NTT):
                    pg = gbanks.next(); pu = ubanks.next()
                    for kk in range(8):
                        k.op(pe, lambda e, pg=pg, kk=kk, jj=jj, tt=tt: e.matmul(pg[:], lhsT=wg[s][:, kk, jj * 128:(jj + 1) * 128], rhs=hn[tt][:, kk, :],
                                                                                start=(kk == 0), stop=(kk == 7)), reads=[wg[s], hn[tt]], writes=[pg])
                    for kk in range(8):
                        k.op(pe, lambda e, pu=pu, kk=kk, jj=jj, tt=tt: e.matmul(pu[:], lhsT=wu[s][:, kk, jj * 128:(jj + 1) * 128], rhs=hn[tt][:, kk, :],
                                                                                start=(kk == 0), stop=(kk == 7)), reads=[wu[s], hn[tt]], writes=[pu])
                    sg = sg_rot.next()
                    k.op(act, lambda e, sg=sg, pg=pg: e.activation(out=sg[:], in_=pg[:], func=AF.Silu), reads=[pg], writes=[sg])
                    k.op(dve, lambda e, sg=sg, pu=pu, jj=jj, tt=tt: e.tensor_tensor(out=hid[s][jj][:, tt * TT:(tt + 1) * TT], in0=pu[:], in1=sg[:], op=ALU.mult),
                         reads=[pu, sg], writes=[hidb[s]])
            for tt in range(NTT):
                for n in range(8):
                    py = ybanks.next()
                    for jj in range(PC):
                        k.op(pe, lambda e, py=py, jj=jj, n=n, tt=tt: e.matmul(py[:], lhsT=wo[s][:, jj, n * 128:(n + 1) * 128], rhs=hid[s][jj][:, tt * TT:(tt + 1) * TT],
                                                                              start=(jj == 0), stop=(jj == PC - 1)), reads=[wo[s], hidb[s]], writes=[py])
                    k.op(dve, lambda e, py=py, n=n, tt=tt: e.tensor_tensor(out=hT[n][tt][:], in0=py[:], in1=hT[n][tt][:], op=ALU.add),
                         reads=[py, hT[n][tt]], writes=[hT[n][tt]])

        norm_to_hn(GC["ln_ple"])
        w_pg = wsq_rot.next()
        load_w(w_pg, io["ple_w_gate"], 1024)
        w_pp = k.sb("w_pp", [128, 2, 1024], BF16)
        load_w(w_pp, io["ple_w_proj"], 1024, kc=2)
        pTb = [k.sb("pT%d" % i, [128, 2, TT], BF16) for i in range(2)]
        pT = [pTb[tt % 2] for tt in range(NTT)]
        pv = io["pT"].t.rearrange("(c p) n -> p c n", p=128)
        gb = Rot(psb[0:2]); pb = Rot(psb[2:4])
        for tt in range(NTT if 'ple' not in cfg.get('skip', ()) else 0):
            k.dma(pool, pT[tt][:], pv[:, :, tt * TT:(tt + 1) * TT], reads=[io["pT"]], writes=[pT[tt]])
            for n in range(8):
                pg = gb.next(); pp = pb.next()
                for kk in range(8):
                    k.op(pe, lambda e, pg=pg, kk=kk, n=n, tt=tt: e.matmul(pg[:], lhsT=w_pg[:, kk, n * 128:(n + 1) * 128], rhs=hn[tt][:, kk, :],
                                                                          start=(kk == 0), stop=(kk == 7)), reads=[w_pg, hn[tt]], writes=[pg])
                for kk in range(2):
                    k.op(pe, lambda e, pp=pp, kk=kk, n=n, tt=tt: e.matmul(pp[:], lhsT=w_pp[:, kk, n * 128:(n + 1) * 128], rhs=pT[tt][:, kk, :],
                                                                          start=(kk == 0), stop=(kk == 1)), reads=[w_pp, pT[tt]], writes=[pp])
                sg = sg_rot.next()
                k.op(act, lambda e, sg=sg, pg=pg: e.activation(out=sg[:], in_=pg[:], func=AF.Sigmoid), reads=[pg], writes=[sg])
                k.op(dve, lambda e, sg=sg, pp=pp: e.tensor_tensor(out=sg[:], in0=pp[:], in1=sg[:], op=ALU.mult), reads=[pp, sg], writes=[sg])
                k.op(dve, lambda e, sg=sg, n=n, tt=tt: e.tensor_tensor(out=hT[n][tt][:], in0=sg[:], in1=hT[n][tt][:], op=ALU.add),
                     reads=[sg, hT[n][tt]], writes=[hT[n][tt]])

    if "hT_out" in io:
        hout = io["hT_out"]
        for cc in range(8):
            for tt in range(NTT):
                k.dma(sp, hout[cc * 128:(cc + 1) * 128, tt * TT:(tt + 1) * TT], hT[cc][tt][:], reads=[hT[cc][tt]], untracked_out=hout)

    obuf_rot = Rot([k.sb("obuf%d" % i, [128, TT], BF16) for i in range(3)])

    def headnorm_proj(w, gcol, out_dram, extra_scale):
        lnb = None
        if extra_scale != 1.0:
            lnb = k.sb("lnb", [128, 1], F32)
            k.op(pool, lambda e: e.memset(lnb[:], float(np.log(extra_scale))), writes=[lnb])
        qb = Rot(psb[0:3])
        for tt in range(NTT):
            for n in range(8):
                pq = qb.next()
                for kk in range(8):
                    k.op(pe, lambda e, pq=pq, kk=kk, n=n, tt=tt: e.matmul(pq[:], lhsT=w[:, kk, n * 128:(n + 1) * 128], rhs=hn[tt][:, kk, :],
                                                                          start=(kk == 0), stop=(kk == 7)), reads=[w, hn[tt]], writes=[pq])
                rstd = emit_rstd(k, c, [(pq[:], pq)], 1.0 / 128, psb[7], sq_rot, tmp_rot, ln_bias=lnb)
                ob = obuf_rot.next()
                k.op(dve, lambda e, ob=ob, pq=pq, n=n, rstd=rstd: e.scalar_tensor_tensor(out=ob[:], in0=pq[:], scalar=gains[:, gcol:gcol + 1], in1=rstd[:],
                                                                                       op0=ALU.mult, op1=ALU.mult), reads=[pq, gains, rstd], writes=[ob])
                k.dma(sp, out_dram[n * 128:(n + 1) * 128, tt * TT:(tt + 1) * TT], ob[:], reads=[ob], untracked_out=out_dram)

    if kv:
        norm_to_hn(GC["kv_norm"])
        w_k = wsq_rot.next()
        load_w(w_k, io["w_kv"], 1024, col0=0)
        headnorm_proj(w_k, GC["k_norm"], io["kT_out"], 1.0)
        w_v = wsq_rot.next()
        load_w(w_v, io["w_kv"], 1024, col0=1024)
        vb = Rot(psb[3:6])
        vo_rot = Rot([k.sb("vo%d" % i, [128, TT], BF16) for i in range(2)])
        vout = io["v_out"]
        for tt in range(NTT):
            for t4 in range(4):
                for nh in range(2):
                    pv_ = vb.next()
                    for kk in range(8):
                        k.op(pe, lambda e, pv_=pv_, kk=kk, nh=nh, tt=tt, t4=t4: e.matmul(pv_[:], lhsT=hn[tt][:, kk, t4 * 128:(t4 + 1) * 128], rhs=w_v[:, kk, nh * 512:(nh + 1) * 512],
                                                                                      start=(kk == 0), stop=(kk == 7)), reads=[w_v, hn[tt]], writes=[pv_])
                    vo = vo_rot.next()
                    k.op(act, lambda e, vo=vo, pv_=pv_: e.activation(out=vo[:], in_=pv_[:], func=AF.Copy), reads=[pv_], writes=[vo])
                    r0 = tt * TT + t4 * 128
                    k.dma(sp, vout[r0:r0 + 128, nh * 512:(nh + 1) * 512], vo[:], reads=[vo], untracked_out=vout)

    if nxt is not None:
        norm_to_hn(GC["ln_mix_next"])
        if nxt == "gdn":
            hnout = io["hnT_out"].t.rearrange("(c p) n -> p c n", p=128)
            for tt in range(NTT):
                k.dma(sp, hnout[:, :, tt * TT:(tt + 1) * TT], hn[tt][:], reads=[hn[tt]], untracked_out=io["hnT_out"])
        else:
            w_q = wsq_rot.next()
            load_w(w_q, io["w_q"], 1024)
            headnorm_proj(w_q, GC["q_norm"], io["qT_out"], 128 ** -0.5)

    outs = [io[n] for n in io if n.endswith("_out")]
    k.finish(outs)


GCOLS = {"ln_ffn": 0, "ln_ple": 8, "kv_norm": 16, "k_norm": 24, "ln_mix_next": 25, "q_norm": 33}
NGCOL = 34


def col8(v):
    return np.ascontiguousarray(np.asarray(v, np.float32).reshape(8, 128).T)


def build_token(cfg):
    nc = bass.Bass("TRN2", target_bir_lowering=False)
    k = KB(nc)
    io = {}
    io["hT_in"] = k.dram("hT_in", [D, NT], F32, "ExternalInput")
    io["gains"] = k.dram("gains", [128, NGCOL], F32, "ExternalInput")
    if cfg["tail"]:
        io["oT"] = k.dram("oT", [D, NT], BF16, "ExternalInput")
        io["w_o"] = k.dram("w_o", [D, D], F32, "ExternalInput")
        io["ffn_w_in"] = k.dram("ffn_w_in", [D, 2 * FH], F32, "ExternalInput")
        io["ffn_w_out"] = k.dram("ffn_w_out", [FH, D], F32, "ExternalInput")
        io["ple_w_gate"] = k.dram("ple_w_gate", [D, D], F32, "ExternalInput")
        io["ple_w_proj"] = k.dram("ple_w_proj", [256, D], F32, "ExternalInput")
        io["pT"] = k.dram("pT", [256, NT], F32, "ExternalInput")
    if cfg.get("kv"):
        io["w_kv"] = k.dram("w_kv", [D, 2 * D], F32, "ExternalInput")
        io["kT_out"] = k.dram("kT_out", [D, NT], BF16, "ExternalOutput")
        io["v_out"] = k.dram("v_out", [NT, D], BF16, "ExternalOutput")
    if cfg["next"] == "gdn":
        io["hnT_out"] = k.dram("hnT_out", [D, NT], BF16, "ExternalOutput")
    elif cfg["next"] == "sb":
        io["w_q"] = k.dram("w_q", [D, D], F32, "ExternalInput")
        io["qT_out"] = k.dram("qT_out", [D, NT], BF16, "ExternalOutput")
    io["hT_out"] = k.dram("hT_out", [D, NT], F32, "ExternalOutput")
    c = emit_consts(k)
    cfg = dict(cfg)
    cfg["gcols"] = GCOLS
    emit_token_phase(k, c, io, cfg)
    return nc, k


def emit_masks(k, c):
    nc = k.nc
    m = k.sb("mstrict", [128, 128], F32)
    k.op(k.pool, lambda e: e.memset(m[:], 1.0), writes=[m])
    k.op(k.pool, lambda e: e.affine_select(out=m[:], in_=m[:], pattern=[[1, 128]], compare_op=ALU.is_gt, fill=0.0,
                                           base=0, channel_multiplier=-1), reads=[m], writes=[m])
    c["mstrict"] = m
    t = k.sb("triinc", [128, 128], BF16)
    k.op(k.pool, lambda e: e.memset(t[:], 1.0), writes=[t])
    k.op(k.pool, lambda e: e.affine_select(out=t[:], in_=t[:], pattern=[[-1, 128]], compare_op=ALU.is_ge, fill=0.0,
                                           base=0, channel_multiplier=1), reads=[t], writes=[t])
    c["triinc"] = t


def emit_sb_phase(k, c, io, nheads=4, seq=S):
    sp, pool, pe, act, dve = k.sp, k.pool, k.pe, k.act, k.dve
    nblk = seq // 128
    nsb = seq // 512
    qT = k.sb("qT", [128, nheads, seq], BF16)
    kT = k.sb("kT", [128, nheads, seq], BF16)
    V = k.sb("V", [128, nblk, nheads * 128], BF16)
    k.dma(sp, kT[:], io["kT"].t.rearrange("(h p) t -> p h t", p=128), reads=[io["kT"]], writes=[kT])
    k.dma(sp, qT[:], io["qT"].t.rearrange("(h p) t -> p h t", p=128), reads=[io["qT"]], writes=[qT])
    k.dma(sp, V[:], io["v"].t.rearrange("(b p) n -> p b n", p=128), reads=[io["v"]], writes=[V])

    E = [k.sb("E%d" % i, [128, 512], F32) for i in range(3)]
    SP = [k.sb("SP%d" % i, [128, 512], BF16) for i in range(3)]
    W1 = [k.sb("W1%d" % i, [128, 512], F32) for i in range(2)]
    WG = [k.sb("WG%d" % i, [128, 512], BF16) for i in range(3)]
    LS = [k.sb("LS%d" % i, [128, 512], BF16) for i in range(2)]
    OB = [k.sb("OB%d" % i, [128, 512], BF16) for i in range(2)]
    pz = [k.ps("pz%d" % i, [128, 512], F32) for i in range(2)]
    pi = [k.ps("pi%d" % i, [128, 512], F32) for i in range(2)]
    po = [k.ps("po%d" % i, [128, 512], F32) for i in range(2)]

    steps = []
    sbi = 0
    for h in range(nheads):
        for sb in range(nsb):
            n_s = 4 * sb + 4
            for idx, i in enumerate(range(n_s - 1, -1, -1)):
                m = i - 4 * sb
                c0 = max(0, 128 * m)
                steps.append(dict(h=h, sb=sb, i=i, c0=c0, diag=(m >= 0), first=(idx == 0), last=(i == 0), sbi=sbi))
            sbi += 1
    ns = len(steps)
    oT = io["oT_out"]

    def stageA(n):
        st = steps[n]
        h, i, c0, t0 = st["h"], st["i"], st["c0"], st["sb"] * 512
        z = pz[n % 2]; e_ = E[n % 3]; s_ = SP[n % 3]
        k.op(pe, lambda e: e.matmul(z[:, c0:512], lhsT=kT[:, h, i * 128:(i + 1) * 128], rhs=qT[:, h, t0 + c0:t0 + 512], start=True, stop=True),
             reads=[kT, qT], writes=[z])
        k.op(act, lambda e: e.activation(out=e_[:, c0:512], in_=z[:, c0:512], func=AF.Exp), reads=[z], writes=[e_])
        if st["diag"]:
            k.op(pool, lambda e: e.tensor_tensor(out=e_[:, c0:c0 + 128], in0=e_[:, c0:c0 + 128], in1=c["mstrict"][:], op=ALU.mult),
                 reads=[e_, c["mstrict"]], writes=[e_])
        k.op(act, lambda e: e.activation(out=s_[:, c0:512], in_=e_[:, c0:512], func=AF.Ln, bias=c["one"][:], scale=1.0), reads=[e_, c["one"]], writes=[s_])

    def stageB(n):
        st = steps[n]
        c0 = st["c0"]
        e_ = E[n % 3]; s_ = SP[n % 3]; inc = pi[n % 2]; w1 = W1[n % 2]; wg = WG[n % 3]; ls = LS[st["sbi"] % 2]
        if st["first"]:
            k.op(pool, lambda e: e.memset(ls[:], 0.0), writes=[ls])
        k.op(pe, lambda e: e.matmul(inc[:, c0:512], lhsT=c["triinc"][:], rhs=s_[:, c0:512], start=True, stop=st["first"]),
             reads=[c["triinc"], s_], writes=[inc])
        if not st["first"]:
            k.op(pe, lambda e: e.matmul(inc[:, c0:512], lhsT=c["ones_bf"][:], rhs=ls[:, c0:512], start=False, stop=True),
                 reads=[c["ones_bf"], ls], writes=[inc])
        if not st["last"]:
            k.op(pool, lambda e: e.tensor_tensor(out=ls[:, c0:512], in0=ls[:, c0:512], in1=s_[:, c0:512], op=ALU.add),
                 reads=[ls, s_], writes=[ls])
        k.op(act, lambda e: e.activation(out=w1[:, c0:512], in_=inc[:, c0:512], func=AF.Exp, scale=-1.0), reads=[inc], writes=[w1])
        if c0 > 0:
            k.op(pool, lambda e: e.memset(wg[:, 0:c0], 0.0), writes=[wg])
        k.op(dve, lambda e: e.tensor_tensor(out=wg[:, c0:512], in0=e_[:, c0:512], in1=w1[:, c0:512], op=ALU.mult),
             reads=[e_, w1], writes=[wg])

    def stageC(n):
        st = steps[n]
        h, i = st["h"], st["i"]
        wg = WG[n % 3]; o = po[st["sbi"] % 2]
        k.op(pe, lambda e: e.matmul(o[:], lhsT=V[:, i, h * 128:(h + 1) * 128], rhs=wg[:], start=st["first"], stop=st["last"]),
             reads=[V, wg], writes=[o])
        if st["last"]:
            ob = OB[st["sbi"] % 2]
            k.op(act, lambda e: e.activation(out=ob[:], in_=o[:], func=AF.Copy), reads=[o], writes=[ob])
            t0 = st["sb"] * 512
            k.dma(sp, oT[h * 128:(h + 1) * 128, t0:t0 + 512], ob[:], reads=[ob], untracked_out=oT)

    for it in range(ns + 2):
        if it < ns:
            stageA(it)
        if 0 <= it - 1 < ns:
            stageB(it - 1)
        if 0 <= it - 2 < ns:
            stageC(it - 2)
    k.finish([oT])


def build_sb(nheads=4, seq=S):
    nc = bass.Bass("TRN2", target_bir_lowering=False)
    k = KB(nc)
    io = {}
    io["qT"] = k.dram("qT", [nheads * 128, seq], BF16, "ExternalInput")
    io["kT"] = k.dram("kT", [nheads * 128, seq], BF16, "ExternalInput")
    io["v"] = k.dram("v", [seq, nheads * 128], BF16, "ExternalInput")
    io["oT_out"] = k.dram("oT_out", [nheads * 128, seq], BF16, "ExternalOutput")
    c = emit_consts(k)
    emit_masks(k, c)
    emit_sb_phase(k, c, io, nheads, seq)
    return nc, k


NEG = -30000.0
AX = mybir.AxisListType


def emit_gdn_consts(k, c):
    def mk(name, dt, init, pattern, cm, cmp, fill):
        t = k.sb(name, [128, 128], dt)
        k.op(k.pool, lambda e: e.memset(t[:], init), writes=[t])
        if pattern is not None:
            k.op(k.pool, lambda e: e.affine_select(out=t[:], in_=t[:], pattern=pattern, compare_op=cmp, fill=fill,
                                                   base=0, channel_multiplier=cm), reads=[t], writes=[t])
        c[name] = t
    mk("triincl", F32, 1.0, [[1, 128]], -1, ALU.is_ge, 0.0)
    mk("strictgt", F32, 1.0, [[-1, 128]], 1, ALU.is_gt, 0.0)
    mk("mneg_strict", F32, 0.0, [[-1, 128]], 1, ALU.is_gt, NEG)
    mk("mneg_inclT", F32, 0.0, [[1, 128]], -1, ALU.is_ge, NEG)
    mk("ident_f", F32, 1.0, [[-1, 128]], 1, ALU.is_equal, 0.0)
    mk("ident_bf", BF16, 1.0, [[-1, 128]], 1, ALU.is_equal, 0.0)
    mk("ones_f", F32, 1.0, None, 0, None, 0.0)


def emit_gdn_phase(k, c, io, seq=S):
    sp, pool, pe, act, dve = k.sp, k.pool, k.pe, k.act, k.dve
    NH = 4
    nst = seq // 512

    W = k.sb("gW", [128, 8, NH * 4 * 128], BF16)
    k.dma(pool, W[:], io["w_qkvg"].t.rearrange("(c p) n -> p c n", p=128), reads=[io["w_qkvg"]], writes=[W])
    Wab = k.sb("gWab", [128, 8, 8], BF16)
    k.dma(pool, Wab[:], io["w_ab"].t.rearrange("(c p) n -> p c n", p=128), reads=[io["w_ab"]], writes=[Wab])
    convw = k.sb("convw", [128, 48], F32)
    k.dma(sp, convw[:], io["convw"][:, :], reads=[io["convw"]], writes=[convw])
    dtb = k.sb("dtb", [128, 4], F32)
    k.dma(sp, dtb[:], io["dtb"][:, :], reads=[io["dtb"]], writes=[dtb])
    nega = k.sb("nega", [128, 4], F32)
    k.dma(sp, nega[:], io["alog"][:, :], reads=[io["alog"]], writes=[nega])
    k.op(act, lambda e: e.activation(out=nega[:], in_=nega[:], func=AF.Exp), reads=[nega], writes=[nega])
    k.op(dve, lambda e: e.tensor_scalar(out=nega[:], in0=nega[:], scalar1=-1.0, scalar2=None, op0=ALU.mult), reads=[nega], writes=[nega])
    normg = k.sb("normg", [128, 128], F32)
    k.dma(sp, normg[:], io["normg"][:, :], reads=[io["normg"]], writes=[normg])
    lnb_q = k.sb("lnb_q", [128, 1], F32)
    k.op(pool, lambda e: e.memset(lnb_q[:], float(np.log(128 ** -0.5))), writes=[lnb_q])

    projb = Rot([k.ps("projb%d" % i, [128, 512], F32) for i in range(2)])
    ssb = k.ps("ssb", [128, 512], F32)
    fb = Rot([k.ps("fb%d" % i, [128, 512], F32) for i in range(4)])
    bfb = k.ps("bfb", [128, 1024], BF16)

    def qv(bank, i):
        return bank[:, i * 128:(i + 1) * 128]

    S32 = [k.sb("S32_%d" % h, [128, 128], F32) for h in range(NH)]
    Sbf = [k.sb("Sbf_%d" % h, [128, 128], BF16) for h in range(NH)]
    for h in range(NH):
        k.op(pool, lambda e, h=h: e.memset(S32[h][:], 0.0), writes=[S32[h]])
        k.op(pool, lambda e, h=h: e.memset(Sbf[h][:], 0.0), writes=[Sbf[h]])
    pre = [k.sb("pre%d" % ch, [128, 515], F32) for ch in range(12)]
    for ch in range(12):
        k.op(pool, lambda e, ch=ch: e.memset(pre[ch][:, 0:3], 0.0), writes=[pre[ch]])

    hnb = [k.sb("ghn%d" % i, [128, 8, 512], BF16) for i in range(2)]
    hnv = io["hnT"].t.rearrange("(c p) t -> p c t", p=128)
    acc_rot = Rot([k.sb("cacc%d" % i, [128, 512], F32) for i in range(2)])
    cs_rot = Rot([k.sb("ccs%d" % i, [128, 512], F32) for i in range(2)])
    sq_rot = Rot([k.sb("gsq%d" % i, [128, 512], BF16) for i in range(2)])
    tmp_rot = Rot([k.sb("gtmp%d" % i, [128, 512], F32) for i in range(3)])
    qTs = [[k.sb("qTs%d_%d" % (p, h), [128, 512], BF16) for h in range(NH)] for p in range(2)]
    kTs = [[k.sb("kTs%d_%d" % (p, h), [128, 512], BF16) for h in range(NH)] for p in range(2)]
    vTs = [[k.sb("vTs%d_%d" % (p, h), [128, 512], BF16) for h in range(NH)] for p in range(2)]
    sgT = [[k.sb("sgT%d_%d" % (p, h), [128, 512], BF16) for h in range(NH)] for p in range(2)]
    ogT = [[k.sb("ogT%d_%d" % (p, h), [128, 512], BF16) for h in range(NH)] for p in range(2)]

    def small(name, w=4):
        return [[k.sb("%s%d_%d" % (name, p, j), [128, w], F32) for j in range(4)] for p in range(2)]
    gcol, beta, nbeta, eG, nbeG, eGL, eGLmG = (small(n) for n in ("gcol", "beta", "nbeta", "eG", "nbeG", "eGL", "eGLmG"))
    st1, st2 = small("st1"), small("st2")

    def tile_tmp(name, dt, single=False):
        l = [[k.sb("%s%d_%d" % (name, p, h), [128, 128], dt) for h in range(NH)] for p in range(1 if single else 2)]
        return l if not single else [l[0], l[0]]
    GM1, GM2, Dm, DTm, bV = (tile_tmp(n, F32) for n in ("GM1", "GM2", "Dm", "DTm", "bV"))
    sqo, tmpo, otok = (tile_tmp(n, F32, True) for n in ("sqo", "tmpo", "otok"))
    CH = F32
    QKDT, Kd = (tile_tmp(n, BF16) for n in ("QKDT", "Kd"))
    vnb, onb = (tile_tmp(n, BF16, True) for n in ("vnb", "onb"))
    Bm, BTm = (tile_tmp(n, CH) for n in ("Bm", "BTm"))
    Xb = tile_tmp("Xb", CH, True)
    Pm = [tile_tmp("Pm%d" % i, CH) for i in range(2)]
    PTm = [tile_tmp("PTm%d" % i, CH) for i in range(2)]
    TTm = [tile_tmp("TTm%d" % i, CH) for i in range(2)]
    ssv = [[k.sb("ssv%d_%d" % (p, h), [128, 1], F32) for h in range(NH)] for p in range(2)]
    lnv1 = [[k.sb("lnv1%d_%d" % (p, h), [128, 1], F32) for h in range(NH)] for p in range(2)]
    rsv = [[k.sb("rsv%d_%d" % (p, h), [128, 1], F32) for h in range(NH)] for p in range(2)]

    def mm(out_b, out_ap, l_b, l_ap, r_b, r_ap, start=True, stop=True):
        k.op(pe, lambda e: e.matmul(out_ap, lhsT=l_ap, rhs=r_ap, start=start, stop=stop), reads=[l_b, r_b], writes=[out_b])

    def tr(out_ap, in_b, in_ap):
        k.op(pe, lambda e: e.transpose(out=out_ap, in_=in_ap, identity=c["ident_bf"][:]), reads=[in_b, c["ident_bf"]], writes=[bfb])

    def ev(i):
        return bfb[:, i * 128:(i + 1) * 128]

    HL = range(NH)
    tile_no = 0
    for st in range(nst):
        sp_ = st % 2
        hb = hnb[st % 2]
        k.dma(sp, hb[:], hnv[:, :, st * 512:(st + 1) * 512], reads=[io["hnT"]], writes=[hb])
        for hl in HL:
            for typ in range(4):
                ps = projb.next()
                col = (hl * 4 + typ) * 128
                for kc in range(8):
                    mm(ps, ps[:], W, W[:, kc, col:col + 128], hb, hb[:, kc, :], start=(kc == 0), stop=(kc == 7))
                if typ == 3:
                    k.op(act, lambda e, ps=ps: e.activation(out=sgT[sp_][hl][:], in_=ps[:], func=AF.Silu), reads=[ps], writes=[sgT[sp_][hl]])
                    continue
                ch = hl * 3 + typ
                pr = pre[ch]
                k.op(act, lambda e, ps=ps, pr=pr: e.activation(out=pr[:, 3:515], in_=ps[:], func=AF.Copy), reads=[ps], writes=[pr])
                acc = acc_rot.next()
                k.op(dve, lambda e, pr=pr, acc=acc, ch=ch: e.tensor_scalar(out=acc[:], in0=pr[:, 0:512], scalar1=convw[:, ch * 4:ch * 4 + 1], scalar2=None, op0=ALU.mult),
                     reads=[pr, convw], writes=[acc])
                for tap in range(1, 4):
                    k.op(dve, lambda e, pr=pr, acc=acc, ch=ch, tap=tap: e.scalar_tensor_tensor(out=acc[:], in0=pr[:, tap:tap + 512], scalar=convw[:, ch * 4 + tap:ch * 4 + tap + 1],
                                                                                           in1=acc[:], op0=ALU.mult, op1=ALU.add), reads=[pr, convw, acc], writes=[acc])
                k.op(pool, lambda e, pr=pr: e.tensor_copy(out=pr[:, 0:3], in_=pr[:, 512:515]), reads=[pr], writes=[pr])
                if typ == 2:
                    k.op(act, lambda e, acc=acc: e.activation(out=vTs[sp_][hl][:], in_=acc[:], func=AF.Silu), reads=[acc], writes=[vTs[sp_][hl]])
                else:
                    cs = cs_rot.next()
                    k.op(act, lambda e, acc=acc, cs=cs: e.activation(out=cs[:], in_=acc[:], func=AF.Silu), reads=[acc], writes=[cs])
                    rstd = emit_rstd(k, c, [(cs[:], cs)], 1.0, ssb, sq_rot, tmp_rot, ln_bias=(lnb_q if typ == 0 else None))
                    dst = qTs[sp_][hl] if typ == 0 else kTs[sp_][hl]
                    k.op(dve, lambda e, cs=cs, rstd=rstd, dst=dst: e.tensor_tensor(out=dst[:], in0=cs[:], in1=rstd[:], op=ALU.mult), reads=[cs, rstd], writes=[dst])
        pab = fb.next()
        for j in range(4):
            jc = slice(j * 128, (j + 1) * 128)
            for kc in range(8):
                mm(pab, pab[:, j * 8:j * 8 + 8], hb, hb[:, kc, jc], Wab, Wab[:, kc, :], start=(kc == 0), stop=(kc == 7))
        ab_sb = [st1[sp_][j] for j in range(4)]
        bb_sb = [st2[sp_][j] for j in range(4)]
        for j in range(4):
            k.op(dve, lambda e, j=j: e.tensor_tensor(out=ab_sb[j][:], in0=pab[:, j * 8:j * 8 + 4], in1=dtb[:], op=ALU.add), reads=[pab, dtb], writes=[ab_sb[j]])
            k.op(dve, lambda e, j=j: e.tensor_copy(out=bb_sb[j][:], in_=pab[:, j * 8 + 4:j * 8 + 8]), reads=[pab], writes=[bb_sb[j]])
        pG = fb.next()
        for j in range(4):
            a1, a2 = ab_sb[j], bb_sb[j]
            k.op(act, lambda e, a1=a1: e.activation(out=a1[:], in_=a1[:], func=AF.Exp), reads=[a1], writes=[a1])
            k.op(act, lambda e, a1=a1: e.activation(out=a1[:], in_=a1[:], func=AF.Ln, bias=c["one"][:], scale=1.0), reads=[a1, c["one"]], writes=[a1])
            g_ = gcol[sp_][j]
            k.op(dve, lambda e, a1=a1, g_=g_: e.tensor_tensor(out=g_[:], in0=a1[:], in1=nega[:], op=ALU.mult), reads=[a1, nega], writes=[g_])
            b_ = beta[sp_][j]
            k.op(act, lambda e, a2=a2: e.activation(out=a2[:], in_=a2[:], func=AF.Exp, scale=-1.0), reads=[a2], writes=[a2])
            k.op(dve, lambda e, a2=a2: e.tensor_scalar(out=a2[:], in0=a2[:], scalar1=1.0, scalar2=None, op0=ALU.add), reads=[a2], writes=[a2])
            k.op(dve, lambda e, a2=a2, b_=b_: e.reciprocal(out=b_[:], in_=a2[:]), reads=[a2], writes=[b_])
            k.op(dve, lambda e, j=j, b_=b_: e.tensor_scalar(out=nbeta[sp_][j][:], in0=b_[:], scalar1=-1.0, scalar2=None, op0=ALU.mult), reads=[b_], writes=[nbeta[sp_][j]])
            mm(pG, pG[:, j * 8:j * 8 + 4], c["triincl"], c["triincl"][:], g_, g_[:])
            mm(pG, pG[:, j * 8 + 4:j * 8 + 8], c["ones_f"], c["ones_f"][:], g_, g_[:])
        for j in range(4):
            a2 = bb_sb[j]
            k.op(act, lambda e, j=j: e.activation(out=eG[sp_][j][:], in_=pG[:, j * 8:j * 8 + 4], func=AF.Exp), reads=[pG], writes=[eG[sp_][j]])
            k.op(act, lambda e, j=j: e.activation(out=eGL[sp_][j][:], in_=pG[:, j * 8 + 4:j * 8 + 8], func=AF.Exp), reads=[pG], writes=[eGL[sp_][j]])
            k.op(act, lambda e, j=j, a2=a2: e.activation(out=a2[:], in_=pG[:, j * 8:j * 8 + 4], func=AF.Copy), reads=[pG], writes=[a2])
        for j in range(4):
            a2 = bb_sb[j]
            k.op(dve, lambda e, j=j, a2=a2: e.tensor_tensor(out=a2[:], in0=pG[:, j * 8 + 4:j * 8 + 8], in1=a2[:], op=ALU.subtract), reads=[pG, a2], writes=[a2])
            k.op(act, lambda e, j=j, a2=a2: e.activation(out=eGLmG[sp_][j][:], in_=a2[:], func=AF.Exp), reads=[a2], writes=[eGLmG[sp_][j]])
            k.op(dve, lambda e, j=j: e.scalar_tensor_tensor(out=nbeG[sp_][j][:], in0=eG[sp_][j][:], scalar=-1.0, in1=beta[sp_][j][:], op0=ALU.mult, op1=ALU.mult),
                 reads=[eG[sp_][j], beta[sp_][j]], writes=[nbeG[sp_][j]])

        def pre_tile(j):
            jc = slice(j * 128, (j + 1) * 128)
            tp = j % 2
            sc = lambda arr, hl: arr[sp_][j][:, hl:hl + 1]
            for hl in HL:
                g1, g2 = GM1[tp][hl], GM2[tp][hl]
                k.op(pool, lambda e, g1=g1, hl=hl: e.tensor_scalar(out=g1[:], in0=c["triincl"][:], scalar1=sc(gcol, hl), scalar2=None, op0=ALU.mult),
                     reads=[c["triincl"], gcol[sp_][j]], writes=[g1])
                k.op(pool, lambda e, g2=g2, hl=hl: e.tensor_scalar(out=g2[:], in0=c["strictgt"][:], scalar1=sc(gcol, hl), scalar2=None, op0=ALU.mult),
                     reads=[c["strictgt"], gcol[sp_][j]], writes=[g2])
            yield
            b_dd = fb.next()
            for hl in HL:
                mm(b_dd, qv(b_dd, hl), GM1[tp][hl], GM1[tp][hl][:], c["strictgt"], c["strictgt"][:], start=True, stop=False)
                mm(b_dd, qv(b_dd, hl), c["ident_f"], c["ident_f"][:], c["mneg_strict"], c["mneg_strict"][:], start=False, stop=True)
            yield
            b_ddT = fb.next()
            for hl in HL:
                mm(b_ddT, qv(b_ddT, hl), GM2[tp][hl], GM2[tp][hl][:], c["triincl"], c["triincl"][:], start=True, stop=False)
                mm(b_ddT, qv(b_ddT, hl), c["ident_f"], c["ident_f"][:], c["mneg_inclT"], c["mneg_inclT"][:], start=False, stop=True)
            for hl in HL:
                k.op(act, lambda e, hl=hl: e.activation(out=Dm[tp][hl][:], in_=qv(b_dd, hl), func=AF.Exp), reads=[b_dd], writes=[Dm[tp][hl]])
            for hl in HL:
                k.op(act, lambda e, hl=hl: e.activation(out=DTm[tp][hl][:], in_=qv(b_ddT, hl), func=AF.Exp), reads=[b_ddT], writes=[DTm[tp][hl]])
            yield
            b_kk = fb.next()
            for hl in HL:
                kT_ = kTs[sp_][hl]
                mm(b_kk, qv(b_kk, hl), kT_, kT_[:, jc], kT_, kT_[:, jc])
            yield
            b_kq = fb.next()
            for hl in HL:
                kT_, qT_ = kTs[sp_][hl], qTs[sp_][hl]
                mm(b_kq, qv(b_kq, hl), kT_, kT_[:, jc], qT_, qT_[:, jc])
            for hl in HL:
                k.op(dve, lambda e, hl=hl: e.scalar_tensor_tensor(out=Bm[tp][hl][:], in0=qv(b_kk, hl), scalar=sc(nbeta, hl), in1=Dm[tp][hl][:], op0=ALU.mult, op1=ALU.mult),
                     reads=[b_kk, nbeta[sp_][j], Dm[tp][hl]], writes=[Bm[tp][hl]])
            for hl in HL:
                k.op(dve, lambda e, hl=hl: e.tensor_tensor(out=QKDT[tp][hl][:], in0=qv(b_kq, hl), in1=DTm[tp][hl][:], op=ALU.mult),
                     reads=[b_kq, DTm[tp][hl]], writes=[QKDT[tp][hl]])
            yield
            yield
            b_bt = fb.next()
            for hl in HL:
                k.op(pe, lambda e, hl=hl: e.transpose(out=qv(b_bt, hl), in_=Bm[tp][hl][:], identity=c["ident_f"][:]),
                     reads=[Bm[tp][hl], c["ident_f"]], writes=[b_bt])
            for hl in HL:
                k.op(act, lambda e, hl=hl: e.activation(out=BTm[tp][hl][:], in_=qv(b_bt, hl), func=AF.Copy), reads=[b_bt], writes=[BTm[tp][hl]])
                k.op(pool, lambda e, hl=hl: e.tensor_tensor(out=TTm[0][tp][hl][:], in0=BTm[tp][hl][:], in1=c["ident_f"][:], op=ALU.add),
                     reads=[BTm[tp][hl], c["ident_f"]], writes=[TTm[0][tp][hl]])
            yield
            for hl in HL:
                tr(ev(hl), vTs[sp_][hl], vTs[sp_][hl][:, jc])
                tr(ev(4 + hl), kTs[sp_][hl], kTs[sp_][hl][:, jc])
            for hl in HL:
                k.op(act, lambda e, hl=hl: e.activation(out=bV[tp][hl][:], in_=ev(hl), func=AF.Copy, scale=sc(beta, hl)), reads=[bfb, beta[sp_][j]], writes=[bV[tp][hl]])
                k.op(act, lambda e, hl=hl: e.activation(out=Kd[tp][hl][:], in_=ev(4 + hl), func=AF.Copy, scale=sc(eGLmG, hl)), reads=[bfb, eGLmG[sp_][j]], writes=[Kd[tp][hl]])
            Pold = {hl: Bm[tp][hl] for hl in HL}
            PTold = {hl: BTm[tp][hl] for hl in HL}
            TTold = {hl: TTm[0][tp][hl] for hl in HL}
            for lvl in range(1, 7):
                yield
                b_p = fb.next()
                for hl in HL:
                    mm(b_p, qv(b_p, hl), PTold[hl], PTold[hl][:], Pold[hl], Pold[hl][:])
                b_pt = None
                if lvl < 6:
                    yield
                    b_pt = fb.next()
                    for hl in HL:
                        mm(b_pt, qv(b_pt, hl), Pold[hl], Pold[hl][:], PTold[hl], PTold[hl][:])
                Pn = {hl: Pm[lvl % 2][tp][hl] for hl in HL}
                for hl in HL:
                    k.op(act, lambda e, hl=hl: e.activation(out=Pn[hl][:], in_=qv(b_p, hl), func=AF.Copy), reads=[b_p], writes=[Pn[hl]])
                PTn = {hl: None for hl in HL}
                if lvl < 6:
                    PTn = {hl: PTm[lvl % 2][tp][hl] for hl in HL}
                    for hl in HL:
                        k.op(dve, lambda e, hl=hl: e.tensor_copy(out=PTn[hl][:], in_=qv(b_pt, hl)), reads=[b_pt], writes=[PTn[hl]])
                yield
                b_t = fb.next()
                for hl in HL:
                    mm(b_t, qv(b_t, hl), Pn[hl], Pn[hl][:], TTold[hl], TTold[hl][:])
                TTn = {hl: TTm[lvl % 2][tp][hl] for hl in HL}
                for hl in HL:
                    k.op(dve, lambda e, hl=hl, to=TTold[hl]: e.tensor_tensor(out=TTn[hl][:], in0=qv(b_t, hl), in1=to[:], op=ALU.add),
                         reads=[b_t, TTold[hl]], writes=[TTn[hl]])
                Pold, PTold, TTold = Pn, PTn, TTn
            fin[j] = TTold

        def rec_tile(j):
            jc = slice(j * 128, (j + 1) * 128)
            tp = j % 2
            sc = lambda arr, hl: arr[sp_][j][:, hl:hl + 1]
            TTold = fin[j]
            b_ks = fb.next()
            for hl in HL:
                kT_ = kTs[sp_][hl]
                mm(b_ks, qv(b_ks, hl), kT_, kT_[:, jc], Sbf[hl], Sbf[hl][:])
            b_qs = fb.next()
            for hl in HL:
                qT_ = qTs[sp_][hl]
                mm(b_qs, qv(b_qs, hl), qT_, qT_[:, jc], Sbf[hl], Sbf[hl][:])
            for hl in HL:
                k.op(dve, lambda e, hl=hl: e.scalar_tensor_tensor(out=Xb[tp][hl][:], in0=qv(b_ks, hl), scalar=sc(nbeG, hl), in1=bV[tp][hl][:], op0=ALU.mult, op1=ALU.add),
                     reads=[b_ks, nbeG[sp_][j], bV[tp][hl]], writes=[Xb[tp][hl]])
            for hl in HL:
                k.op(act, lambda e, hl=hl: e.activation(out=tmpo[tp][hl][:], in_=qv(b_qs, hl), func=AF.Copy, scale=sc(eG, hl)), reads=[b_qs, eG[sp_][j]], writes=[tmpo[tp][hl]])
            b_vn = fb.next()
            for hl in HL:
                mm(b_vn, qv(b_vn, hl), TTold[hl], TTold[hl][:], Xb[tp][hl], Xb[tp][hl][:])
            for hl in HL:
                k.op(act, lambda e, hl=hl: e.activation(out=vnb[tp][hl][:], in_=qv(b_vn, hl), func=AF.Copy), reads=[b_vn], writes=[vnb[tp][hl]])
            b_o = fb.next()
            for hl in HL:
                mm(b_o, qv(b_o, hl), QKDT[tp][hl], QKDT[tp][hl][:], vnb[tp][hl], vnb[tp][hl][:])
            b_sn = fb.next()
            for hl in HL:
                mm(b_sn, qv(b_sn, hl), Kd[tp][hl], Kd[tp][hl][:], vnb[tp][hl], vnb[tp][hl][:])
            for hl in HL:
                k.op(dve, lambda e, hl=hl: e.tensor_tensor(out=otok[tp][hl][:], in0=qv(b_o, hl), in1=tmpo[tp][hl][:], op=ALU.add),
                     reads=[b_o, tmpo[tp][hl]], writes=[otok[tp][hl]])
            for hl in HL:
                k.op(dve, lambda e, hl=hl: e.scalar_tensor_tensor(out=S32[hl][:], in0=S32[hl][:], scalar=sc(eGL, hl), in1=qv(b_sn, hl), op0=ALU.mult, op1=ALU.add),
                     reads=[S32[hl], eGL[sp_][j], b_sn], writes=[S32[hl]])
                k.op(act, lambda e, hl=hl: e.activation(out=Sbf[hl][:], in_=S32[hl][:], func=AF.Copy), reads=[S32[hl]], writes=[Sbf[hl]])
            for hl in HL:
                k.op(act, lambda e, hl=hl: e.activation(out=sqo[tp][hl][:], in_=otok[tp][hl][:], func=AF.Square), reads=[otok[tp][hl]], writes=[sqo[tp][hl]])
                k.op(dve, lambda e, hl=hl: e.tensor_reduce(out=ssv[tp][hl][:], in_=sqo[tp][hl][:], axis=AX.X, op=ALU.add), reads=[sqo[tp][hl]], writes=[ssv[tp][hl]])
                k.op(act, lambda e, hl=hl: e.activation(out=lnv1[tp][hl][:], in_=ssv[tp][hl][:], func=AF.Ln, bias=c["eps"][:], scale=1.0 / 128),
                     reads=[ssv[tp][hl], c["eps"]], writes=[lnv1[tp][hl]])
                k.op(act, lambda e, hl=hl: e.activation(out=rsv[tp][hl][:], in_=lnv1[tp][hl][:], func=AF.Exp, scale=-0.5), reads=[lnv1[tp][hl]], writes=[rsv[tp][hl]])
                k.op(pool, lambda e, hl=hl: e.tensor_scalar(out=otok[tp][hl][:], in0=otok[tp][hl][:], scalar1=rsv[tp][hl][:, 0:1], scalar2=None, op0=ALU.mult),
                     reads=[otok[tp][hl], rsv[tp][hl]], writes=[otok[tp][hl]])
                k.op(pool, lambda e, hl=hl: e.tensor_tensor(out=onb[tp][hl][:], in0=otok[tp][hl][:], in1=normg[:], op=ALU.mult),
                     reads=[otok[tp][hl], normg], writes=[onb[tp][hl]])
            for hl in HL:
                tr(ev(hl), onb[tp][hl], onb[tp][hl][:])
            for hl in HL:
                k.op(dve, lambda e, hl=hl: e.tensor_tensor(out=ogT[sp_][hl][:, jc], in0=ev(hl), in1=sgT[sp_][hl][:, jc], op=ALU.mult),
                     reads=[bfb, sgT[sp_][hl]], writes=[ogT[sp_][hl]])
        fin = {}
        for jp in (0, 2):
            gens = [pre_tile(jp), pre_tile(jp + 1)]
            alive = True
            while alive:
                alive = False
                for g_ in gens:
                    try:
                        next(g_)
                        alive = True
                    except StopIteration:
                        pass
            rec_tile(jp)
            rec_tile(jp + 1)
        for hl in HL:
            k.dma(sp, io["ogT_out"][hl * 128:(hl + 1) * 128, st * 512:(st + 1) * 512], ogT[sp_][hl][:], reads=[ogT[sp_][hl]], untracked_out=io["ogT_out"])
    k.finish([io["ogT_out"]])


def build_gdn(seq=S):
    nc = bass.Bass("TRN2", target_bir_lowering=False)
    k = KB(nc)
    io = {}
    io["hnT"] = k.dram("hnT", [D, seq], BF16, "ExternalInput")
    io["w_qkvg"] = k.dram("w_qkvg", [D, 2048], F32, "ExternalInput")
    io["w_ab"] = k.dram("w_ab", [D, 8], F32, "ExternalInput")
    io["convw"] = k.dram("convw", [128, 48], F32, "ExternalInput")
    io["dtb"] = k.dram("dtb", [128, 4], F32, "ExternalInput")
    io["alog"] = k.dram("alog", [128, 4], F32, "ExternalInput")
    io["normg"] = k.dram("normg", [128, 128], F32, "ExternalInput")
    io["ogT_out"] = k.dram("ogT_out", [512, seq], BF16, "ExternalOutput")
    c = emit_consts(k)
    emit_gdn_consts(k, c)
    emit_gdn_phase(k, c, io, seq)
    return nc, k


def gdn_host_inputs(inp, layer, r):
    heads = [4 * r + hl for hl in range(4)]
    w_in = inp["gdn_w_in"][layer]
    cols = []
    for h in heads:
        for typ in range(4):
            cols.append(w_in[:, typ * 1024 + h * 128: typ * 1024 + (h + 1) * 128])
    w_qkvg = np.ascontiguousarray(np.concatenate(cols, axis=1))
    w_ab = np.ascontiguousarray(np.concatenate([w_in[:, 4096 + heads[0]:4096 + heads[0] + 4], w_in[:, 4104 + heads[0]:4104 + heads[0] + 4]], axis=1))
    cw = inp["gdn_conv"][layer]
    convw = np.zeros((128, 48), np.float32)
    for hl, h in enumerate(heads):
        for typ in range(3):
            ch = hl * 3 + typ
            convw[:, ch * 4:(ch + 1) * 4] = cw[:, typ * 1024 + h * 128: typ * 1024 + (h + 1) * 128].T
    dtb = np.ascontiguousarray(np.broadcast_to(inp["gdn_dt_bias"][layer][heads[0]:heads[0] + 4][None, :], (128, 4))).astype(np.float32)
    alog = np.ascontiguousarray(np.broadcast_to(inp["gdn_a_log"][layer][heads[0]:heads[0] + 4][None, :], (128, 4))).astype(np.float32)
    normg = np.ascontiguousarray(np.broadcast_to(inp["gdn_norm"][layer][None, :], (128, 128))).astype(np.float32)
    return dict(w_qkvg=w_qkvg, w_ab=w_ab, convw=convw, dtb=dtb, alog=alog, normg=normg)


_PROGS = {}


def _prog(key, builder):
    if key not in _PROGS:
        _PROGS[key] = builder()[0]
    return _PROGS[key]


def _run(nc, maps):
    res = run_bass_kernel_spmd(nc, maps, core_ids=list(range(NCORES)))
    return res.results


def _gains(inp, layer, nxt_layer, kv, q_layer):
    g = np.zeros((128, NGCOL), np.float32)
    if layer is not None:
        g[:, 0:8] = col8(inp["ln_ffn"][layer])
        g[:, 8:16] = col8(inp["ln_ple"][layer])
    if kv:
        g[:, 16:24] = col8(inp["kv_norm"])
        g[:, 24] = inp["k_norm"]
    if nxt_layer is not None:
        g[:, 25:33] = col8(inp["ln_mix"][nxt_layer])
    if q_layer is not None:
        g[:, 33] = inp["sb_q_norm"][q_layer]
    return g


def kernel(**inp):
    inp = {k_: np.asarray(v) for k_, v in inp.items()}
    x, p = inp["x"], inp["p"]
    f32 = np.float32

    def tok(c):
        return c // 2, slice((c % 2) * NT, (c % 2 + 1) * NT)

    nc = _prog("tokA", lambda: build_token(dict(tail=False, next="gdn")))
    g = _gains(inp, None, 0, False, None)
    maps = []
    for c in range(NCORES):
        b, sl = tok(c)
        maps.append({"hT_in": np.ascontiguousarray(x[b, sl].T), "gains": g})
    res = _run(nc, maps)
    hT = [res[c]["hT_out"] for c in range(NCORES)]
    hnT = [res[c]["hnT_out"] for c in range(NCORES)]

    def tail_maps(layer, oT_full, w_o, extra):
        maps = []
        for c in range(NCORES):
            b, sl = tok(c)
            m = {"hT_in": hT[c], "oT": np.ascontiguousarray(oT_full[b][:, sl]), "w_o": w_o,
                 "ffn_w_in": inp["ffn_w_in"][layer], "ffn_w_out": inp["ffn_w_out"][layer],
                 "ple_w_gate": inp["ple_w_gate"][layer], "ple_w_proj": inp["ple_w_proj"][layer],
                 "pT": np.ascontiguousarray(p[layer, b, sl].T)}
            m.update(extra)
            maps.append(m)
        return maps

    k_full = v_full = None
    for layer in range(DEPTH):
        if layer < 2:
            nc = _prog("gdn", build_gdn)
            maps = []
            for c in range(NCORES):
                b, r = c // 2, c % 2
                m = gdn_host_inputs(inp, layer, r)
                m["hnT"] = np.ascontiguousarray(np.concatenate([hnT[2 * b], hnT[2 * b + 1]], axis=1))
                maps.append(m)
            res = _run(nc, maps)
            oT_full = [np.concatenate([res[2 * b]["ogT_out"], res[2 * b + 1]["ogT_out"]], axis=0) for b in range(NB)]
            w_o = inp["gdn_w_out"][layer]
        else:
            nc = _prog("sb", build_sb)
            maps = []
            for c in range(NCORES):
                b, r = c // 2, c % 2
                rows = slice(512 * r, 512 * (r + 1))
                q_full = np.concatenate([qT[2 * b], qT[2 * b + 1]], axis=1)
                maps.append({"qT": np.ascontiguousarray(q_full[rows]), "kT": np.ascontiguousarray(k_full[b][rows]),
                             "v": np.ascontiguousarray(v_full[b][:, rows])})
            res = _run(nc, maps)
            oT_full = [np.concatenate([res[2 * b]["oT_out"], res[2 * b + 1]["oT_out"]], axis=0) for b in range(NB)]
            w_o = inp["sb_w_out"][layer - 2]
        if layer == 0:
            nc = _prog("tokB", lambda: build_token(dict(tail=True, next="gdn")))
            maps = tail_maps(layer, oT_full, w_o, {"gains": _gains(inp, layer, 1, False, None)})
        elif layer == 1:
            nc = _prog("tokC", lambda: build_token(dict(tail=True, next="sb", kv=True)))
            maps = tail_maps(layer, oT_full, w_o, {"gains": _gains(inp, layer, 2, True, 0), "w_kv": inp["w_kv"], "w_q": inp["sb_w_q"][0]})
        elif layer == 2:
            nc = _prog("tokD", lambda: build_token(dict(tail=True, next="sb")))
            maps = tail_maps(layer, oT_full, w_o, {"gains": _gains(inp, layer, 3, False, 1), "w_q": inp["sb_w_q"][1]})
        else:
            nc = _prog("tokE", lambda: build_token(dict(tail=True, next=None)))
            maps = tail_maps(layer, oT_full, w_o, {"gains": _gains(inp, layer, None, False, None)})
        res = _run(nc, maps)
        hT = [res[c]["hT_out"] for c in range(NCORES)]
        if layer == 0:
            hnT = [res[c]["hnT_out"] for c in range(NCORES)]
        if layer == 1:
            k_full = [np.concatenate([res[2 * b]["kT_out"], res[2 * b + 1]["kT_out"]], axis=1) for b in range(NB)]
            v_full = [np.concatenate([res[2 * b]["v_out"], res[2 * b + 1]["v_out"]], axis=0) for b in range(NB)]
        if layer in (1, 2):
            qT = [res[c]["qT_out"] for c in range(NCORES)]

    out = np.empty((NB, S, D), f32)
    for c in range(NCORES):
        b, sl = tok(c)
        out[b, sl] = np.asarray(hT[c], f32).T
    return out
```
